# Optimizing a Trainium2 kernel written in Bass

```python
import math, functools
import jax, jax.numpy as jnp
from jax import lax
import numpy as np

D_MODEL = 1024
BATCH = 8
SEQ = 2048
DEPTH = 4
DEC_BATCH = 128
DEC_SEQ = 8
PAST_LEN = 2048
PAGE_SIZE = 128

N_A_LAYERS = DEPTH // 2
N_B_LAYERS = DEPTH - N_A_LAYERS
EPS = 1e-6

A_HEADS = 8
A_DK = 128
A_DV = 128
CONV_W = 4
CHUNK = 64
A_QK_DIM = A_HEADS * A_DK
A_V_DIM = A_HEADS * A_DV
A_CONV_DIM = 2 * A_QK_DIM + A_V_DIM
A_PROJ_DIM = A_CONV_DIM + A_V_DIM + 2 * A_HEADS

B_GROUPS = ((128, 1), (512, 4), (2048, 16))
N_GROUPS = 3
B_HEADS = 16
B_DH = 64
B_KV_HEADS = 4
B_QPK = B_HEADS // B_KV_HEADS
B_BLOCK = 128
MAX_WINDOW = 2048

D_FF = -(-8 * D_MODEL // (3 * 256)) * 256

kernel_name = 'yoco_gated_deltanet_dilated_window_step'


def rms_norm(x, gain):
    xf = x.astype(jnp.float32)
    y = xf * lax.rsqrt(jnp.mean(xf * xf, axis=-1, keepdims=True) + EPS)
    return (y * gain.astype(jnp.float32)).astype(x.dtype)


def l2_normalize(x):
    return x * lax.rsqrt(jnp.sum(x * x, axis=-1, keepdims=True) + EPS)


def swiglu(x, w_in, w_out):
    gate, up = jnp.split(x @ w_in, 2, axis=-1)
    return (jax.nn.silu(gate) * up) @ w_out


def causal_depthwise_conv(xh, w):
    c = xh.shape[-1]
    return lax.conv_general_dilated(xh, w[:, None, :].astype(xh.dtype), window_strides=(1,), padding='VALID',
                                    dimension_numbers=('NWC', 'WIO', 'NWC'), feature_group_count=c)


def gated_delta_rule(q, k, v, g, beta, s0):
    b, t, h, dk = q.shape
    n = -(-t // CHUNK)
    pad = n * CHUNK - t

    def blocks(z):
        z = jnp.pad(z, [(0, 0), (0, pad)] + [(0, 0)] * (z.ndim - 2))
        z = z.reshape((b, n, CHUNK) + z.shape[2:])
        return jnp.transpose(z, (1, 0, 3, 2) + tuple(range(4, z.ndim)))

    q, k, v, g, beta = blocks(q), blocks(k), blocks(v), blocks(g), blocks(beta)
    gcum = jnp.cumsum(g, axis=-1)
    idx = np.arange(CHUNK)
    incl = idx[:, None] >= idx[None, :]
    strict = idx[:, None] > idx[None, :]
    decay = jnp.exp(jnp.where(incl, gcum[..., :, None] - gcum[..., None, :], -jnp.inf))
    a_mat = jnp.where(strict, beta[..., :, None] * decay * jnp.einsum('nbhtd,nbhsd->nbhts', k, k), 0.0)
    gam = jnp.exp(gcum)
    rhs = jnp.concatenate([(beta * gam)[..., None] * k, beta[..., None] * v], axis=-1)
    sol = lax.linalg.triangular_solve(jnp.eye(CHUNK, dtype=jnp.float32) + a_mat, rhs,
                                      left_side=True, lower=True, unit_diagonal=True)
    w_mat, u_base = sol[..., :dk], sol[..., dk:]
    attn = decay * jnp.einsum('nbhtd,nbhsd->nbhts', q, k)
    q_dec = gam[..., None] * q
    k_dec = jnp.exp(gcum[..., -1:] - gcum)[..., None] * k
    g_end = jnp.exp(gcum[..., -1])

    def step(s, xs):
        w_c, u_c, attn_c, q_c, k_c, ge_c = xs
        u = u_c - jnp.einsum('bhcd,bhde->bhce', w_c, s)
        o = jnp.einsum('bhcd,bhde->bhce', q_c, s) + jnp.einsum('bhcs,bhse->bhce', attn_c, u)
        s = ge_c[..., None, None] * s + jnp.einsum('bhcd,bhce->bhde', k_c, u)
        return s, o

    s_fin, o = lax.scan(step, s0, (w_mat, u_base, attn, q_dec, k_dec, g_end))
    o = jnp.transpose(o, (1, 0, 3, 2, 4)).reshape(b, n * CHUNK, h, -1)[:, :t]
    return o, s_fin


def gdn_mixer(h, conv_hist, s0, w_in, conv_w, a_log, dt_bias, o_gain, w_out):
    b, l, _ = h.shape
    f32 = jnp.float32
    proj = h @ w_in
    qkv = proj[..., :A_CONV_DIM]
    gate = proj[..., A_CONV_DIM:A_CONV_DIM + A_V_DIM]
    beta_raw = proj[..., A_CONV_DIM + A_V_DIM:A_CONV_DIM + A_V_DIM + A_HEADS]
    decay_raw = proj[..., A_CONV_DIM + A_V_DIM + A_HEADS:]
    xh = jnp.concatenate([conv_hist.astype(qkv.dtype), qkv], axis=1)
    new_hist = xh[:, -(CONV_W - 1):]
    c = jax.nn.silu(causal_depthwise_conv(xh, conv_w)).astype(f32)
    q = l2_normalize(c[..., :A_QK_DIM].reshape(b, l, A_HEADS, A_DK)) * (A_DK ** -0.5)
    k = l2_normalize(c[..., A_QK_DIM:2 * A_QK_DIM].reshape(b, l, A_HEADS, A_DK))
    v = c[..., 2 * A_QK_DIM:].reshape(b, l, A_HEADS, A_DV)
    beta = jax.nn.sigmoid(beta_raw.astype(f32))
    g = -jnp.exp(a_log.astype(f32)) * jax.nn.softplus(decay_raw.astype(f32) + dt_bias.astype(f32))
    o, s_new = gated_delta_rule(q, k, v, g, beta, s0.astype(f32))
    o = rms_norm(o, o_gain) * jax.nn.silu(gate.astype(f32).reshape(b, l, A_HEADS, A_DV))
    y = o.reshape(b, l, A_V_DIM).astype(h.dtype) @ w_out
    return y, s_new.astype(s0.dtype), new_hist.astype(conv_hist.dtype)


def alibi_slopes():
    n = N_GROUPS * B_HEADS
    return (2.0 ** (-8.0 * np.arange(1, n + 1) / n)).astype(np.float32).reshape(N_GROUPS, B_HEADS)


def dilated_band_attention(q, win, dil, slope, *, k, v):
    b, s, h, dh = q.shape
    nkv = k.shape[2]
    qpk = h // nkv
    span = win // dil
    n_dec = s // dil
    nb = -(-n_dec // B_BLOCK)
    lp = nb * B_BLOCK

    def residues(z):
        z = jnp.swapaxes(z.reshape((b, n_dec, dil) + z.shape[2:]), 1, 2)
        return jnp.pad(z, [(0, 0), (0, 0), (0, lp - n_dec)] + [(0, 0)] * (z.ndim - 3))

    def key_windows(z):
        zp = jnp.pad(residues(z), [(0, 0), (0, 0), (B_BLOCK, 0), (0, 0), (0, 0)])
        prev = zp[:, :, :lp].reshape(b, dil, nb, B_BLOCK, nkv, dh)
        cur = zp[:, :, B_BLOCK:].reshape(b, dil, nb, B_BLOCK, nkv, dh)
        return jnp.concatenate([prev, cur], axis=3)

    qr = residues(q).reshape(b, dil, nb, B_BLOCK, nkv, qpk, dh)
    kw = key_windows(k)
    vw = key_windows(v)
    qi = np.arange(B_BLOCK)[:, None]
    kj = np.arange(2 * B_BLOCK)[None, :]
    delta = B_BLOCK + qi - kj
    valid = (delta >= 0) & (delta <= span) & ((np.arange(nb)[:, None, None] > 0) | (kj >= B_BLOCK))
    bias = -slope[:, :, None, None] * (dil * delta).astype(np.float32)
    scores = jnp.einsum('brnqgpd,brnkgd->brngpqk', qr, kw, preferred_element_type=jnp.float32) * (dh ** -0.5) + bias
    scores = jnp.where(valid[:, None, None], scores, -jnp.inf)
    lse = jax.nn.logsumexp(scores, axis=-1)
    probs = jnp.exp(scores - lse[..., None])
    out = jnp.einsum('brngpqk,brnkgd->brnqgpd', probs, vw.astype(jnp.float32))
    out = jnp.swapaxes(out.reshape(b, dil, lp, h, dh)[:, :, :n_dec], 1, 2).reshape(b, s, h, dh)
    lse = jnp.transpose(lse, (0, 1, 2, 5, 3, 4)).reshape(b, dil, lp, h)[:, :, :n_dec]
    lse = jnp.swapaxes(lse, 1, 2).reshape(b, s, h)
    return out, lse


def dilated_gather_attention(q, win, dil, slope, *, k_all, v_all):
    b, l, h, dh = q.shape
    t, nkv = k_all.shape[1], k_all.shape[2]
    qpk = h // nkv
    dist = dil * np.arange(win // dil + 1)
    pos = (t - l) + np.arange(l)[:, None] - dist[None, :]
    valid = pos >= 0
    idx = np.maximum(pos, 0)
    kg = k_all[:, idx]
    vg = v_all[:, idx]
    qr = q.reshape(b, l, nkv, qpk, dh)
    bias = -slope[:, :, None] * dist.astype(np.float32)
    scores = jnp.einsum('blgpd,bltgd->blgpt', qr, kg, preferred_element_type=jnp.float32) * (dh ** -0.5) + bias
    scores = jnp.where(valid[:, None, None, :], scores, -jnp.inf)
    lse = jax.nn.logsumexp(scores, axis=-1)
    probs = jnp.exp(scores - lse[..., None])
    out = jnp.einsum('blgpt,bltgd->blgpd', probs, vg.astype(jnp.float32)).reshape(b, l, h, dh)
    return out, lse.reshape(b, l, h)


def dilated_mixture(h, w_q, w_o, attend):
    b, l, _ = h.shape
    q = (h @ w_q).reshape(b, l, N_GROUPS, B_HEADS, B_DH)
    slopes = alibi_slopes()
    outs, lses = [], []
    for gi, (win, dil) in enumerate(B_GROUPS):
        o, s = attend(q[:, :, gi], win, dil, slopes[gi].reshape(B_KV_HEADS, B_QPK))
        outs.append(o)
        lses.append(s)
    weights = jax.nn.softmax(jnp.stack(lses, axis=0), axis=0)
    o = jnp.einsum('gblh,gblhd->blhd', weights, jnp.stack(outs, axis=0))
    return o.reshape(b, l, B_HEADS * B_DH).astype(h.dtype) @ w_o


def decoder_trunk(x, conv_hist, delta_s, kv_past, norms, kv_norm, a_w_in, a_conv_w, a_log, a_dt_bias,
                  a_o_gain, a_w_out, b_w_kv, b_w_q, b_w_o, ffn_w_in, ffn_w_out):
    new_hist, new_delta = [], []
    attend = None
    k_new = v_new = None
    for layer in range(DEPTH):
        if layer < N_A_LAYERS:
            y, s, hst = gdn_mixer(rms_norm(x, norms[layer, 0]), conv_hist[layer], delta_s[layer],
                                  a_w_in[layer], a_conv_w[layer], a_log[layer], a_dt_bias[layer],
                                  a_o_gain[layer], a_w_out[layer])
            new_hist.append(hst)
            new_delta.append(s)
        else:
            if layer == N_A_LAYERS:
                b, l, _ = x.shape
                kv = (rms_norm(x, kv_norm) @ b_w_kv).reshape(b, l, 2, B_KV_HEADS, B_DH)
                k_new, v_new = kv[:, :, 0], kv[:, :, 1]
                if kv_past is None:
                    attend = functools.partial(dilated_band_attention, k=k_new, v=v_new)
                else:
                    k_all = jnp.concatenate([kv_past[0].astype(k_new.dtype), k_new], axis=1)
                    v_all = jnp.concatenate([kv_past[1].astype(v_new.dtype), v_new], axis=1)
                    attend = functools.partial(dilated_gather_attention, k_all=k_all, v_all=v_all)
            j = layer - N_A_LAYERS
            y = dilated_mixture(rms_norm(x, norms[layer, 0]), b_w_q[j], b_w_o[j], attend)
        x = x + rms_norm(y, norms[layer, 1])
        x = x + rms_norm(swiglu(rms_norm(x, norms[layer, 2]), ffn_w_in[layer], ffn_w_out[layer]), norms[layer, 3])
    return x, jnp.stack(new_hist, axis=0), jnp.stack(new_delta, axis=0), k_new, v_new


def setup_inputs(seed: int = 0) -> dict:
    key = jax.random.key(seed)
    ks = jax.random.split(key, 20)
    f32 = jnp.float32
    win_buf = min(MAX_WINDOW, PAST_LEN)

    def nrm(k, shape, scale):
        return jax.random.normal(k, shape, f32) * scale

    dt = jnp.exp(jax.random.uniform(ks[11], (N_A_LAYERS, A_HEADS), f32, math.log(1e-3), math.log(1e-1)))
    return {
        'x_prompt': nrm(ks[0], (BATCH, SEQ, D_MODEL), 1.0),
        'x_sample': nrm(ks[1], (DEC_BATCH, DEC_SEQ, D_MODEL), 1.0),
        'state_conv': nrm(ks[2], (N_A_LAYERS, DEC_BATCH, CONV_W - 1, A_CONV_DIM), 1.0),
        'state_delta': nrm(ks[3], (N_A_LAYERS, DEC_BATCH, A_HEADS, A_DK, A_DV), 0.3),
        'cache_k': nrm(ks[4], (DEC_BATCH, win_buf, B_KV_HEADS, B_DH), 1.0),
        'cache_v': nrm(ks[5], (DEC_BATCH, win_buf, B_KV_HEADS, B_DH), 1.0),
        'norms': 1.0 + nrm(ks[6], (DEPTH, 4, D_MODEL), 0.05),
        'kv_norm': 1.0 + nrm(ks[7], (D_MODEL,), 0.05),
        'a_w_in': nrm(ks[8], (N_A_LAYERS, D_MODEL, A_PROJ_DIM), D_MODEL ** -0.5),
        'a_conv_w': nrm(ks[9], (N_A_LAYERS, CONV_W, A_CONV_DIM), CONV_W ** -0.5),
        'a_log': jnp.log(jax.random.uniform(ks[10], (N_A_LAYERS, A_HEADS), f32, 1.0, 16.0)),
        'a_dt_bias': dt + jnp.log(-jnp.expm1(-dt)),
        'a_o_gain': 1.0 + nrm(ks[12], (N_A_LAYERS, A_DV), 0.05),
        'a_w_out': nrm(ks[13], (N_A_LAYERS, A_V_DIM, D_MODEL), A_V_DIM ** -0.5),
        'b_w_kv': nrm(ks[14], (D_MODEL, 2 * B_KV_HEADS * B_DH), D_MODEL ** -0.5),
        'b_w_q': nrm(ks[15], (N_B_LAYERS, D_MODEL, N_GROUPS * B_HEADS * B_DH), D_MODEL ** -0.5),
        'b_w_o': nrm(ks[16], (N_B_LAYERS, B_HEADS * B_DH, D_MODEL), (B_HEADS * B_DH) ** -0.5),
        'ffn_w_in': nrm(ks[17], (DEPTH, D_MODEL, 2 * D_FF), D_MODEL ** -0.5),
        'ffn_w_out': nrm(ks[18], (DEPTH, D_FF, D_MODEL), D_FF ** -0.5),
    }


def reference(x_prompt, x_sample, state_conv, state_delta, cache_k, cache_v, norms, kv_norm, a_w_in,
              a_conv_w, a_log, a_dt_bias, a_o_gain, a_w_out, b_w_kv, b_w_q, b_w_o, ffn_w_in, ffn_w_out):
    bp, sp = x_prompt.shape[0], x_prompt.shape[1]
    zero_hist = jnp.zeros((N_A_LAYERS, bp, CONV_W - 1, A_CONV_DIM), x_prompt.dtype)
    zero_delta = jnp.zeros((N_A_LAYERS, bp, A_HEADS, A_DK, A_DV), x_prompt.dtype)
    y_prompt, conv_p, delta_p, k_p, v_p = decoder_trunk(
        x_prompt, zero_hist, zero_delta, None, norms, kv_norm, a_w_in, a_conv_w, a_log, a_dt_bias,
        a_o_gain, a_w_out, b_w_kv, b_w_q, b_w_o, ffn_w_in, ffn_w_out)
    win_p = min(MAX_WINDOW, sp)
    cache_k_prompt = k_p[:, sp - win_p:]
    cache_v_prompt = v_p[:, sp - win_p:]
    y_sample, conv_s, delta_s, k_s, v_s = decoder_trunk(
        x_sample, state_conv, state_delta, (cache_k, cache_v), norms, kv_norm, a_w_in, a_conv_w, a_log,
        a_dt_bias, a_o_gain, a_w_out, b_w_kv, b_w_q, b_w_o, ffn_w_in, ffn_w_out)
    return (y_prompt, y_sample, conv_p, delta_p, cache_k_prompt, cache_v_prompt, conv_s, delta_s, k_s, v_s)
```

```python
import math
import numpy as np
import ml_dtypes
import concourse.bass as bass
import concourse.mybir as mybir
from concourse.bass_utils import run_bass_kernel_spmd
from contextlib import ExitStack

F32, BF16 = mybir.dt.float32, mybir.dt.bfloat16
ALU, AF = mybir.AluOpType, mybir.ActivationFunctionType
AX = mybir.AxisListType

D = 1024
SEQ = 2048
NCORE = 8
NSEQ_S = 16
LS = 8
EPS = 1e-6
DFF = 2816
NFF = 22
QKV = 3072
APROJ = 4112
NDS = 8
WSLOT = 3072
NWS = 4
NEGBIG = -1.0e5
PG = 512
ARBYTES = 83456
B_GROUPS = ((128, 1), (512, 4), (2048, 16))


NCT = 8
NKT = 13


def sample_key_pos(t):
    k = np.arange(128)
    if t < NCT:
        return np.where(k < 96, 16 * k + t, -1)
    if t < NKT - 1:
        return 1536 + 128 * (t - NCT) + k
    return np.where(k < LS, SEQ + k, -1)


def alibi_slopes():
    n = 48
    return (2.0 ** (-8.0 * np.arange(1, n + 1) / n)).astype(np.float32).reshape(3, 16)


class Sched:
    def __init__(s, nc, es, dry):
        s.nc, s.dry = nc, dry
        s.eng = dict(pe=nc.tensor, act=nc.scalar, dve=nc.vector, pool=nc.gpsimd, sp=nc.sync)
        s.cnt = dict(pe=0, act=0, dve=0, pool=0)
        s.state = {}
        s.waited = {e: {} for e in s.eng}
        s.dcnt = {q: [0] * NDS for q in ('sp', 'pool')}
        s.dnext = {q: 0 for q in ('sp', 'pool')}
        s.nins = 0
        if not dry:
            s.sem = {e: es.enter_context(nc.semaphore("sem_" + e)) for e in s.cnt}
            s.dsem = {q: [es.enter_context(nc.semaphore(f"dsem_{q}{i}")) for i in range(NDS)]
                      for q in ('sp', 'pool')}

    def _semh(s, key):
        return s.sem[key] if isinstance(key, str) else s.dsem[key[0]][key[1]]

    def _wait(s, e, ev):
        key, val = ev
        if s.waited[e].get(key, 0) >= val:
            return
        s.waited[e][key] = val
        if not s.dry:
            s.eng[e].wait_ge(s._semh(key), val)

    def _sync(s, e, reads, writes, is_dma):
        for k in reads:
            st = s.state.get(k)
            if st and st[0] is not None:
                ev = st[0]
                if e == 'pe' and ev[0] == 'pe':
                    continue
                s._wait(e, ev)
        for k in writes:
            st = s.state.get(k)
            if not st:
                continue
            evs = ([st[0]] if st[0] is not None else []) + list(st[1].items())
            for ev in evs:
                if (not is_dma) and ev[0] == e:
                    continue
                s._wait(e, ev)

    def _record(s, ev, reads, writes):
        for k in reads:
            st = s.state.setdefault(k, [None, {}])
            st[1][ev[0]] = max(st[1].get(ev[0], 0), ev[1])
        for k in writes:
            s.state[k] = [ev, {}]

    def op(s, e, fn, reads=(), writes=(), inc=True):
        s._sync(e, reads, writes, False)
        s.nins += 1
        if inc:
            s.cnt[e] += 1
            ev = (e, s.cnt[e])
        else:
            ev = (e, s.cnt[e] + 1)
        if not s.dry:
            ins = fn(s.eng[e])
            if inc:
                ins.then_inc(s.sem[e], 1)
        s._record(ev, reads, writes)

    def dma(s, q, out, in_, reads=(), writes=()):
        i = s.dnext[q]
        s.dnext[q] = (i + 1) % NDS
        if s.dcnt[q][i]:
            s._wait(q, ((q, i), s.dcnt[q][i]))
        s._sync(q, reads, writes, True)
        s.dcnt[q][i] += 16
        ev = ((q, i), s.dcnt[q][i])
        s.nins += 1
        if not s.dry:
            s.eng[q].dma_start(out=out, in_=in_).then_inc(s.dsem[q][i], 16)
        s._record(ev, reads, writes)

    def finish(s):
        for q in ('sp', 'pool'):
            for i in range(NDS):
                if s.dcnt[q][i]:
                    s._wait('sp', ((q, i), s.dcnt[q][i]))


class AV:
    def __init__(s, ap, off, nb, shape, es):
        s.ap, s.off, s.nb, s.shape, s.es = ap, off, nb, shape, es

    def k(s, lo=0, hi=None):
        hi = s.nb if hi is None else hi
        return [('pg', p) for p in range((s.off + lo) // PG, (s.off + hi - 1) // PG + 1)]

    def kc(s, c, n=1):
        per = s.nb // s.shape[1]
        return s.k(c * per, (c + n) * per)


def bc(ap, shape, axis):
    return ap.unsqueeze(axis).to_broadcast(list(shape))


CF = {}
CB = {}


def _build_consts():
    f = []
    off = 0

    def addf(name, arr):
        nonlocal off
        arr = np.asarray(arr, np.float32)
        assert arr.shape[0] == 128
        CF[name] = (off, arr.shape[1])
        f.append(arr)
        off += arr.shape[1]

    i = np.arange(128)
    q, s_ = i[:, None], i[None, :]
    same = (q // LS) == (s_ // LS)
    addf('ident', np.eye(128))
    addf('ones', np.ones((128, 128)))
    addf('negones', -np.ones((128, 128)))
    addf('tri_p', (q <= s_))
    addf('tri_s', (q <= s_) & same)
    addf('suf_p', (q > s_))
    addf('suf_s', (q > s_) & same)
    addf('neg1_p', np.where(q > s_, 0.0, NEGBIG))
    addf('neg2_p', np.where(s_ > q, 0.0, NEGBIG))
    addf('neg3_p', np.where(s_ >= q, 0.0, NEGBIG))
    addf('neg1_s', np.where((q > s_) & same, 0.0, NEGBIG))
    addf('neg2_s', np.where((s_ > q) & same, 0.0, NEGBIG))
    addf('neg3_s', np.where((s_ >= q) & same, 0.0, NEGBIG))
    addf('seqsel', (i[:, None] // LS) == np.arange(NSEQ_S)[None, :])
    j = np.arange(256)[None, :]
    addf('deltac', np.maximum(j - q, 0).astype(np.float32))
    cf = np.concatenate(f, axis=1)

    b = []
    offb = 0

    def addb(name, arr):
        nonlocal offb
        arr = np.asarray(arr, np.float32)
        CB[name] = (offb, arr.shape[1])
        b.append(arr)
        offb += arr.shape[1]

    addb('ident', np.eye(128))
    addb('ones', np.ones((128, 128)))
    dl = j - q
    addb('maskj', np.where(j < 128, dl >= 0, dl <= 128))
    sl = alibi_slopes()
    tab = np.zeros((NKT, 128, 4, 3, 2, 2, LS), np.float32)
    for t in range(NKT):
        pos = sample_key_pos(t)
        for l in range(LS):
            dist = (SEQ + l) - pos
            for grp, (win, dil) in enumerate(B_GROUPS):
                valid = (pos >= 0) & (dist >= 0) & (dist <= win) & (dist % dil == 0)
                for g in range(4):
                    for hh in range(2):
                        for pp in range(2):
                            h = 4 * g + 2 * pp + hh
                            tab[t, :, g, grp, pp, hh, l] = np.where(
                                valid, np.exp(-sl[grp, h] * np.maximum(dist, 0).astype(np.float64)), 0.0)
    dcm = np.zeros((128, 14, 128), np.float32)
    for lev in range(7):
        m = 1 << lev
        M = ((q // (2 * m)) == (s_ // (2 * m))) & ((q % (2 * m)) >= m) & ((s_ % (2 * m)) < m)
        dcm[:, 2 * lev, :] = M
        dcm[:, 2 * lev + 1, :] = M.T
    global DCM_ARR
    DCM_ARR = np.ascontiguousarray(dcm.reshape(128, 14 * 128)).astype(ml_dtypes.bfloat16)
    esamp = np.ascontiguousarray(tab.transpose(1, 0, 2, 3, 4, 5, 6).reshape(128, NKT * 384)).astype(ml_dtypes.bfloat16)
    cb = np.concatenate(b, axis=1).astype(ml_dtypes.bfloat16)
    return cf, cb, esamp


class Builder:
    HTK = [('HT', c) for c in range(8)]

    def __init__(b, nc, es, dry, wplan, stage):
        b.nc, b.es, b.dry, b.stage = nc, es, dry, stage
        b.S = Sched(nc, es, dry)
        b.wplan = wplan
        b.wcur = 0
        b.wissued = 0
        b.psrr = 0
        b.dram = {}
        b.aoff = 0
        b.nrr = 4

    def sb(b, name, shape, dt):
        return b.es.enter_context(b.nc.sbuf_tensor(name, list(shape), dt))

    def din(b, name, shape, dt=F32):
        t = b.nc.dram_tensor(name, list(shape), dt, kind="ExternalInput").ap()
        b.dram[name] = t
        return t

    def dout(b, name, shape, dt=F32):
        t = b.nc.dram_tensor(name, list(shape), dt, kind="ExternalOutput").ap()
        b.dram[name] = t
        return t

    def bank(b, i, n=1):
        return b.PS[:, 512 * i:512 * (i + n)]

    def bankb(b, i):
        return b.PS[:, 512 * i:512 * (i + 1)].bitcast(BF16)

    def rr(b, n=1):
        if b.psrr + n > b.nrr:
            b.psrr = 0
        i = b.psrr
        b.psrr = (b.psrr + n) % b.nrr
        return i

    @staticmethod
    def pk(i, n=1):
        return [('ps', i + k) for k in range(n)]

    def wnext(b, spec):
        if b.dry:
            b.wplan.append(spec)
            idx = len(b.wplan) - 1
        else:
            idx = b.wcur
            assert b.wplan[idx] == spec, (b.wplan[idx], spec)
            b.wcur += 1
            while b.wissued < min(len(b.wplan), idx + NWS - 1):
                b._wissue(b.wissued)
                b.wissued += 1
        name, lyr, r0, nr, c0, ncols = spec
        kc = nr // 128
        slot = idx % NWS
        ap = b.WB[:, slot * WSLOT: slot * WSLOT + kc * ncols].rearrange("p (c n) -> p c n", c=kc)
        return ap, ('w', slot)

    def _wissue(b, idx):
        name, lyr, r0, nr, c0, ncols = b.wplan[idx]
        kc = nr // 128
        assert kc * ncols <= WSLOT
        slot = idx % NWS
        nb = b.nblk
        g, i = idx // nb, idx % nb
        dst2 = b.WB[:, slot * WSLOT: slot * WSLOT + kc * ncols]
        if g == 0:
            src = b.dram[name]
            if lyr is not None:
                src = src[lyr]
            src = src[r0:r0 + nr, c0:c0 + ncols].rearrange("(c p) n -> p c n", p=128)
            b.S.dma('pool', dst2.rearrange("p (c n) -> p c n", c=kc), src, writes=[('w', slot)])
            b.S.dma('sp', b.WSCR[i, :, 0:kc * ncols], dst2, reads=[('w', slot)], writes=[('wscr', i)])
        else:
            assert b.wplan[i] == b.wplan[idx]
            b.S.dma('sp', dst2, b.WSCR[i, :, 0:kc * ncols], reads=[('wscr', i)], writes=[('w', slot)])

    def pieces(b, ntok):
        return [(o, min(512, ntok - o)) for o in range(0, ntok, 512)]

    def fm_matmul(b, ps_ap, w_ap, wkey, col0, ncol, act_ap, actkeys, kcn, pskeys):
        for kc in range(kcn):
            b.S.op('pe', lambda e, kc=kc: e.matmul(ps_ap, lhsT=w_ap[:, kc, col0:col0 + ncol], rhs=act_ap(kc),
                                                  start=(kc == 0), stop=(kc == kcn - 1)),
                   reads=[wkey] + list(actkeys), writes=pskeys, inc=(kc == kcn - 1))


    def abegin(b):
        b.aoff = 0

    def aalloc(b, shape, dt):
        es = 4 if dt == F32 else 2
        nel = int(np.prod(shape[1:]))
        nb = (nel * es + 511) // 512 * 512
        off = b.aoff
        b.aoff += nb
        assert b.aoff <= ARBYTES, ("arena overflow", b.aoff)
        ap = b.AR[0:shape[0], off // 2: off // 2 + nel * es // 2]
        if dt == F32:
            ap = ap.bitcast(F32)
        if len(shape) == 3:
            ap = ap.rearrange("p (a c) -> p a c", a=shape[1])
        elif len(shape) == 4:
            ap = ap.rearrange("p (a c d) -> p a c d", a=shape[1], c=shape[2])
        return AV(ap, off, nel * es, shape, es)

    def cf(b, name, c0=0, c1=None):
        o, w = CF[name]
        return b.CFT[:, o + c0: o + (w if c1 is None else c1)]

    def cb(b, name, c0=0, c1=None):
        o, w = CB[name]
        return b.CBT[:, o + c0: o + (w if c1 is None else c1)]

    def ncol(b, l, i):
        k = (l * 4 + i) * 8
        return b.NORMC[:, k:k + 8]

    def _rstd_fm(b, src, srckey, o, n, SQ, RS):
        S = b.S
        S.op('act', lambda e: e.activation(out=SQ.ap[:, 0:5, 0:n], in_=src[:, 0:5, o:o + n], func=AF.Square),
             reads=[srckey], writes=SQ.kc(0, 5))
        S.op('dve', lambda e: e.tensor_tensor(out=SQ.ap[:, 5:8, 0:n], in0=src[:, 5:8, o:o + n], in1=src[:, 5:8, o:o + n], op=ALU.mult),
             reads=[srckey], writes=SQ.kc(5, 3))
        bi = b.rr()
        for c in range(8):
            S.op('pe', lambda e, c=c: e.matmul(b.bank(bi)[:, 0:n], lhsT=b.cb('ones'), rhs=SQ.ap[:, c, 0:n],
                                               start=(c == 0), stop=(c == 7)),
                 reads=SQ.kc(c), writes=b.pk(bi), inc=(c == 7))
        S.op('act', lambda e: e.activation(out=RS.ap[:, 0:n], in_=b.bank(bi)[:, 0:n], func=AF.Ln,
                                           bias=b.epsc[:, 0:1], scale=1.0 / D),
             reads=[], writes=b.pk(bi) + RS.k())
        S.op('act', lambda e: e.activation(out=RS.ap[:, 0:n], in_=RS.ap[:, 0:n], func=AF.Exp, scale=-0.5),
             reads=RS.k(), writes=RS.k())

    def rmsnorm_fm(b, src, srckey, gain_col, dst, dstkey, ntok):
        S = b.S
        save = b.aoff
        SQ = b.aalloc([128, 8, 512], BF16)
        RS = b.aalloc([128, 512], F32)
        for (o, n) in b.pieces(ntok):
            b._rstd_fm(src, srckey, o, n, SQ, RS)
            for c in range(8):
                eng = 'dve'
                S.op(eng, lambda e, c=c: e.scalar_tensor_tensor(
                    out=dst[:, c, o:o + n], in0=src[:, c, o:o + n], scalar=gain_col[:, c:c + 1],
                    in1=RS.ap[:, 0:n], op0=ALU.mult, op1=ALU.mult),
                    reads=[srckey] + RS.k(), writes=(dstkey(c) if callable(dstkey) else [(dstkey, c)]))
        b.aoff = save

    def postnorm_add(b, gain_col, ntok):
        S = b.S
        save = b.aoff
        SQ = b.aalloc([128, 8, 512], BF16)
        RS = b.aalloc([128, 512], F32)
        for (o, n) in b.pieces(ntok):
            b._rstd_fm(b.YT, 'YT', o, n, SQ, RS)
            for c in range(8):
                S.op('dve', lambda e, c=c: e.scalar_tensor_tensor(
                    out=b.YT[:, c, o:o + n], in0=b.YT[:, c, o:o + n], scalar=gain_col[:, c:c + 1],
                    in1=RS.ap[:, 0:n], op0=ALU.mult, op1=ALU.mult),
                    reads=['YT'] + RS.k(), writes=['YT'])
            S.op('dve', lambda e: e.tensor_tensor(out=b.XT[:, :, o:o + n], in0=b.XT[:, :, o:o + n],
                                                  in1=b.YT[:, :, o:o + n], op=ALU.add),
                 reads=['YT', 'XT'], writes=['XT'])
        b.aoff = save

    def ffn(b, l, ntok):
        S = b.S
        b.nrr = 6
        b.psrr = 0
        b.abegin()
        ACT = b.aalloc([128, NFF, 512], BF16)
        TMPB = b.aalloc([128, 2, 512], BF16)
        b.rmsnorm_fm(b.XT, 'XT', b.ncol(l, 2), b.HT, 'HT', ntok)
        it = 0
        for blk in range(11):
            wg, kg = b.wnext(('ffn_w_in', l, 0, D, 256 * blk, 256))
            wu, ku = b.wnext(('ffn_w_in', l, 0, D, DFF + 256 * blk, 256))
            for j in range(2):
                fc = 2 * blk + j
                for (o, n) in b.pieces(ntok):
                    bg, bu = b.rr(), b.rr()
                    b.fm_matmul(b.bank(bg)[:, 0:n], wg, kg, 128 * j, 128, lambda kc: b.HT[:, kc, o:o + n], b.HTK, 8, b.pk(bg))
                    b.fm_matmul(b.bank(bu)[:, 0:n], wu, ku, 128 * j, 128, lambda kc: b.HT[:, kc, o:o + n], b.HTK, 8, b.pk(bu))
                    tb = it % 2
                    it += 1
                    S.op('act', lambda e: e.activation(out=TMPB.ap[:, tb, 0:n], in_=b.bank(bg)[:, 0:n], func=AF.Silu),
                         reads=[], writes=b.pk(bg) + TMPB.kc(tb))
                    S.op('dve', lambda e: e.tensor_tensor(out=ACT.ap[:, fc, o:o + n], in0=b.bank(bu)[:, 0:n],
                                                          in1=TMPB.ap[:, tb, 0:n], op=ALU.mult),
                         reads=TMPB.kc(tb), writes=b.pk(bu) + ACT.kc(fc))
        for dc in range(8):
            w, k = b.wnext(('ffn_w_out', l, 0, DFF, 128 * dc, 128))
            for (o, n) in b.pieces(ntok):
                bi = b.rr()
                b.fm_matmul(b.bank(bi)[:, 0:n], w, k, 0, 128, lambda kc: ACT.ap[:, kc, o:o + n],
                            ACT.k(), NFF, b.pk(bi))
                S.op('act', lambda e: e.activation(out=b.YT[:, dc, o:o + n], in_=b.bank(bi)[:, 0:n], func=AF.Copy),
                     reads=[], writes=b.pk(bi) + ['YT'])
        b.postnorm_add(b.ncol(l, 3), ntok)

    def gdn(b, l, g, ntok):
        S = b.S
        sample = (g == 4)
        b.nrr = 4 if sample else 6
        b.psrr = 0
        nt = ntok // 128
        n = ntok
        sfx = '_s' if sample else '_p'
        nsq = 2 if sample else 6
        b.abegin()
        b.rmsnorm_fm(b.XT, 'XT', b.ncol(l, 0), b.HT, 'HT', ntok)
        QK = b.aalloc([128, 16, n], BF16)
        VT = b.aalloc([128, 8, n], BF16)
        OT = b.aalloc([128, 8, n], BF16)
        BD = b.aalloc([128, 4, 16], F32)
        SM = b.aalloc([128, 10, 4, 8], F32)
        sm = lambda i: SM.ap[:, i, 0:nt, :]
        mark = b.aoff

        w, wk = b.wnext(('a_w_in', l, 0, D, 4096, 16))
        for t in range(nt):
            bi = b.rr()
            for kc in range(8):
                S.op('pe', lambda e, kc=kc: e.matmul(b.bank(bi)[:, 0:16], lhsT=b.HT[:, kc, 128 * t:128 * (t + 1)],
                                                     rhs=w[:, kc, :], start=(kc == 0), stop=(kc == 7)),
                     reads=b.HTK + [wk], writes=b.pk(bi), inc=(kc == 7))
            S.op('act', lambda e: e.activation(out=BD.ap[:, t, :], in_=b.bank(bi)[:, 0:16], func=AF.Copy),
                 reads=[], writes=b.pk(bi) + BD.k())
        braw, draw = BD.ap[:, 0:nt, 0:8], BD.ap[:, 0:nt, 8:16]
        smk = SM.k()
        S.op('dve', lambda e: e.scalar_tensor_tensor(out=sm(2), in0=braw, scalar=-1.0, in1=braw, op0=ALU.mult, op1=ALU.min), reads=BD.k(), writes=smk)
        S.op('act', lambda e: e.activation(out=sm(2), in_=sm(2), func=AF.Exp), reads=smk, writes=smk)
        S.op('act', lambda e: e.activation(out=sm(2), in_=sm(2), func=AF.Ln, bias=b.onec[:, 0:1]), reads=smk, writes=smk)
        S.op('dve', lambda e: e.scalar_tensor_tensor(out=sm(1), in0=braw, scalar=0.0, in1=sm(2), op0=ALU.min, op1=ALU.subtract),
             reads=BD.k() + smk, writes=smk)
        S.op('dve', lambda e: e.tensor_tensor(out=sm(3), in0=draw, in1=bc(b.DTB[:, 8 * l:8 * l + 8], [128, nt, 8], 1), op=ALU.add),
             reads=BD.k(), writes=smk)
        S.op('dve', lambda e: e.scalar_tensor_tensor(out=sm(2), in0=sm(3), scalar=-1.0, in1=sm(3), op0=ALU.mult, op1=ALU.min), reads=smk, writes=smk)
        S.op('act', lambda e: e.activation(out=sm(2), in_=sm(2), func=AF.Exp), reads=smk, writes=smk)
        S.op('act', lambda e: e.activation(out=sm(2), in_=sm(2), func=AF.Ln, bias=b.onec[:, 0:1]), reads=smk, writes=smk)
        S.op('dve', lambda e: e.scalar_tensor_tensor(out=sm(3), in0=sm(3), scalar=0.0, in1=sm(2), op0=ALU.max, op1=ALU.add),
             reads=smk, writes=smk)
        S.op('dve', lambda e: e.tensor_tensor(out=sm(0), in0=sm(3), in1=bc(b.AEXPN[:, 8 * l:8 * l + 8], [128, nt, 8], 1), op=ALU.mult),
             reads=smk, writes=smk)

        CBUF = b.aalloc([128, 3, n + 48], F32)
        CACC2 = b.aalloc([128, 3, n], F32)
        CSIL2 = b.aalloc([128, 3, n], F32)
        RI2 = b.aalloc([128, 3, n], F32)
        if sample:
            SCIN = b.aalloc([48, QKV], F32)
            SHIST = b.aalloc([128, 24, 48], F32)
            SHO = b.aalloc([128, 24, 48], F32)
            S.dma('sp', SCIN.ap, b.dram['sconv'][l].rearrange("b j c -> (b j) c"), writes=SCIN.k())
            for r in range(3):
                bi = b.rr()
                for c8 in range(8):
                    c = 8 * r + c8
                    S.op('pe', lambda e, c=c, c8=c8: e.transpose(out=b.bank(bi)[:, 48 * c8:48 * (c8 + 1)],
                                                              in_=SCIN.ap[:, 128 * c:128 * (c + 1)], identity=b.cf('ident', 0, 48)[0:48, :]),
                         reads=SCIN.k(), writes=b.pk(bi), inc=(c8 == 7))
                S.op('act', lambda e: e.activation(out=SHIST.ap[:, 8 * r:8 * r + 8, :],
                                                   in_=b.bank(bi)[:, 0:384].rearrange("p (c j) -> p c j", c=8), func=AF.Copy),
                     reads=[], writes=b.pk(bi) + SHIST.k())
        sub = lambda V3, j: AV(V3.ap[:, j, :], V3.off + j * (V3.nb // 3), V3.nb // 3, [128, V3.shape[2]], 4)
        for blk in range(8):
            w, wk = b.wnext(('a_w_in', l, 0, D, 384 * blk, 384))
            st = []
            for j in range(3):
                c = 3 * blk + j
                bi = b.rr()
                b.fm_matmul(b.bank(bi)[:, 0:n], w, wk, 128 * j, 128, lambda kc: b.HT[:, kc, 0:n], b.HTK, 8, b.pk(bi))
                cbk = CBUF.kc(j)
                CACC, CSIL, RI = sub(CACC2, j), sub(CSIL2, j), sub(RI2, j)
                cw = (lambda c: (lambda jj: b.CONVW[:, (l * 24 + c) * 4 + jj:(l * 24 + c) * 4 + jj + 1]))(c)
                if not sample:
                    cb_ = CBUF.ap[:, j, :]
                    S.op('pool', lambda e: e.tensor_copy(out=cb_[:, 0:3], in_=b.HIST[:, l, c, :]), reads=['HIST'], writes=cbk)
                    S.op('act', lambda e: e.activation(out=cb_[:, 3:3 + n], in_=b.bank(bi)[:, 0:n], func=AF.Copy),
                         reads=[], writes=b.pk(bi) + cbk)
                    S.op('pool', lambda e: e.tensor_copy(out=b.HIST[:, l, c, :], in_=cb_[:, n:n + 3]), reads=cbk, writes=['HIST'])
                    xs = (lambda cb_: (lambda jj: cb_[:, jj:jj + n]))(cb_)
                    acc = CACC.ap[:, 0:n]
                else:
                    cb3 = CBUF.ap[:, j, 0:176].rearrange("p (s j) -> p s j", s=16)
                    S.op('pool', lambda e: e.tensor_copy(out=cb3[:, :, 0:3], in_=SHIST.ap[:, c, :].rearrange("p (s j) -> p s j", s=16)),
                         reads=SHIST.k(), writes=cbk)
                    S.op('act', lambda e: e.activation(out=cb3[:, :, 3:11], in_=b.bank(bi)[:, 0:n].rearrange("p (s j) -> p s j", s=16), func=AF.Copy),
                         reads=[], writes=b.pk(bi) + cbk)
                    S.op('pool', lambda e: e.tensor_copy(out=SHO.ap[:, c, :].rearrange("p (s j) -> p s j", s=16), in_=cb3[:, :, 8:11]),
                         reads=cbk, writes=SHO.k())
                    xs = (lambda cb3: (lambda jj: cb3[:, :, jj:jj + 8]))(cb3)
                    acc = CACC.ap[:, 0:n].rearrange("p (s j) -> p s j", s=16)
                S.op('pool', lambda e: e.tensor_scalar(out=acc, in0=xs(0), scalar1=cw(0), scalar2=0.0, op0=ALU.mult, op1=ALU.add),
                     reads=cbk, writes=CACC.k())
                st.append((c, cbk, CACC, CSIL, RI, cw, xs, acc))
            for (c, cbk, CACC, CSIL, RI, cw, xs, acc) in st:
                for jj in range(1, 4):
                    S.op('dve', lambda e, jj=jj: e.scalar_tensor_tensor(out=acc, in0=xs(jj), scalar=cw(jj), in1=acc, op0=ALU.mult, op1=ALU.add),
                         reads=cbk + CACC.k(), writes=CACC.k())
            for (c, cbk, CACC, CSIL, RI, cw, xs, acc) in st:
                if c < 16:
                    S.op('act', lambda e: e.activation(out=CSIL.ap[:, 0:n], in_=CACC.ap[:, 0:n], func=AF.Silu), reads=CACC.k(), writes=CSIL.k())
                else:
                    S.op('act', lambda e: e.activation(out=VT.ap[:, c - 16, 0:n], in_=CACC.ap[:, 0:n], func=AF.Silu),
                         reads=CACC.k(), writes=VT.kc(c - 16))
            qk = [x for x in st if x[0] < 16]
            banks = []
            for (c, cbk, CACC, CSIL, RI, cw, xs, acc) in qk:
                S.op('dve', lambda e: e.tensor_tensor(out=RI.ap[:, 0:n], in0=CSIL.ap[:, 0:n], in1=CSIL.ap[:, 0:n], op=ALU.mult), reads=CSIL.k(), writes=RI.k())
                b2 = b.rr()
                S.op('pe', lambda e: e.matmul(b.bank(b2)[:, 0:n], lhsT=b.cf('ones'), rhs=RI.ap[:, 0:n], start=True, stop=True),
                     reads=RI.k(), writes=b.pk(b2))
                banks.append(b2)
            for (c, cbk, CACC, CSIL, RI, cw, xs, acc), b2 in zip(qk, banks):
                S.op('act', lambda e: e.activation(out=RI.ap[:, 0:n], in_=b.bank(b2)[:, 0:n], func=AF.Ln, bias=b.epsc[:, 0:1]),
                     reads=[], writes=b.pk(b2) + RI.k())
            for (c, cbk, CACC, CSIL, RI, cw, xs, acc) in qk:
                S.op('act', lambda e: e.activation(out=RI.ap[:, 0:n], in_=RI.ap[:, 0:n], func=AF.Exp, scale=-0.5,
                                                   bias=(b.lnqs[:, 0:1] if c < 8 else b.zeroc[:, 0:1])),
                     reads=RI.k(), writes=RI.k())
            for (c, cbk, CACC, CSIL, RI, cw, xs, acc) in qk:
                S.op('dve', lambda e: e.tensor_tensor(out=QK.ap[:, c, 0:n], in0=CSIL.ap[:, 0:n], in1=RI.ap[:, 0:n], op=ALU.mult),
                     reads=CSIL.k() + RI.k(), writes=QK.kc(c))
        if sample:
            b._conv_out(SHO.ap, SHO.k(), 48, b.dram['convs'][l].rearrange("b j c -> (b j) c"))
        elif g == 3:
            b._conv_out(b.HIST[:, l], ['HIST'], 3, b.dram['convp'][l])
        b.aoff = mark

        KTOK = b.aalloc([128, 8, 128], BF16)
        VTOK = b.aalloc([128, 8, 128], BF16)
        VB = b.aalloc([128, 8, 128], BF16)
        KDK = b.aalloc([128, 8, 128], BF16)
        UB = b.aalloc([128, 8, 128], BF16)
        O32 = b.aalloc([128, 8, 128], F32)
        ON = KTOK
        T8 = b.aalloc([128, 12, 8], F32)
        RG = b.aalloc([128, 4, 128], F32)
        LBI = b.aalloc([128, 4, 128], F32)
        NB = b.aalloc([128, 2, 512], F32)
        E1F = b.aalloc([128, 512], F32)
        E3B = b.aalloc([128, 512], BF16)
        AA = b.aalloc([128, 2, 512], F32)
        A32 = AV(AA.ap[:, 0, :], AA.off, AA.nb // 2, [128, 512], 4)
        E2F = AV(AA.ap[:, 1, :], AA.off + AA.nb // 2, AA.nb // 2, [128, 512], 4)
        WS = b.aalloc([128, 2, 512], F32)
        AOF2 = b.aalloc([128, 2, 1024], BF16)
        XRB = b.aalloc([128, 4, 128], BF16)
        TPB = b.aalloc([128, 2, 512], BF16)
        DCM = b.aalloc([128, 14, 128], BF16)
        ATT = b.aalloc([128, 4, 128], BF16)
        TF = WS
        WSB = AV(WS.ap[:, 0, :].bitcast(BF16).rearrange("p (a c) -> p a c", a=2), WS.off, WS.nb // 2, [128, 2, 512], 2)
        S.dma('sp', DCM.ap.rearrange("p a c -> p (a c)"), b.dram['dcm'], writes=DCM.k())
        if sample:
            GSEL = b.aalloc([128, 16, 8], F32)
            GES = b.aalloc([128, 16, 8], F32)
            SL32 = b.aalloc([128, 2, 1024], F32)
            SLB = b.aalloc([128, 2, 1024], BF16)
            KQS = b.aalloc([128, 2, 1024], F32)
            UM = b.aalloc([128, 8, 128], BF16)
        SBh = b.SB
        t8 = lambda i: T8.ap[:, i, :]
        t8k = T8.k()
        tri, suf = b.cf('tri' + sfx), b.cf('suf' + sfx)
        neg = [b.cf('neg%d%s' % (i, sfx)) for i in (1, 2, 3)]
        identb = b.cb('ident')
        h4 = lambda ap3: ap3.rearrange("p (h e) -> p h e", h=4)

        for t in range(nt):
            tc_ = slice(128 * t, 128 * (t + 1))
            G_t, LB_t = SM.ap[:, 0, t, :], SM.ap[:, 1, t, :]
            for (src0, dstv) in ((8, KTOK), (None, VTOK)):
                bi = b.rr()
                for h in range(8):
                    srcap = QK.ap[:, src0 + h, tc_] if src0 is not None else VT.ap[:, h, tc_]
                    S.op('pe', lambda e, h=h, srcap=srcap: e.transpose(out=b.bankb(bi)[:, 128 * h:128 * (h + 1)], in_=srcap, identity=identb),
                         reads=(QK.kc(8 + h) if src0 is not None else VT.kc(h)), writes=b.pk(bi), inc=(h == 7))
                S.op('act', lambda e: e.activation(out=dstv.ap, in_=b.bankb(bi).rearrange("p (h e) -> p h e", h=8), func=AF.Copy),
                     reads=[], writes=b.pk(bi) + dstv.k())
            bi = b.rr()
            S.op('pe', lambda e: e.matmul(b.bank(bi)[:, 0:8], lhsT=tri, rhs=G_t, start=True, stop=True), reads=smk, writes=b.pk(bi))
            S.op('pe', lambda e: e.matmul(b.bank(bi)[:, 8:16], lhsT=suf, rhs=G_t, start=True, stop=True), reads=smk, writes=b.pk(bi))
            if not sample:
                S.op('pe', lambda e: e.matmul(b.bank(bi)[:, 16:24], lhsT=b.cf('ones'), rhs=G_t, start=True, stop=True), reads=smk, writes=b.pk(bi))
            S.op('act', lambda e: e.activation(out=t8(0), in_=b.bank(bi)[:, 0:8], func=AF.Copy), reads=[], writes=b.pk(bi) + t8k)
            S.op('act', lambda e: e.activation(out=t8(5), in_=b.bank(bi)[:, 8:16], func=AF.Exp), reads=[], writes=b.pk(bi) + t8k)
            if not sample:
                S.op('act', lambda e: e.activation(out=t8(6), in_=b.bank(bi)[:, 16:24], func=AF.Exp), reads=[], writes=b.pk(bi) + t8k)
            S.op('dve', lambda e: e.tensor_tensor(out=t8(1), in0=t8(0), in1=LB_t, op=ALU.add), reads=t8k + smk, writes=t8k)
            S.op('act', lambda e: e.activation(out=t8(2), in_=t8(0), func=AF.Exp), reads=t8k, writes=t8k)
            S.op('act', lambda e: e.activation(out=t8(3), in_=t8(1), func=AF.Exp), reads=t8k, writes=t8k)
            S.op('act', lambda e: e.activation(out=t8(4), in_=LB_t, func=AF.Exp), reads=smk, writes=t8k)
            if sample:
                S.op('dve', lambda e: e.tensor_tensor(out=GSEL.ap, in0=bc(b.cf('seqsel'), [128, 16, 8], 2), in1=bc(G_t, [128, 16, 8], 1), op=ALU.mult),
                     reads=smk, writes=GSEL.k())
                b2 = b.rr()
                S.op('pe', lambda e: e.matmul(b.bank(b2)[:, 0:128], lhsT=b.cf('ones'), rhs=GSEL.ap.rearrange("p s h -> p (s h)"), start=True, stop=True),
                     reads=GSEL.k(), writes=b.pk(b2))
                S.op('act', lambda e: e.activation(out=GES.ap.rearrange("p s h -> p (s h)"), in_=b.bank(b2)[:, 0:128], func=AF.Exp),
                     reads=[], writes=b.pk(b2) + GES.k())
            S.op('dve', lambda e: e.tensor_tensor(out=VB.ap, in0=VTOK.ap, in1=bc(t8(4), [128, 8, 128], 2), op=ALU.mult),
                 reads=VTOK.k() + t8k, writes=VB.k())
            S.op('dve', lambda e: e.tensor_tensor(out=KDK.ap, in0=KTOK.ap, in1=bc(t8(5), [128, 8, 128], 2), op=ALU.mult),
                 reads=KTOK.k() + t8k, writes=KDK.k())
            if sample:
                bks, bqs = 6, 7
                for s_ in range(NSEQ_S):
                    sl = s_ % 2
                    S.dma('sp', SL32.ap[:, sl, :].rearrange("p (h e) -> p h e", h=8),
                          b.dram['sdelta'][l, s_].rearrange("h d e -> d h e"), writes=SL32.kc(sl))
                    S.op('act', lambda e: e.activation(out=SLB.ap[:, sl, :], in_=SL32.ap[:, sl, :], func=AF.Copy),
                         reads=SL32.kc(sl), writes=SLB.kc(sl))
                    for h in range(8):
                        lw = SLB.ap[:, sl, 128 * h:128 * (h + 1)]
                        S.op('pe', lambda e, h=h, lw=lw: e.matmul(b.PS[:, 2048 + 128 * h + 8 * s_: 2048 + 128 * h + 8 * s_ + 8], lhsT=lw,
                                                                  rhs=QK.ap[:, 8 + h, 8 * s_:8 * s_ + 8], start=True, stop=True),
                             reads=SLB.kc(sl) + QK.kc(8 + h), writes=b.pk(4, 2), inc=False)
                        S.op('pe', lambda e, h=h, lw=lw: e.matmul(b.PS[:, 3072 + 128 * h + 8 * s_: 3072 + 128 * h + 8 * s_ + 8], lhsT=lw,
                                                                  rhs=QK.ap[:, h, 8 * s_:8 * s_ + 8], start=True, stop=True),
                             reads=SLB.kc(sl) + QK.kc(h), writes=b.pk(6, 2), inc=(h == 7))
                S.op('act', lambda e: e.activation(out=KQS.ap[:, 0, :], in_=b.PS[:, 2048:3072], func=AF.Copy), reads=[], writes=b.pk(4, 2) + KQS.kc(0))
                S.op('act', lambda e: e.activation(out=KQS.ap[:, 1, :], in_=b.PS[:, 3072:4096], func=AF.Copy), reads=[], writes=b.pk(6, 2) + KQS.kc(1))
                for kq in range(2):
                    for h in range(8):
                        S.op('pe', lambda e, h=h, kq=kq: e.transpose(out=b.PS[:, 2048 + 1024 * kq + 128 * h: 2048 + 1024 * kq + 128 * (h + 1)],
                                                                     in_=KQS.ap[:, kq, 128 * h:128 * (h + 1)], identity=b.cf('ident')),
                             reads=KQS.kc(kq), writes=b.pk(4 + 2 * kq, 2), inc=(h == 7))

            for hf in range(2):
                hs = slice(4 * hf, 4 * hf + 4)
                S.op('dve', lambda e: e.tensor_tensor(out=RG.ap, in0=bc(tri, [128, 4, 128], 1), in1=bc(G_t[:, hs], [128, 4, 128], 2), op=ALU.mult),
                     reads=smk, writes=RG.k())
                S.op('dve', lambda e: e.tensor_tensor(out=LBI.ap, in0=bc(b.cf('ident'), [128, 4, 128], 1), in1=bc(LB_t[:, hs], [128, 4, 128], 2), op=ALU.mult),
                     reads=smk, writes=LBI.k())
                rgf = RG.ap.rearrange("p h e -> p (h e)")
                lbf = LBI.ap.rearrange("p h e -> p (h e)")
                edst = [E1F, E2F, E3B]
                for i in range(3):
                    nbi = i % 2
                    if i == 0:
                        S.op('dve', lambda e: e.tensor_tensor(out=h4(NB.ap[:, nbi, :]), in0=bc(neg[0], [128, 4, 128], 1), in1=bc(t8(1)[:, hs], [128, 4, 128], 2), op=ALU.add),
                             reads=t8k, writes=NB.kc(nbi))
                        terms = [(b.cf('negones'), rgf, RG.k()), (b.cf('ident'), NB.ap[:, nbi, :], NB.kc(nbi))]
                    else:
                        S.op('dve', lambda e, i=i: e.tensor_tensor(out=h4(NB.ap[:, nbi, :]), in0=bc(neg[i], [128, 4, 128], 1), in1=bc(t8(0)[:, hs], [128, 4, 128], 2), op=ALU.subtract),
                             reads=t8k, writes=NB.kc(nbi))
                        terms = [(b.cf('ones'), rgf, RG.k())] + ([(b.cf('ones'), lbf, LBI.k())] if i == 1 else []) + [(b.cf('ident'), NB.ap[:, nbi, :], NB.kc(nbi))]
                    bi = b.rr()
                    for j, (lt, rh, rk) in enumerate(terms):
                        S.op('pe', lambda e, lt=lt, rh=rh, j=j: e.matmul(b.bank(bi), lhsT=lt, rhs=rh, start=(j == 0), stop=(j == len(terms) - 1)),
                             reads=rk, writes=b.pk(bi), inc=(j == len(terms) - 1))
                    S.op('act', lambda e, i=i: e.activation(out=edst[i].ap, in_=b.bank(bi), func=AF.Exp), reads=[], writes=b.pk(bi) + edst[i].k())
                bkk = b.rr()
                for j in range(4):
                    kT = QK.ap[:, 8 + 4 * hf + j, tc_]
                    S.op('pe', lambda e, j=j, kT=kT: e.matmul(b.bank(bkk)[:, 128 * j:128 * (j + 1)], lhsT=kT, rhs=kT, start=True, stop=True),
                         reads=QK.kc(8 + 4 * hf + j), writes=b.pk(bkk), inc=(j == 3))
                S.op('dve', lambda e: e.tensor_tensor(out=A32.ap, in0=b.bank(bkk), in1=E1F.ap, op=ALU.mult), reads=E1F.k(), writes=b.pk(bkk) + A32.k())
                S.op('dve', lambda e: e.tensor_tensor(out=E2F.ap, in0=b.bank(bkk), in1=E2F.ap, op=ALU.mult), reads=E2F.k(), writes=b.pk(bkk) + E2F.k())
                bqk = b.rr()
                for j in range(4):
                    kT = QK.ap[:, 8 + 4 * hf + j, tc_]
                    qT = QK.ap[:, 4 * hf + j, tc_]
                    S.op('pe', lambda e, j=j, kT=kT, qT=qT: e.matmul(b.bank(bqk)[:, 128 * j:128 * (j + 1)], lhsT=kT, rhs=qT, start=True, stop=True),
                         reads=QK.kc(8 + 4 * hf + j) + QK.kc(4 * hf + j), writes=b.pk(bqk), inc=(j == 3))
                S.op('dve', lambda e: e.tensor_tensor(out=ATT.ap.rearrange("p h e -> p (h e)"), in0=b.bank(bqk), in1=E3B.ap, op=ALU.mult),
                     reads=E3B.k(), writes=b.pk(bqk) + ATT.k())
                aa4 = AA.ap.rearrange("p a (h e) -> p a h e", h=4)
                tp4 = TPB.ap.rearrange("p a (h e) -> p a h e", h=4)
                ws4 = WSB.ap.rearrange("p a (h e) -> p a h e", h=4)
                nlev = 3 if sample else 7
                hq = lambda q: slice(2 * q, 2 * q + 2)

                def kpair(V, base, q):
                    return V.k(base + 512 * q, base + 512 * q + 512) + V.k(base + 1024 + 512 * q, base + 1024 + 512 * q + 512)

                def mask_level(lev, q):
                    sl_ = lev % 2
                    out = AOF2.ap[:, sl_, :].rearrange("p (a h e) -> p a h e", a=2, h=4)[:, :, hq(q), :]
                    S.op('dve', lambda e: e.tensor_tensor(out=out, in0=aa4[:, :, hq(q), :], in1=bc(DCM.ap[:, 2 * lev:2 * lev + 2, :], [128, 2, 2, 128], 2), op=ALU.mult),
                         reads=AA.k() + DCM.k(), writes=kpair(AOF2, sl_ * 2048, q))
                for q in range(2):
                    for a_ in range(2):
                        S.op('dve', lambda e, a_=a_: e.scalar_tensor_tensor(out=tp4[:, a_, hq(q), :], in0=aa4[:, a_, hq(q), :], scalar=-1.0,
                                                                          in1=bc(DCM.ap[:, a_, :], [128, 2, 128], 1), op0=ALU.mult, op1=ALU.mult),
                             reads=AA.k() + DCM.k(), writes=kpair(TPB, 0, q))
                    S.op('dve', lambda e: e.tensor_tensor(out=tp4[:, :, hq(q), :], in0=tp4[:, :, hq(q), :],
                                                          in1=b.cb('ident').unsqueeze(1).unsqueeze(1).to_broadcast([128, 2, 2, 128]), op=ALU.add),
                         reads=kpair(TPB, 0, q), writes=kpair(TPB, 0, q))
                for q in range(2):
                    mask_level(1, q)
                for lev in range(1, nlev):
                    last = (lev == nlev - 1)
                    sl_ = lev % 2
                    aof = AOF2.ap[:, sl_, :]
                    bw = [b.rr(), b.rr()]
                    for q in range(2):
                        for jj in range(2):
                            j = 2 * q + jj
                            js = slice(128 * j, 128 * (j + 1))
                            if not last:
                                S.op('pe', lambda e, j=j, jj=jj, js=js: e.matmul(b.bank(bw[q])[:, 128 * jj:128 * (jj + 1)], lhsT=aof[:, 512 + 128 * j:512 + 128 * (j + 1)],
                                                                                 rhs=TPB.ap[:, 0, js], start=True, stop=True),
                                     reads=kpair(AOF2, sl_ * 2048, q) + kpair(TPB, 0, q), writes=b.pk(bw[q]), inc=False)
                            S.op('pe', lambda e, jj=jj, js=js: e.matmul(b.bank(bw[q])[:, 256 + 128 * jj:256 + 128 * (jj + 1)], lhsT=aof[:, js],
                                                                      rhs=TPB.ap[:, 1, js], start=True, stop=True),
                                 reads=kpair(AOF2, sl_ * 2048, q) + kpair(TPB, 0, q), writes=b.pk(bw[q]), inc=(jj == 1))
                    if not last:
                        for q in range(2):
                            mask_level(lev + 1, q)
                    for q in range(2):
                        if not last:
                            S.op('act', lambda e: e.activation(out=ws4[:, :, hq(q), :], in_=b.bank(bw[q]).rearrange("p (a h e) -> p a h e", a=2, h=2), func=AF.Copy),
                                 reads=[], writes=b.pk(bw[q]) + kpair(WSB, 0, q))
                        else:
                            S.op('act', lambda e: e.activation(out=ws4[:, 1, hq(q), :], in_=b.bank(bw[q])[:, 256:512].rearrange("p (h e) -> p h e", h=2), func=AF.Copy),
                                 reads=[], writes=b.pk(bw[q]) + kpair(WSB, 0, q))
                    bd = [b.rr(), b.rr()]
                    for q in range(2):
                        for jj in range(2):
                            j = 2 * q + jj
                            js = slice(128 * j, 128 * (j + 1))
                            if not last:
                                S.op('pe', lambda e, jj=jj, js=js: e.matmul(b.bank(bd[q])[:, 128 * jj:128 * (jj + 1)], lhsT=TPB.ap[:, 1, js], rhs=WSB.ap[:, 0, js], start=True, stop=True),
                                     reads=kpair(TPB, 0, q) + kpair(WSB, 0, q), writes=b.pk(bd[q]), inc=False)
                            S.op('pe', lambda e, jj=jj, js=js: e.matmul(b.bank(bd[q])[:, 256 + 128 * jj:256 + 128 * (jj + 1)], lhsT=TPB.ap[:, 0, js], rhs=WSB.ap[:, 1, js], start=True, stop=True),
                                 reads=kpair(TPB, 0, q) + kpair(WSB, 0, q), writes=b.pk(bd[q]), inc=(jj == 1))
                    for q in range(2):
                        if not last:
                            S.op('dve', lambda e: e.tensor_tensor(out=tp4[:, :, hq(q), :], in0=tp4[:, :, hq(q), :],
                                                                  in1=b.bank(bd[q]).rearrange("p (a h e) -> p a h e", a=2, h=2), op=ALU.subtract),
                                 reads=kpair(TPB, 0, q), writes=b.pk(bd[q]) + kpair(TPB, 0, q))
                        else:
                            S.op('dve', lambda e: e.tensor_tensor(out=tp4[:, 1, hq(q), :], in0=tp4[:, 1, hq(q), :],
                                                                  in1=b.bank(bd[q])[:, 256:512].rearrange("p (h e) -> p h e", h=2), op=ALU.subtract),
                                 reads=kpair(TPB, 0, q), writes=b.pk(bd[q]) + kpair(TPB, 0, q))
                heads = lambda q: (4 * hf + 2 * q, 4 * hf + 2 * q + 1)
                h2 = lambda ap2: ap2.rearrange("p (h e) -> p h e", h=2)
                tf0 = lambda q: TF.ap[:, 0, 256 * q:256 * q + 256]
                tf1 = lambda q: TF.ap[:, 1, 256 * q:256 * q + 256]
                tf0k = lambda q: TF.k(1024 * q, 1024 * q + 1024)
                tf1k = lambda q: TF.k(2048 + 1024 * q, 2048 + 1024 * q + 1024)
                xrk = lambda q: XRB.k(512 * q, 512 * q + 512)
                ubk = lambda q: UB.k(256 * (4 * hf + 2 * q), 256 * (4 * hf + 2 * q) + 512)
                o32k = lambda q: O32.k(512 * (4 * hf + 2 * q), 512 * (4 * hf + 2 * q) + 1024)
                tpk = lambda q: TPB.k(1024 + 512 * q, 1024 + 512 * q + 512)
                sbk = lambda q: [('SB', 2 * hf + q)]
                s3k = lambda q: [('S32', l, 2 * hf + q)]
                hsq = lambda q: slice(4 * hf + 2 * q, 4 * hf + 2 * q + 2)
                b1, b2_, b3 = [None, None], [None, None], [None, None]
                for q in range(2):
                    if not sample:
                        b1[q] = b.rr()
                        for jj, h in enumerate(heads(q)):
                            S.op('pe', lambda e, h=h, jj=jj: e.matmul(b.bank(b1[q])[:, 128 * jj:128 * (jj + 1)], lhsT=QK.ap[:, 8 + h, tc_], rhs=SBh[:, h, :], start=True, stop=True),
                                 reads=QK.kc(8 + h) + sbk(q), writes=b.pk(b1[q]), inc=False)
                        for jj, h in enumerate(heads(q)):
                            S.op('pe', lambda e, h=h, jj=jj: e.matmul(b.bank(b1[q])[:, 256 + 128 * jj:256 + 128 * (jj + 1)], lhsT=QK.ap[:, h, tc_], rhs=SBh[:, h, :], start=True, stop=True),
                                 reads=QK.kc(h) + sbk(q), writes=b.pk(b1[q]), inc=(jj == 1))
                for q in range(2):
                    if not sample:
                        ks_ap, qs_ap = b.bank(b1[q])[:, 0:256], b.bank(b1[q])[:, 256:512]
                        ksk = qsk = b.pk(b1[q])
                    else:
                        ks_ap = b.PS[:, 2048 + 512 * hf + 256 * q: 2048 + 512 * hf + 256 * (q + 1)]
                        qs_ap = b.PS[:, 3072 + 512 * hf + 256 * q: 3072 + 512 * hf + 256 * (q + 1)]
                        ksk, qsk = b.pk(4 + hf), b.pk(6 + hf)
                    S.op('dve', lambda e: e.tensor_tensor(out=h2(tf0(q)), in0=h2(ks_ap), in1=bc(t8(3)[:, hsq(q)], [128, 2, 128], 2), op=ALU.mult),
                         reads=t8k, writes=ksk + tf0k(q))
                    S.op('dve', lambda e: e.tensor_tensor(out=XRB.ap[:, 2 * q:2 * q + 2, :], in0=VB.ap[:, hsq(q), :], in1=h2(tf0(q)), op=ALU.subtract),
                         reads=VB.k() + tf0k(q), writes=xrk(q))
                    S.op('dve', lambda e: e.tensor_tensor(out=h2(tf1(q)), in0=h2(qs_ap), in1=bc(t8(2)[:, hsq(q)], [128, 2, 128], 2), op=ALU.mult),
                         reads=t8k, writes=qsk + tf1k(q))
                for q in range(2):
                    b2_[q] = b.rr()
                    for jj in range(2):
                        j = 2 * q + jj
                        S.op('pe', lambda e, j=j, jj=jj: e.matmul(b.bank(b2_[q])[:, 128 * jj:128 * (jj + 1)], lhsT=TPB.ap[:, 1, 128 * j:128 * (j + 1)], rhs=XRB.ap[:, j, :], start=True, stop=True),
                             reads=tpk(q) + xrk(q), writes=b.pk(b2_[q]), inc=(jj == 1))
                for q in range(2):
                    S.op('act', lambda e: e.activation(out=UB.ap[:, hsq(q), :], in_=h2(b.bank(b2_[q])[:, 0:256]), func=AF.Copy), reads=[], writes=b.pk(b2_[q]) + ubk(q))
                for q in range(2):
                    for jj, h in enumerate(heads(q)):
                        j = 2 * q + jj
                        S.op('pe', lambda e, h=h, j=j, jj=jj: e.matmul(b.bank(b2_[q])[:, 256 + 128 * jj:256 + 128 * (jj + 1)], lhsT=ATT.ap[:, j, :], rhs=UB.ap[:, h, :], start=True, stop=True),
                             reads=ATT.k() + ubk(q), writes=b.pk(b2_[q]), inc=(jj == 1))
                    if not sample:
                        b3[q] = b.rr()
                        for jj, h in enumerate(heads(q)):
                            S.op('pe', lambda e, h=h, jj=jj: e.matmul(b.bank(b3[q])[:, 128 * jj:128 * (jj + 1)], lhsT=KDK.ap[:, h, :], rhs=UB.ap[:, h, :], start=True, stop=True),
                                 reads=KDK.k() + ubk(q), writes=b.pk(b3[q]), inc=(jj == 1))
                for q in range(2):
                    S.op('dve', lambda e: e.tensor_tensor(out=O32.ap[:, hsq(q), :], in0=h2(b.bank(b2_[q])[:, 256:512]), in1=h2(tf1(q)), op=ALU.add),
                         reads=tf1k(q), writes=b.pk(b2_[q]) + o32k(q))
                    if not sample:
                        s32 = b.S32[:, l, hsq(q), :]
                        S.op('dve', lambda e: e.tensor_tensor(out=s32, in0=s32, in1=bc(t8(6)[:, hsq(q)], [128, 2, 128], 2), op=ALU.mult),
                             reads=s3k(q) + t8k, writes=s3k(q))
                        S.op('dve', lambda e: e.tensor_tensor(out=s32, in0=h2(b.bank(b3[q])[:, 0:256]), in1=s32, op=ALU.add),
                             reads=s3k(q), writes=b.pk(b3[q]) + s3k(q))
                for q in range(2):
                    if not sample:
                        s32 = b.S32[:, l, hsq(q), :]
                        S.op('act', lambda e: e.activation(out=SBh[:, hsq(q), :], in_=s32, func=AF.Copy), reads=s3k(q), writes=sbk(q))
            if sample:
                for s_ in range(NSEQ_S):
                    sl = s_ % 2
                    S.dma('sp', SL32.ap[:, sl, :].rearrange("p (h e) -> p h e", h=8),
                          b.dram['sdelta'][l, s_].rearrange("h d e -> d h e"), writes=SL32.kc(sl))
                    S.op('dve', lambda e: e.tensor_tensor(out=UM.ap, in0=UB.ap, in1=b.cf('seqsel', s_, s_ + 1).unsqueeze(2).to_broadcast([128, 8, 128]), op=ALU.mult),
                         reads=UB.k(), writes=UM.k())
                    for h in range(8):
                        S.op('pe', lambda e, h=h: e.matmul(b.PS[:, 2048 + 128 * h:2048 + 128 * (h + 1)], lhsT=KDK.ap[:, h, :], rhs=UM.ap[:, h, :], start=True, stop=True),
                             reads=KDK.k() + UM.k(), writes=b.pk(4, 2), inc=(h == 7))
                    sv = SL32.ap[:, sl, :].rearrange("p (h e) -> p h e", h=8)
                    S.op('dve', lambda e: e.tensor_tensor(out=sv, in0=sv, in1=bc(GES.ap[:, s_, :], [128, 8, 128], 2), op=ALU.mult),
                         reads=SL32.kc(sl) + GES.k(), writes=SL32.kc(sl))
                    S.op('dve', lambda e: e.tensor_tensor(out=SL32.ap[:, sl, :], in0=b.PS[:, 2048:3072], in1=SL32.ap[:, sl, :], op=ALU.add),
                         reads=SL32.kc(sl), writes=b.pk(4, 2) + SL32.kc(sl))
                    S.dma('sp', b.dram['deltas'][l, s_].rearrange("h d e -> d h e"), sv, reads=SL32.kc(sl))
            S.op('act', lambda e: e.activation(out=TF.ap.rearrange("p a c -> p (a c)"), in_=O32.ap.rearrange("p h e -> p (h e)"), func=AF.Square),
                 reads=O32.k(), writes=TF.k())
            S.op('dve', lambda e: e.tensor_reduce(out=t8(7), in_=TF.ap.rearrange("p a (h e) -> p (a h) e", e=128), axis=AX.X, op=ALU.add),
                 reads=TF.k(), writes=t8k)
            S.op('act', lambda e: e.activation(out=t8(8), in_=t8(7), func=AF.Ln, bias=b.epsc[:, 0:1], scale=1.0 / 128), reads=t8k, writes=t8k)
            S.op('act', lambda e: e.activation(out=t8(8), in_=t8(8), func=AF.Exp, scale=-0.5), reads=t8k, writes=t8k)
            S.op('dve', lambda e: e.tensor_tensor(out=ON.ap, in0=O32.ap, in1=bc(t8(8), [128, 8, 128], 2), op=ALU.mult),
                 reads=O32.k() + t8k, writes=ON.k())
            bi = b.rr()
            for h in range(8):
                S.op('pe', lambda e, h=h: e.transpose(out=b.bankb(bi)[:, 128 * h:128 * (h + 1)], in_=ON.ap[:, h, :], identity=identb),
                     reads=ON.k(), writes=b.pk(bi), inc=(h == 7))
            S.op('act', lambda e: e.activation(out=OT.ap[:, :, tc_], in_=b.bankb(bi).rearrange("p (h e) -> p h e", h=8), func=AF.Copy,
                                               scale=b.OGAIN[:, l:l + 1]),
                 reads=[], writes=b.pk(bi) + OT.k())
        if (not sample) and g == 3:
            S.dma('sp', b.dram['deltap'][l].rearrange("h d e -> d h e"), b.S32[:, l], reads=[('S32', l, i) for i in range(4)])

        b.aoff = mark
        GT = b.aalloc([128, 2, n], BF16)
        it = 0
        for blk in range(4):
            w, wk = b.wnext(('a_w_in', l, 0, D, 3072 + 256 * blk, 256))
            for j in range(2):
                h = 2 * blk + j
                bi = b.rr()
                b.fm_matmul(b.bank(bi)[:, 0:n], w, wk, 128 * j, 128, lambda kc: b.HT[:, kc, 0:n], b.HTK, 8, b.pk(bi))
                tb = it % 2
                it += 1
                S.op('act', lambda e: e.activation(out=GT.ap[:, tb, 0:n], in_=b.bank(bi)[:, 0:n], func=AF.Silu), reads=[], writes=b.pk(bi) + GT.kc(tb))
                S.op('dve', lambda e: e.tensor_tensor(out=OT.ap[:, h, 0:n], in0=OT.ap[:, h, 0:n], in1=GT.ap[:, tb, 0:n], op=ALU.mult),
                     reads=GT.kc(tb) + OT.k(), writes=OT.k())
        for blk in range(4):
            w, wk = b.wnext(('a_w_out', l, 0, D, 256 * blk, 256))
            for j in range(2):
                dc = 2 * blk + j
                bi = b.rr()
                b.fm_matmul(b.bank(bi)[:, 0:n], w, wk, 128 * j, 128, lambda kc: OT.ap[:, kc, 0:n], OT.k(), 8, b.pk(bi))
                S.op('act', lambda e: e.activation(out=b.YT[:, dc, 0:n], in_=b.bank(bi)[:, 0:n], func=AF.Copy), reads=[], writes=b.pk(bi) + ['YT'])
        b.abegin()
        b.postnorm_add(b.ncol(l, 1), ntok)

    def _conv_out(b, src, srck, rows, dst):
        S = b.S
        CVO = b.aalloc([rows, QKV], F32)
        for r in range(6):
            bi = b.rr()
            for c4 in range(4):
                c = 4 * r + c4
                S.op('pe', lambda e, c=c, c4=c4: e.transpose(out=b.bank(bi)[0:rows, 128 * c4:128 * (c4 + 1)], in_=src[:, c, :], identity=b.cf('ident')),
                     reads=srck, writes=b.pk(bi), inc=(c4 == 3))
            S.op('act', lambda e: e.activation(out=CVO.ap[:, 512 * r:512 * (r + 1)], in_=b.bank(bi)[0:rows, :], func=AF.Copy),
                 reads=[], writes=b.pk(bi) + CVO.k())
        S.dma('sp', dst, CVO.ap, reads=CVO.k())

    def kvproj(b, g, ntok):
        S = b.S
        b.nrr = 6
        b.psrr = 0
        sample = (g == 4)
        n = ntok
        b.abegin()
        HK = b.aalloc([128, 8, n], BF16)
        hk = HK.k()
        b.rmsnorm_fm(b.XT, 'XT', b.KVNC, HK.ap, (lambda c: HK.kc(c)), ntok)
        KVO = b.aalloc([128, 2, 512], F32)
        VST = b.aalloc([32, 16, 256], BF16)
        wa, ka = b.wnext(('b_w_kv', None, 0, D, 0, 256))
        wb, kb = b.wnext(('b_w_kv', None, 0, D, 256, 256))
        ktkey = 'KTS' if sample else 'KT2'
        for kv in range(4):
            bi = b.rr()
            for half in range(2):
                for kc in range(8):
                    S.op('pe', lambda e, kc=kc, half=half: e.matmul(b.bank(bi)[64 * half:64 * half + 64, 0:n], lhsT=wa[:, kc, 64 * kv:64 * kv + 64],
                                                                     rhs=HK.ap[:, kc, :], start=(kc == 0), stop=(kc == 7)),
                         reads=hk + [ka], writes=b.pk(bi), inc=(kc == 7 and half == 1))
            dst = b.KTS[:, kv, :] if sample else b.KT2[:, kv, 512 * g:512 * g + n]
            S.op('act', lambda e: e.activation(out=dst, in_=b.bank(bi)[:, 0:n], func=AF.Copy), reads=[], writes=b.pk(bi) + [ktkey])
        for t in range(n // 128):
            it = t % 2
            bk, bv = b.rr(), b.rr()
            for (bb_, w_, k_) in ((bk, wa, ka), (bv, wb, kb)):
                for kc in range(8):
                    S.op('pe', lambda e, kc=kc, bb_=bb_, w_=w_: e.matmul(b.bank(bb_)[:, 0:256], lhsT=HK.ap[:, kc, 128 * t:128 * (t + 1)], rhs=w_[:, kc, :],
                                                                         start=(kc == 0), stop=(kc == 7)),
                         reads=hk + [k_], writes=b.pk(bb_), inc=(kc == 7))
            S.op('act', lambda e: e.activation(out=KVO.ap[:, it, 0:256], in_=b.bank(bk)[:, 0:256], func=AF.Copy), reads=[], writes=b.pk(bk) + KVO.kc(it))
            S.op('act', lambda e: e.activation(out=KVO.ap[:, it, 256:512], in_=b.bank(bv)[:, 0:256], func=AF.Copy), reads=[], writes=b.pk(bv) + KVO.kc(it))
            if not sample:
                S.op('act', lambda e: e.activation(out=b.VD[:, 0, 4 * g + t, :], in_=KVO.ap[:, it, 256:512], func=AF.Copy), reads=KVO.kc(it), writes=['VD'])
                r0 = 512 * g + 128 * t
                S.dma('sp', b.dram['ckp'][r0:r0 + 128, :], KVO.ap[:, it, 0:256], reads=KVO.kc(it))
                S.dma('sp', b.dram['cvp'][r0:r0 + 128, :], KVO.ap[:, it, 256:512], reads=KVO.kc(it))
            else:
                S.op('act', lambda e: e.activation(out=b.VS[:, :], in_=KVO.ap[:, it, 256:512], func=AF.Copy), reads=KVO.kc(it), writes=['VS'])
                S.dma('sp', b.dram['cks'], KVO.ap[:, it, 0:256], reads=KVO.kc(it))
                S.dma('sp', b.dram['cvs'], KVO.ap[:, it, 256:512], reads=KVO.kc(it))
        if sample:
            return
        for r in range(4):
            bi = b.rr()
            for kc in range(8):
                S.op('pe', lambda e, kc=kc: e.matmul(b.bank(bi)[:, 0:256], lhsT=HK.ap[:, kc, r:512:4], rhs=wb[:, kc, :], start=(kc == 0), stop=(kc == 7)),
                     reads=hk + [kb], writes=b.pk(bi), inc=(kc == 7))
            S.op('dve', lambda e: e.tensor_copy(out=b.VD[:, 1, 4 * r + g, :], in_=b.bank(bi)[:, 0:256]), reads=[], writes=b.pk(bi) + ['VD'])
        for r2 in range(8):
            bi = b.rr()
            for rr_ in range(2):
                r = 2 * r2 + rr_
                for kc in range(8):
                    S.op('pe', lambda e, kc=kc, r=r, rr_=rr_: e.matmul(b.bank(bi)[0:32, 256 * rr_:256 * rr_ + 256], lhsT=HK.ap[:, kc, r:512:16], rhs=wb[:, kc, :],
                                                                     start=(kc == 0), stop=(kc == 7)),
                         reads=hk + [kb], writes=b.pk(bi), inc=(kc == 7 and rr_ == 1))
            S.op('dve', lambda e: e.tensor_copy(out=VST.ap[:, 2 * r2:2 * r2 + 2, :], in_=b.bank(bi)[0:32, :].rearrange("p (r c) -> p r c", r=2)),
                 reads=[], writes=b.pk(bi) + VST.k())
        S.dma('sp', b.VD[32 * g:32 * g + 32, 2, :, :], VST.ap, reads=VST.k(), writes=['VD'])

    def attn(b, l, g, ntok):
        if g == 4:
            return b.attn_sample(l)
        S = b.S
        n = 512
        j = l - 2
        sl_ = alibi_slopes()
        dil = (1, 4, 16)
        b.abegin()
        b.rmsnorm_fm(b.XT, 'XT', b.ncol(l, 0), b.HT, 'HT', ntok)
        OT = b.aalloc([128, 8, n], BF16)
        QT = b.aalloc([128, 6, n], BF16)
        ET = b.aalloc([128, 6, 256], BF16)
        EX = b.aalloc([128, 2, 512], BF16)
        PT = b.aalloc([128, 2, 512], BF16)
        RD = b.aalloc([128, 512], F32)
        NUM, DEN = 6, 7
        b.nrr = 6
        b.psrr = 0
        it = 0
        for pr in range(8):
            kvh = pr // 2
            qb = pr % 2
            for grp in range(3):
                w, wk = b.wnext(('b_w_q', j, 0, D, 1024 * grp + 128 * pr, 128))
                bi = b.rr()
                b.fm_matmul(b.bank(bi)[:, 0:n], w, wk, 0, 128, lambda kc: b.HT[:, kc, 0:n], b.HTK, 8, b.pk(bi))
                S.op('act', lambda e: e.activation(out=QT.ap[:, qb * 3 + grp, :], in_=b.bank(bi)[:, 0:n], func=AF.Copy, scale=0.125),
                     reads=[], writes=b.pk(bi) + QT.kc(qb * 3 + grp))
            for hh in range(2):
                for grp in range(3):
                    c = float(sl_[grp, 2 * pr + hh]) * dil[grp]
                    ei = hh * 3 + grp
                    S.op('act', lambda e: e.activation(out=ET.ap[:, ei, :], in_=b.cf('deltac'), func=AF.Exp, scale=-c), reads=[], writes=ET.kc(ei))
                    S.op('dve', lambda e: e.tensor_tensor(out=ET.ap[:, ei, :], in0=ET.ap[:, ei, :], in1=b.cb('maskj'), op=ALU.mult),
                         reads=ET.kc(ei), writes=ET.kc(ei))
            combos = [(0, 1), (0, 0), (1, 1), (1, 0), (2, 1)]
            if g == 0:
                combos = [(0, 1), (0, 0), (1, 1), (2, 1)]
            items = [(hh, grp, own) for hh in range(2) for (grp, own) in combos]
            first = {0: True, 1: True}
            firstd = {0: True, 1: True}

            def stA(i):
                hh, grp, own = items[i]
                P = slice(64 * hh, 64 * hh + 64)
                nsub, qn = (4, 128) if grp < 2 else (16, 32)
                bi = b.rr()
                qi = qb * 3 + grp
                for sub in range(nsub):
                    if grp == 0:
                        qcols = slice(128 * sub, 128 * sub + 128)
                        k0 = max(512 * g + 128 * sub - (0 if own else 128), 0)
                        kcols = slice(k0, k0 + 128)
                    elif grp == 1:
                        qcols = slice(sub, 512, 4)
                        gg = g if own else g - 1
                        kcols = slice(512 * gg + sub, 512 * gg + 512, 4)
                    else:
                        qcols = slice(sub, 512, 16)
                        kcols = slice(sub, SEQ, 16)
                    S.op('pe', lambda e, sub=sub, qcols=qcols, kcols=kcols: e.matmul(b.bank(bi)[:, qn * sub:qn * (sub + 1)], lhsT=b.KT2[P, kvh, kcols],
                                                                                       rhs=QT.ap[P, qi, qcols], start=True, stop=True),
                         reads=['KT2'] + QT.kc(qi), writes=b.pk(bi), inc=(sub == nsub - 1))
                xi = i % 2
                S.op('act', lambda e: e.activation(out=EX.ap[:, xi, :], in_=b.bank(bi), func=AF.Exp), reads=[], writes=b.pk(bi) + EX.kc(xi))
                ei = hh * 3 + grp
                if grp < 2:
                    tab = bc(ET.ap[:, ei, (0 if own else 128):(128 if own else 256)], [128, 4, 128], 1)
                else:
                    tab = bc(ET.ap[:, ei, 32 * g:32 * g + 32], [128, 16, 32], 1)
                S.op('dve', lambda e: e.tensor_tensor(out=PT.ap[:, xi, :].rearrange("p (s q) -> p s q", s=nsub),
                                                      in0=EX.ap[:, xi, :].rearrange("p (s q) -> p s q", s=nsub), in1=tab, op=ALU.mult),
                     reads=EX.kc(xi) + ET.kc(ei), writes=PT.kc(xi))
                if g == 0 and grp == 0 and not own:
                    S.op('pool', lambda e: e.memset(PT.ap[:, xi, 0:128], 0.0), reads=[], writes=PT.kc(xi))

            def stB(i):
                hh, grp, own = items[i]
                P = slice(64 * hh, 64 * hh + 64)
                nsub, qn = (4, 128) if grp < 2 else (16, 32)
                xi = i % 2
                for sub in range(nsub):
                    if grp == 0:
                        vt = max(4 * g + sub - (0 if own else 1), 0)
                        ocols = slice(128 * sub, 128 * sub + 128)
                    elif grp == 1:
                        vt = 4 * sub + (g if own else g - 1)
                        ocols = slice(sub, 512, 4)
                    else:
                        vt = sub
                        ocols = slice(sub, 512, 16)
                    st = first[hh]
                    first[hh] = False
                    rhs = PT.ap[:, xi, qn * sub:qn * (sub + 1)]
                    S.op('pe', lambda e, vt=vt, ocols=ocols, st=st, rhs=rhs: e.matmul(b.bank(NUM)[P, ocols], lhsT=b.VD[:, grp, vt, 64 * kvh:64 * kvh + 64], rhs=rhs,
                                                                                       start=st, stop=False, skip_group_check=True),
                         reads=['VD'] + PT.kc(xi), writes=b.pk(NUM), inc=False)
                std = firstd[hh]
                firstd[hh] = False
                if grp == 0:
                    oap = b.bank(DEN)[P, 0:512].rearrange("p (s q) -> p s q", s=4)
                else:
                    oap = b.bank(DEN)[P, 0:512].rearrange("p (i r) -> p r i", r=nsub)
                S.op('pe', lambda e: e.matmul(oap, lhsT=b.cb('ones', 0, 64), rhs=PT.ap[:, xi, :].rearrange("p (s q) -> p s q", s=nsub),
                                              start=std, stop=False, skip_group_check=True),
                     reads=PT.kc(xi), writes=b.pk(DEN), inc=True)

            for i in range(len(items) + 1):
                if i < len(items):
                    stA(i)
                if i >= 1:
                    stB(i - 1)
            S.op('dve', lambda e: e.reciprocal(out=RD.ap, in_=b.bank(DEN)), reads=[], writes=b.pk(DEN) + RD.k())
            S.op('dve', lambda e: e.tensor_tensor(out=OT.ap[:, pr, :], in0=b.bank(NUM), in1=RD.ap, op=ALU.mult), reads=RD.k(), writes=b.pk(NUM) + OT.kc(pr))
        b._wo(j, OT, n)
        b.abegin()
        b.postnorm_add(b.ncol(l, 1), ntok)

    def _wo(b, j, OT, n):
        S = b.S
        for blk in range(4):
            w, wk = b.wnext(('b_w_o', j, 0, D, 256 * blk, 256))
            for jj in range(2):
                dc = 2 * blk + jj
                bi = b.rr()
                b.fm_matmul(b.bank(bi)[:, 0:n], w, wk, 128 * jj, 128, lambda kc: OT.ap[:, kc, 0:n], OT.k(), 8, b.pk(bi))
                S.op('act', lambda e: e.activation(out=b.YT[:, dc, 0:n], in_=b.bank(bi)[:, 0:n], func=AF.Copy), reads=[], writes=b.pk(bi) + ['YT'])

    def attn_sample(b, l):
        S = b.S
        j = l - 2
        n = 128
        b.abegin()
        b.rmsnorm_fm(b.XT, 'XT', b.ncol(l, 0), b.HT, 'HT', n)
        QS = b.aalloc([64, 48, n], BF16)
        OT = b.aalloc([128, 8, n], BF16)
        ES = b.aalloc([128, NKT, 384], BF16)
        KST = b.aalloc([128, 2, 12 * 256], BF16)
        VBF = b.aalloc([128, 2, 13 * 256], BF16)
        KTT = b.aalloc([64, 2, 512], BF16)
        KTL = b.aalloc([64, 4, 128], BF16)
        EX = b.aalloc([128, 2, 384], BF16)
        PT = b.aalloc([128, 2, 384], BF16)
        PJ = b.aalloc([128, 2, 128], BF16)
        O2 = b.aalloc([128, 2, 64], BF16)
        RD = b.aalloc([128, 1], F32)
        S.dma('sp', ES.ap.rearrange("p t c -> p (t c)"), b.dram['esamp'], writes=ES.k())
        S.op('pool', lambda e: e.memset(VBF.ap.rearrange("p a c -> p (a c)"), 0.0), writes=VBF.k())
        S.op('pool', lambda e: e.memset(KTL.ap.rearrange("p a c -> p (a c)"), 0.0), writes=KTL.k())
        for pr in range(8):
            for grp in range(3):
                w, wk = b.wnext(('b_w_q', j, 0, D, 1024 * grp + 128 * pr, 128))
                for hh in range(2):
                    bi = b.rr()
                    for kc in range(8):
                        S.op('pe', lambda e, kc=kc: e.matmul(b.bank(bi)[0:64, 0:n], lhsT=w[:, kc, 64 * hh:64 * hh + 64], rhs=b.HT[:, kc, 0:n],
                                                             start=(kc == 0), stop=(kc == 7)),
                             reads=b.HTK + [wk], writes=b.pk(bi), inc=(kc == 7))
                    qi = grp * 16 + 2 * pr + hh
                    S.op('act', lambda e: e.activation(out=QS.ap[:, qi, :], in_=b.bank(bi)[0:64, 0:n], func=AF.Copy, scale=0.125),
                         reads=[], writes=b.pk(bi) + QS.kc(qi))
        ACC = 7
        b.nrr = 6
        b.psrr = 0
        it = 0
        qv = QS.ap.rearrange("p (g c) t -> p g c t", g=3)
        for s_ in range(NSEQ_S):
            sl = s_ % 2
            for (nm, dst) in (('ck', KST), ('cv', VBF)):
                src = b.dram[nm][s_]
                cs = bass.AP(src.tensor, src.offset, [[4096, 96], [256, 8], [1, 256]])
                S.dma('pool', dst.ap[0:96, sl, 0:NCT * 256].rearrange("p (t c) -> p t c", t=NCT), cs, writes=dst.kc(sl))
                S.dma('pool', dst.ap[:, sl, NCT * 256:12 * 256].rearrange("p (t c) -> p t c", t=4), src[1536:2048, :].rearrange("(t p) c -> p t c", p=128), writes=dst.kc(sl))
            S.dma('sp', VBF.ap[0:8, sl, 12 * 256:13 * 256], b.VS[8 * s_:8 * s_ + 8, :], reads=['VS'], writes=VBF.kc(sl))
            S.op('act', lambda e: e.activation(out=KTL.ap[:, :, 0:8], in_=b.KTS[0:64, :, 8 * s_:8 * s_ + 8], func=AF.Copy), reads=['KTS'], writes=KTL.k())
            def stA(t):
                if t >= NKT - 1:
                    return
                x2 = t % 2
                npk = 96 if t < NCT else 128
                bt = b.rr()
                for kv in range(4):
                    S.op('pe', lambda e, kv=kv: e.transpose(out=b.bankb(bt)[0:64, 128 * kv:128 * kv + npk],
                                                            in_=KST.ap[0:npk, sl, 256 * t + 64 * kv:256 * t + 64 * kv + 64], identity=b.cb('ident')[0:npk, 0:npk]),
                         reads=KST.kc(sl), writes=b.pk(bt), inc=(kv == 3))
                S.op('act', lambda e: e.activation(out=KTT.ap[:, x2, :], in_=b.bankb(bt)[0:64, 0:512], func=AF.Copy), reads=[], writes=b.pk(bt) + KTT.kc(x2))

            def stB(t):
                x2 = t % 2
                npk = 96 if t < NCT else 128
                if t < NKT - 1:
                    ktap = lambda kv: KTT.ap[:, x2, 128 * kv:128 * kv + npk]
                    ktk = KTT.kc(x2)
                else:
                    ktap = lambda kv: KTL.ap[:, kv, :]
                    ktk = KTL.k()
                bi = b.rr()
                for kv in range(4):
                    S.op('pe', lambda e, kv=kv: e.matmul(b.bank(bi)[0:npk, 96 * kv:96 * kv + 96].rearrange("p (g r l) -> p g r l", g=3, r=4),
                                                       lhsT=ktap(kv), rhs=qv[:, :, 4 * kv:4 * kv + 4, 8 * s_:8 * s_ + 8], start=True, stop=True),
                         reads=ktk + QS.k(), writes=b.pk(bi), inc=(kv == 3))
                xi = t % 2
                S.op('act', lambda e: e.activation(out=EX.ap[0:npk, xi, :], in_=b.bank(bi)[0:npk, 0:384], func=AF.Exp), reads=[], writes=b.pk(bi) + EX.kc(xi))
                S.op('dve', lambda e: e.tensor_tensor(out=PT.ap[0:npk, xi, :], in0=EX.ap[0:npk, xi, :], in1=ES.ap[0:npk, t, :], op=ALU.mult),
                     reads=EX.kc(xi) + ES.k(), writes=PT.kc(xi))
                p4 = PT.ap[0:npk, xi, :].rearrange("p (k g c) -> p k g c", k=4, g=3)
                pj = PJ.ap[0:npk, xi, :].rearrange("p (k c) -> p k c", k=4)
                S.op('dve', lambda e: e.tensor_tensor(out=pj, in0=p4[:, :, 0, :], in1=p4[:, :, 1, :], op=ALU.add), reads=PT.kc(xi), writes=PJ.kc(xi))
                S.op('dve', lambda e: e.tensor_tensor(out=pj, in0=pj, in1=p4[:, :, 2, :], op=ALU.add), reads=PT.kc(xi) + PJ.kc(xi), writes=PJ.kc(xi))

            def stC(t):
                xi = t % 2
                npk = 96 if t < NCT else 128
                for kv in range(4):
                    lw = PJ.ap[0:npk, xi, 32 * kv:32 * kv + 32]
                    S.op('pe', lambda e, kv=kv, lw=lw: e.matmul(b.bank(ACC)[32 * kv:32 * kv + 32, 0:64], lhsT=lw, rhs=VBF.ap[0:npk, sl, 256 * t + 64 * kv:256 * t + 64 * kv + 64],
                                                              start=(t == 0), stop=False, skip_group_check=True, tile_position=(0, 32 * kv)),
                         reads=PJ.kc(xi) + VBF.kc(sl), writes=b.pk(ACC), inc=False)
                S.op('pe', lambda e: e.matmul(b.bank(ACC)[:, 64:65], lhsT=PJ.ap[0:npk, xi, :], rhs=b.cb('ones', 0, 1)[0:npk, :],
                                              start=False, stop=False, skip_group_check=True),
                     reads=PJ.kc(xi), writes=b.pk(ACC), inc=True)

            for step in range(NKT + 2):
                if step < NKT:
                    stA(step)
                if 0 <= step - 1 < NKT:
                    stB(step - 1)
                if 0 <= step - 2 < NKT:
                    stC(step - 2)
            S.op('dve', lambda e: e.reciprocal(out=RD.ap, in_=b.bank(ACC)[:, 64:65]), reads=[], writes=b.pk(ACC) + RD.k())
            S.op('dve', lambda e: e.tensor_scalar(out=O2.ap, in0=bc(b.bank(ACC)[:, 0:64], [128, 2, 64], 1), scalar1=RD.ap[:, 0:1], scalar2=None, op0=ALU.mult),
                 reads=RD.k(), writes=b.pk(ACC) + O2.k())
            bi = b.rr()
            S.op('pe', lambda e: e.transpose(out=b.bankb(bi)[:, 0:128], in_=O2.ap.rearrange("p a c -> p (a c)"), identity=b.cb('ident')),
                 reads=O2.k(), writes=b.pk(bi))
            for hh in range(2):
                P = slice(64 * hh, 64 * hh + 64)
                src = b.bankb(bi)[P, 0:128].rearrange("p (k r h l) -> p k r h l", k=4, h=2, r=2)[:, :, :, hh, :]
                dst = OT.ap[P, :, 8 * s_:8 * s_ + 8].rearrange("p (k r) l -> p k r l", k=4)
                S.op('act', lambda e, src=src, dst=dst: e.activation(out=dst, in_=src, func=AF.Copy), reads=[], writes=b.pk(bi) + OT.k())
        b._wo(j, OT, n)
        b.abegin()
        b.postnorm_add(b.ncol(l, 1), n)

    def load_group(b, g, ntok):
        S = b.S
        b.abegin()
        XIN = b.aalloc([128, 2, D], F32)
        src = b.dram['xp'] if g < 4 else b.dram['xs']
        for t in range(ntok // 128):
            r0 = (512 * g + 128 * t) if g < 4 else 0
            xi = t % 2
            S.dma('sp', XIN.ap[:, xi, :], src[r0:r0 + 128, :], writes=XIN.kc(xi))
            bi = b.rr(2)
            for c in range(8):
                S.op('pe', lambda e, c=c: e.transpose(out=b.bank(bi, 2)[:, 128 * c:128 * (c + 1)],
                                                      in_=XIN.ap[:, xi, 128 * c:128 * (c + 1)], identity=b.cf('ident')),
                     reads=XIN.kc(xi), writes=b.pk(bi, 2), inc=(c == 7))
            S.op('act', lambda e: e.activation(out=b.XT[:, :, 128 * t:128 * (t + 1)],
                                               in_=b.bank(bi, 2).rearrange("p (c n) -> p c n", c=8), func=AF.Copy),
                 reads=[], writes=b.pk(bi, 2) + ['XT'])

    def store_group(b, g, ntok):
        S = b.S
        b.abegin()
        XIN = b.aalloc([128, 2, D], F32)
        dst = b.dram['yp'] if g < 4 else b.dram['ys']
        for t in range(ntok // 128):
            r0 = (512 * g + 128 * t) if g < 4 else 0
            xi = t % 2
            bi = b.rr(2)
            for c in range(8):
                S.op('pe', lambda e, c=c: e.transpose(out=b.bank(bi, 2)[:, 128 * c:128 * (c + 1)],
                                                      in_=b.XT[:, c, 128 * t:128 * (t + 1)], identity=b.cf('ident')),
                     reads=['XT'], writes=b.pk(bi, 2), inc=(c == 7))
            S.op('act', lambda e: e.activation(out=XIN.ap[:, xi, :], in_=b.bank(bi, 2), func=AF.Copy),
                 reads=[], writes=b.pk(bi, 2) + XIN.kc(xi))
            S.dma('sp', dst[r0:r0 + 128, :], XIN.ap[:, xi, :], reads=XIN.kc(xi))

    def build(b):
        nc, S = b.nc, b.S
        for nm, shp in IN_SPECS:
            b.din(nm, shp, BF16 if nm in ('cb', 'esamp', 'dcm') else F32)
        for nm, shp in OUT_SPECS:
            b.dout(nm, shp)
        b.PS = b.es.enter_context(nc.psum_tensor("PS", [128, 4096], F32))
        if not b.dry:
            b.nblk = len(b.wplan) // 5
            b.WSCR = nc.dram_tensor("wscr", [b.nblk, 128, WSLOT], BF16, kind="Internal").ap()
        b.CFT = b.sb('CFT', [128, CF_W], F32)
        b.CBT = b.sb('CBT', [128, CB_W], BF16)
        b.NORMC = b.sb('NORMC', [128, 128], F32)
        b.KVNC = b.sb('KVNC', [128, 8], F32)
        b.CC = b.sb('CC', [128, 4], F32)
        b.AEXPN = b.sb('AEXPN', [128, 16], F32)
        b.DTB = b.sb('DTB', [128, 16], F32)
        b.OGAIN = b.sb('OGAIN', [128, 2], F32)
        b.CONVW = b.sb('CONVW', [128, 2 * 24 * 4], F32)
        b.WB = b.sb('WB', [128, NWS * WSLOT], BF16)
        b.XT = b.sb('XT', [128, 8, 512], F32)
        b.YT = b.sb('YT', [128, 8, 512], F32)
        b.HT = b.sb('HT', [128, 8, 512], BF16)
        b.S32 = b.sb('S32', [128, 2, 8, 128], F32)
        b.SB = b.sb('SB', [128, 8, 128], BF16)
        b.HIST = b.sb('HIST', [128, 2, 24, 3], F32)
        b.KT2 = b.sb('KT2', [128, 4, SEQ], BF16)
        b.VD = b.sb('VD', [128, 3, 16, 256], BF16)
        b.KTS = b.sb('KTS', [128, 4, 128], BF16)
        b.VS = b.sb('VS', [128, 256], BF16)
        b.AR = b.sb('AR', [128, ARBYTES // 2], BF16)
        b.epsc, b.onec, b.zeroc, b.lnqs = b.CC[:, 0:1], b.CC[:, 1:2], b.CC[:, 2:3], b.CC[:, 3:4]
        S.dma('sp', b.CFT[:], b.dram['cf'], writes=['c0'])
        S.dma('sp', b.CBT[:], b.dram['cb'], writes=['c1'])
        S.dma('sp', b.NORMC[:], b.dram['norms_c'], writes=['c2'])
        S.dma('sp', b.KVNC[:], b.dram['kvn_c'], writes=['c3'])
        S.dma('sp', b.AEXPN[:], b.dram['a_log'].partition_broadcast(128), writes=['c4'])
        S.dma('sp', b.DTB[:], b.dram['a_dt_bias'].partition_broadcast(128), writes=['c5'])
        S.dma('sp', b.OGAIN[:], b.dram['ogain_c'], writes=['c6'])
        S.dma('sp', b.CONVW[:], b.dram['convw_c'], writes=['c7'])
        S.op('pool', lambda e: e.memset(b.CC[:, 0:1], EPS), writes=['c8'])
        S.op('pool', lambda e: e.memset(b.CC[:, 1:2], 1.0), writes=['c8'])
        S.op('pool', lambda e: e.memset(b.CC[:, 2:3], 0.0), writes=['c8'])
        S.op('pool', lambda e: e.memset(b.CC[:, 3:4], -0.5 * math.log(128.0)), writes=['c8'])
        S.op('act', lambda e: e.activation(out=b.AEXPN[:], in_=b.AEXPN[:], func=AF.Exp), reads=['c4'], writes=['c4'])
        S.op('act', lambda e: e.mul(out=b.AEXPN[:], in_=b.AEXPN[:], mul=-1.0), reads=['c4'], writes=['c4'])
        S.op('pool', lambda e: e.memset(b.S32[:], 0.0), writes=[('S32', l_, i) for l_ in range(2) for i in range(4)])
        S.op('pool', lambda e: e.memset(b.SB[:], 0.0), writes=[('SB', i) for i in range(4)])
        S.op('pool', lambda e: e.memset(b.HIST[:], 0.0), writes=['HIST'])
        S.op('pool', lambda e: e.memset(b.KT2[:], 0.0), writes=['KT2'])
        S.op('pool', lambda e: e.memset(b.VD[:], 0.0), writes=['VD'])
        ck = ['c%d' % i for i in range(9)]
        for e in ('pe', 'act', 'dve', 'pool'):
            S._sync(e, ck, [], False)
        st = b.stage
        for g in range(5):
            ntok = 512 if g < 4 else 128
            if g < 4:
                pass
            b.load_group(g, ntok)
            for l in range(4):
                if st < 10 * l + 1:
                    break
                if l < 2:
                    if g < 4:
                        S.op('act', lambda e: e.activation(out=b.SB[:], in_=b.S32[:, l], func=AF.Copy), reads=[('S32', l, i) for i in range(4)], writes=[('SB', i) for i in range(4)])
                    b.gdn(l, g, ntok)
                else:
                    b.attn(l, g, ntok)
                if st < 10 * l + 2:
                    break
                b.ffn(l, ntok)
                if l == 1 and st >= 13:
                    b.kvproj(g, ntok)
            b.store_group(g, ntok)
        S.finish()


IN_SPECS = [
    ('xp', [SEQ, D]), ('xs', [128, D]), ('sconv', [2, NSEQ_S, 3, QKV]), ('sdelta', [2, NSEQ_S, 8, 128, 128]),
    ('ck', [NSEQ_S, SEQ, 256]), ('cv', [NSEQ_S, SEQ, 256]),
    ('norms_c', [128, 128]), ('kvn_c', [128, 8]), ('a_log', [16]), ('a_dt_bias', [16]), ('ogain_c', [128, 2]),
    ('convw_c', [128, 192]), ('cf', [128, 0]), ('cb', [128, 0]), ('esamp', [128, NKT * 384]), ('dcm', [128, 14 * 128]),
    ('a_w_in', [2, D, APROJ]), ('a_w_out', [2, D, D]), ('b_w_kv', [D, 512]), ('b_w_q', [2, D, 3072]), ('b_w_o', [2, D, D]),
    ('ffn_w_in', [4, D, 2 * DFF]), ('ffn_w_out', [4, DFF, D]),
]
OUT_SPECS = [
    ('yp', [SEQ, D]), ('ys', [128, D]), ('convp', [2, 3, QKV]), ('deltap', [2, 8, 128, 128]),
    ('ckp', [SEQ, 256]), ('cvp', [SEQ, 256]), ('convs', [2, NSEQ_S, 3, QKV]), ('deltas', [2, NSEQ_S, 8, 128, 128]),
    ('cks', [128, 256]), ('cvs', [128, 256]),
]

CF_ARR, CB_ARR, ESAMP_ARR = _build_consts()
CF_W, CB_W = CF_ARR.shape[1], CB_ARR.shape[1]
for _i, (_n, _s) in enumerate(IN_SPECS):
    if _n == 'cf':
        IN_SPECS[_i] = ('cf', [128, CF_W])
    if _n == 'cb':
        IN_SPECS[_i] = ('cb', [128, CB_W])


def build_program(stage=99):
    wplan = []
    nc0 = bass.Bass("TRN2", target_bir_lowering=False)
    with ExitStack() as es0:
        Builder(nc0, es0, True, wplan, stage).build()
    nc = bass.Bass("TRN2", target_bir_lowering=False)
    es = ExitStack()
    b1 = Builder(nc, es, False, wplan, stage)
    b1.build()
    es.close()
    return nc, b1


def make_in_maps(inp):
    f = lambda a: np.ascontiguousarray(np.asarray(a, dtype=np.float32))
    norms = f(inp['norms'])
    shared = dict(
        norms_c=f(norms.reshape(16, 8, 128).transpose(2, 0, 1).reshape(128, 128)),
        kvn_c=f(f(inp['kv_norm']).reshape(8, 128).T),
        a_log=f(inp['a_log']).reshape(16), a_dt_bias=f(inp['a_dt_bias']).reshape(16),
        ogain_c=f(f(inp['a_o_gain']).T),
        convw_c=f(f(inp['a_conv_w']).reshape(2, 4, 24, 128).transpose(3, 0, 2, 1).reshape(128, 192)),
        cf=CF_ARR, cb=CB_ARR, esamp=ESAMP_ARR, dcm=DCM_ARR,
        a_w_in=f(inp['a_w_in']), a_w_out=f(inp['a_w_out']), b_w_kv=f(inp['b_w_kv']), b_w_q=f(inp['b_w_q']),
        b_w_o=f(inp['b_w_o']), ffn_w_in=f(inp['ffn_w_in']), ffn_w_out=f(inp['ffn_w_out']),
    )
    maps = []
    for c in range(NCORE):
        sl = slice(NSEQ_S * c, NSEQ_S * (c + 1))
        m = dict(shared)
        m['xp'] = f(inp['x_prompt'][c])
        m['xs'] = f(np.asarray(inp['x_sample'])[sl].reshape(128, D))
        m['sconv'] = f(np.asarray(inp['state_conv'])[:, sl])
        m['sdelta'] = f(np.asarray(inp['state_delta'])[:, sl])
        m['ck'] = f(np.asarray(inp['cache_k'])[sl].reshape(NSEQ_S, SEQ, 256))
        m['cv'] = f(np.asarray(inp['cache_v'])[sl].reshape(NSEQ_S, SEQ, 256))
        maps.append(m)
    return maps


_PROG = {}


def kernel(**inputs):
    if 'nc' not in _PROG:
        _PROG['nc'] = build_program()[0]
    nc = _PROG['nc']
    maps = make_in_maps(inputs)
    res = run_bass_kernel_spmd(nc, maps, core_ids=list(range(NCORE)))
    R = res.results
    st = lambda k: np.stack([np.asarray(R[c][k], dtype=np.float32) for c in range(NCORE)], axis=0)
    y_prompt = st('yp')
    y_sample = st('ys').reshape(NCORE * NSEQ_S, LS, D)
    conv_p = st('convp').transpose(1, 0, 2, 3)
    delta_p = st('deltap').transpose(1, 0, 2, 3, 4)
    ck_p = st('ckp').reshape(NCORE, SEQ, 4, 64)
    cv_p = st('cvp').reshape(NCORE, SEQ, 4, 64)
    conv_s = np.concatenate([np.asarray(R[c]['convs'], dtype=np.float32) for c in range(NCORE)], axis=1)
    delta_s = np.concatenate([np.asarray(R[c]['deltas'], dtype=np.float32) for c in range(NCORE)], axis=1)
    ck_s = st('cks').reshape(NCORE * NSEQ_S, LS, 4, 64)
    cv_s = st('cvs').reshape(NCORE * NSEQ_S, LS, 4, 64)
    return (y_prompt, y_sample, np.ascontiguousarray(conv_p), np.ascontiguousarray(delta_p), ck_p, cv_p,
            np.ascontiguousarray(conv_s), np.ascontiguousarray(delta_s), ck_s, cv_s)
```

```python
import math
import numpy as np
import ml_dtypes
import concourse.bass as bass
import concourse.mybir as mybir
from concourse.bass_utils import run_bass_kernel_spmd
from contextlib import ExitStack

F32, BF16 = mybir.dt.float32, mybir.dt.bfloat16
ALU, AF = mybir.AluOpType, mybir.ActivationFunctionType
AX = mybir.AxisListType

D = 1024
SEQ = 2048
NCORE = 8
NSEQ_S = 16
LS = 8
EPS = 1e-6
DFF = 2816
NFF = 22
QKV = 3072
APROJ = 4112
NDS = 8
WSLOT = 3072
NWS = 4
NEGBIG = -1.0e5
PG = 512
ARBYTES = 83456
B_GROUPS = ((128, 1), (512, 4), (2048, 16))


NCT = 8
NKT = 13


def sample_key_pos(t):
    k = np.arange(128)
    if t < NCT:
        return np.where(k < 96, 16 * k + t, -1)
    if t < NKT - 1:
        return 1536 + 128 * (t - NCT) + k
    return np.where(k < LS, SEQ + k, -1)


def alibi_slopes():
    n = 48
    return (2.0 ** (-8.0 * np.arange(1, n + 1) / n)).astype(np.float32).reshape(3, 16)


class Sched:
    def __init__(s, nc, es, dry):
        s.nc, s.dry = nc, dry
        s.eng = dict(pe=nc.tensor, act=nc.scalar, dve=nc.vector, pool=nc.gpsimd, sp=nc.sync)
        s.cnt = dict(pe=0, act=0, dve=0, pool=0)
        s.state = {}
        s.waited = {e: {} for e in s.eng}
        s.dcnt = {q: [0] * NDS for q in ('sp', 'pool')}
        s.dnext = {q: 0 for q in ('sp', 'pool')}
        s.nins = 0
        if not dry:
            s.sem = {e: es.enter_context(nc.semaphore("sem_" + e)) for e in s.cnt}
            s.dsem = {q: [es.enter_context(nc.semaphore(f"dsem_{q}{i}")) for i in range(NDS)]
                      for q in ('sp', 'pool')}

    def _semh(s, key):
        return s.sem[key] if isinstance(key, str) else s.dsem[key[0]][key[1]]

    def _wait(s, e, ev):
        key, val = ev
        if s.waited[e].get(key, 0) >= val:
            return
        s.waited[e][key] = val
        if not s.dry:
            s.eng[e].wait_ge(s._semh(key), val)

    def _sync(s, e, reads, writes, is_dma):
        for k in reads:
            st = s.state.get(k)
            if st and st[0] is not None:
                ev = st[0]
                if e == 'pe' and ev[0] == 'pe':
                    continue
                s._wait(e, ev)
        for k in writes:
            st = s.state.get(k)
            if not st:
                continue
            evs = ([st[0]] if st[0] is not None else []) + list(st[1].items())
            for ev in evs:
                if (not is_dma) and ev[0] == e:
                    continue
                s._wait(e, ev)

    def _record(s, ev, reads, writes):
        for k in reads:
            st = s.state.setdefault(k, [None, {}])
            st[1][ev[0]] = max(st[1].get(ev[0], 0), ev[1])
        for k in writes:
            s.state[k] = [ev, {}]

    def op(s, e, fn, reads=(), writes=(), inc=True):
        s._sync(e, reads, writes, False)
        s.nins += 1
        if inc:
            s.cnt[e] += 1
            ev = (e, s.cnt[e])
        else:
            ev = (e, s.cnt[e] + 1)
        if not s.dry:
            ins = fn(s.eng[e])
            if inc:
                ins.then_inc(s.sem[e], 1)
        s._record(ev, reads, writes)

    def dma(s, q, out, in_, reads=(), writes=()):
        i = s.dnext[q]
        s.dnext[q] = (i + 1) % NDS
        if s.dcnt[q][i]:
            s._wait(q, ((q, i), s.dcnt[q][i]))
        s._sync(q, reads, writes, True)
        s.dcnt[q][i] += 16
        ev = ((q, i), s.dcnt[q][i])
        s.nins += 1
        if not s.dry:
            s.eng[q].dma_start(out=out, in_=in_).then_inc(s.dsem[q][i], 16)
        s._record(ev, reads, writes)

    def finish(s):
        for q in ('sp', 'pool'):
            for i in range(NDS):
                if s.dcnt[q][i]:
                    s._wait('sp', ((q, i), s.dcnt[q][i]))


class AV:
    def __init__(s, ap, off, nb, shape, es):
        s.ap, s.off, s.nb, s.shape, s.es = ap, off, nb, shape, es

    def k(s, lo=0, hi=None):
        hi = s.nb if hi is None else hi
        return [('pg', p) for p in range((s.off + lo) // PG, (s.off + hi - 1) // PG + 1)]

    def kc(s, c, n=1):
        per = s.nb // s.shape[1]
        return s.k(c * per, (c + n) * per)


def bc(ap, shape, axis):
    return ap.unsqueeze(axis).to_broadcast(list(shape))


CF = {}
CB = {}


def _build_consts():
    f = []
    off = 0

    def addf(name, arr):
        nonlocal off
        arr = np.asarray(arr, np.float32)
        assert arr.shape[0] == 128
        CF[name] = (off, arr.shape[1])
        f.append(arr)
        off += arr.shape[1]

    i = np.arange(128)
    q, s_ = i[:, None], i[None, :]
    same = (q // LS) == (s_ // LS)
    addf('ident', np.eye(128))
    addf('ones', np.ones((128, 128)))
    addf('negones', -np.ones((128, 128)))
    addf('tri_p', (q <= s_))
    addf('tri_s', (q <= s_) & same)
    addf('suf_p', (q > s_))
    addf('suf_s', (q > s_) & same)
    addf('neg1_p', np.where(q > s_, 0.0, NEGBIG))
    addf('neg2_p', np.where(s_ > q, 0.0, NEGBIG))
    addf('neg3_p', np.where(s_ >= q, 0.0, NEGBIG))
    addf('neg1_s', np.where((q > s_) & same, 0.0, NEGBIG))
    addf('neg2_s', np.where((s_ > q) & same, 0.0, NEGBIG))
    addf('neg3_s', np.where((s_ >= q) & same, 0.0, NEGBIG))
    addf('seqsel', (i[:, None] // LS) == np.arange(NSEQ_S)[None, :])
    j = np.arange(256)[None, :]
    addf('deltac', np.maximum(j - q, 0).astype(np.float32))
    cf = np.concatenate(f, axis=1)

    b = []
    offb = 0

    def addb(name, arr):
        nonlocal offb
        arr = np.asarray(arr, np.float32)
        CB[name] = (offb, arr.shape[1])
        b.append(arr)
        offb += arr.shape[1]

    addb('ident', np.eye(128))
    addb('ones', np.ones((128, 128)))
    dl = j - q
    addb('maskj', np.where(j < 128, dl >= 0, dl <= 128))
    sl = alibi_slopes()
    tab = np.zeros((NKT, 128, 4, 3, 2, 2, LS), np.float32)
    for t in range(NKT):
        pos = sample_key_pos(t)
        for l in range(LS):
            dist = (SEQ + l) - pos
            for grp, (win, dil) in enumerate(B_GROUPS):
                valid = (pos >= 0) & (dist >= 0) & (dist <= win) & (dist % dil == 0)
                for g in range(4):
                    for hh in range(2):
                        for pp in range(2):
                            h = 4 * g + 2 * pp + hh
                            tab[t, :, g, grp, pp, hh, l] = np.where(
                                valid, np.exp(-sl[grp, h] * np.maximum(dist, 0).astype(np.float64)), 0.0)
    dcm = np.zeros((128, 14, 128), np.float32)
    for lev in range(7):
        m = 1 << lev
        M = ((q // (2 * m)) == (s_ // (2 * m))) & ((q % (2 * m)) >= m) & ((s_ % (2 * m)) < m)
        dcm[:, 2 * lev, :] = M
        dcm[:, 2 * lev + 1, :] = M.T
    global DCM_ARR
    DCM_ARR = np.ascontiguousarray(dcm.reshape(128, 14 * 128)).astype(ml_dtypes.bfloat16)
    esamp = np.ascontiguousarray(tab.transpose(1, 0, 2, 3, 4, 5, 6).reshape(128, NKT * 384)).astype(ml_dtypes.bfloat16)
    cb = np.concatenate(b, axis=1).astype(ml_dtypes.bfloat16)
    return cf, cb, esamp


class Builder:
    HTK = [('HT', c) for c in range(8)]

    def __init__(b, nc, es, dry, wplan, stage):
        b.nc, b.es, b.dry, b.stage = nc, es, dry, stage
        b.S = Sched(nc, es, dry)
        b.wplan = wplan
        b.wcur = 0
        b.wissued = 0
        b.psrr = 0
        b.dram = {}
        b.aoff = 0
        b.nrr = 4

    def sb(b, name, shape, dt):
        return b.es.enter_context(b.nc.sbuf_tensor(name, list(shape), dt))

    def din(b, name, shape, dt=F32):
        t = b.nc.dram_tensor(name, list(shape), dt, kind="ExternalInput").ap()
        b.dram[name] = t
        return t

    def dout(b, name, shape, dt=F32):
        t = b.nc.dram_tensor(name, list(shape), dt, kind="ExternalOutput").ap()
        b.dram[name] = t
        return t

    def bank(b, i, n=1):
        return b.PS[:, 512 * i:512 * (i + n)]

    def bankb(b, i):
        return b.PS[:, 512 * i:512 * (i + 1)].bitcast(BF16)

    def rr(b, n=1):
        if b.psrr + n > b.nrr:
            b.psrr = 0
        i = b.psrr
        b.psrr = (b.psrr + n) % b.nrr
        return i

    @staticmethod
    def pk(i, n=1):
        return [('ps', i + k) for k in range(n)]

    def wnext(b, spec):
        if b.dry:
            b.wplan.append(spec)
            idx = len(b.wplan) - 1
        else:
            idx = b.wcur
            assert b.wplan[idx] == spec, (b.wplan[idx], spec)
            b.wcur += 1
            while b.wissued < min(len(b.wplan), idx + NWS - 1):
                b._wissue(b.wissued)
                b.wissued += 1
        name, lyr, r0, nr, c0, ncols = spec
        kc = nr // 128
        slot = idx % NWS
        ap = b.WB[:, slot * WSLOT: slot * WSLOT + kc * ncols].rearrange("p (c n) -> p c n", c=kc)
        return ap, ('w', slot)

    def _wissue(b, idx):
        name, lyr, r0, nr, c0, ncols = b.wplan[idx]
        kc = nr // 128
        assert kc * ncols <= WSLOT
        slot = idx % NWS
        nb = b.nblk
        g, i = idx // nb, idx % nb
        dst2 = b.WB[:, slot * WSLOT: slot * WSLOT + kc * ncols]
        if g == 0:
            src = b.dram[name]
            if lyr is not None:
                src = src[lyr]
            src = src[r0:r0 + nr, c0:c0 + ncols].rearrange("(c p) n -> p c n", p=128)
            b.S.dma('pool', dst2.rearrange("p (c n) -> p c n", c=kc), src, writes=[('w', slot)])
            b.S.dma('sp', b.WSCR[i, :, 0:kc * ncols], dst2, reads=[('w', slot)], writes=[('wscr', i)])
        else:
            assert b.wplan[i] == b.wplan[idx]
            b.S.dma('sp', dst2, b.WSCR[i, :, 0:kc * ncols], reads=[('wscr', i)], writes=[('w', slot)])

    def pieces(b, ntok):
        return [(o, min(512, ntok - o)) for o in range(0, ntok, 512)]

    def fm_matmul(b, ps_ap, w_ap, wkey, col0, ncol, act_ap, actkeys, kcn, pskeys):
        for kc in range(kcn):
            b.S.op('pe', lambda e, kc=kc: e.matmul(ps_ap, lhsT=w_ap[:, kc, col0:col0 + ncol], rhs=act_ap(kc),
                                                  start=(kc == 0), stop=(kc == kcn - 1)),
                   reads=[wkey] + list(actkeys), writes=pskeys, inc=(kc == kcn - 1))


    def abegin(b):
        b.aoff = 0

    def aalloc(b, shape, dt):
        es = 4 if dt == F32 else 2
        nel = int(np.prod(shape[1:]))
        nb = (nel * es + 511) // 512 * 512
        off = b.aoff
        b.aoff += nb
        assert b.aoff <= ARBYTES, ("arena overflow", b.aoff)
        ap = b.AR[0:shape[0], off // 2: off // 2 + nel * es // 2]
        if dt == F32:
            ap = ap.bitcast(F32)
        if len(shape) == 3:
            ap = ap.rearrange("p (a c) -> p a c", a=shape[1])
        elif len(shape) == 4:
            ap = ap.rearrange("p (a c d) -> p a c d", a=shape[1], c=shape[2])
        return AV(ap, off, nel * es, shape, es)

    def cf(b, name, c0=0, c1=None):
        o, w = CF[name]
        return b.CFT[:, o + c0: o + (w if c1 is None else c1)]

    def cb(b, name, c0=0, c1=None):
        o, w = CB[name]
        return b.CBT[:, o + c0: o + (w if c1 is None else c1)]

    def ncol(b, l, i):
        k = (l * 4 + i) * 8
        return b.NORMC[:, k:k + 8]

    def _rstd_fm(b, src, srckey, o, n, SQ, RS):
        S = b.S
        S.op('act', lambda e: e.activation(out=SQ.ap[:, 0:5, 0:n], in_=src[:, 0:5, o:o + n], func=AF.Square),
             reads=[srckey], writes=SQ.kc(0, 5))
        S.op('dve', lambda e: e.tensor_tensor(out=SQ.ap[:, 5:8, 0:n], in0=src[:, 5:8, o:o + n], in1=src[:, 5:8, o:o + n], op=ALU.mult),
             reads=[srckey], writes=SQ.kc(5, 3))
        bi = b.rr()
        for c in range(8):
            S.op('pe', lambda e, c=c: e.matmul(b.bank(bi)[:, 0:n], lhsT=b.cb('ones'), rhs=SQ.ap[:, c, 0:n],
                                               start=(c == 0), stop=(c == 7)),
                 reads=SQ.kc(c), writes=b.pk(bi), inc=(c == 7))
        S.op('act', lambda e: e.activation(out=RS.ap[:, 0:n], in_=b.bank(bi)[:, 0:n], func=AF.Ln,
                                           bias=b.epsc[:, 0:1], scale=1.0 / D),
             reads=[], writes=b.pk(bi) + RS.k())
        S.op('act', lambda e: e.activation(out=RS.ap[:, 0:n], in_=RS.ap[:, 0:n], func=AF.Exp, scale=-0.5),
             reads=RS.k(), writes=RS.k())

    def rmsnorm_fm(b, src, srckey, gain_col, dst, dstkey, ntok):
        S = b.S
        save = b.aoff
        SQ = b.aalloc([128, 8, 512], BF16)
        RS = b.aalloc([128, 512], F32)
        for (o, n) in b.pieces(ntok):
            b._rstd_fm(src, srckey, o, n, SQ, RS)
            for c in range(8):
                eng = 'dve'
                S.op(eng, lambda e, c=c: e.scalar_tensor_tensor(
                    out=dst[:, c, o:o + n], in0=src[:, c, o:o + n], scalar=gain_col[:, c:c + 1],
                    in1=RS.ap[:, 0:n], op0=ALU.mult, op1=ALU.mult),
                    reads=[srckey] + RS.k(), writes=(dstkey(c) if callable(dstkey) else [(dstkey, c)]))
        b.aoff = save

    def postnorm_add(b, gain_col, ntok):
        S = b.S
        save = b.aoff
        SQ = b.aalloc([128, 8, 512], BF16)
        RS = b.aalloc([128, 512], F32)
        for (o, n) in b.pieces(ntok):
            b._rstd_fm(b.YT, 'YT', o, n, SQ, RS)
            for c in range(8):
                S.op('dve', lambda e, c=c: e.scalar_tensor_tensor(
                    out=b.YT[:, c, o:o + n], in0=b.YT[:, c, o:o + n], scalar=gain_col[:, c:c + 1],
                    in1=RS.ap[:, 0:n], op0=ALU.mult, op1=ALU.mult),
                    reads=['YT'] + RS.k(), writes=['YT'])
            S.op('dve', lambda e: e.tensor_tensor(out=b.XT[:, :, o:o + n], in0=b.XT[:, :, o:o + n],
                                                  in1=b.YT[:, :, o:o + n], op=ALU.add),
                 reads=['YT', 'XT'], writes=['XT'])
        b.aoff = save

    def ffn(b, l, ntok):
        S = b.S
        b.nrr = 6
        b.psrr = 0
        b.abegin()
        ACT = b.aalloc([128, NFF, 512], BF16)
        TMPB = b.aalloc([128, 2, 512], BF16)
        b.rmsnorm_fm(b.XT, 'XT', b.ncol(l, 2), b.HT, 'HT', ntok)
        it = 0
        for blk in range(11):
            wg, kg = b.wnext(('ffn_w_in', l, 0, D, 256 * blk, 256))
            wu, ku = b.wnext(('ffn_w_in', l, 0, D, DFF + 256 * blk, 256))
            for j in range(2):
                fc = 2 * blk + j
                for (o, n) in b.pieces(ntok):
                    bg, bu = b.rr(), b.rr()
                    b.fm_matmul(b.bank(bg)[:, 0:n], wg, kg, 128 * j, 128, lambda kc: b.HT[:, kc, o:o + n], b.HTK, 8, b.pk(bg))
                    b.fm_matmul(b.bank(bu)[:, 0:n], wu, ku, 128 * j, 128, lambda kc: b.HT[:, kc, o:o + n], b.HTK, 8, b.pk(bu))
                    tb = it % 2
                    it += 1
                    S.op('act', lambda e: e.activation(out=TMPB.ap[:, tb, 0:n], in_=b.bank(bg)[:, 0:n], func=AF.Silu),
                         reads=[], writes=b.pk(bg) + TMPB.kc(tb))
                    S.op('dve', lambda e: e.tensor_tensor(out=ACT.ap[:, fc, o:o + n], in0=b.bank(bu)[:, 0:n],
                                                          in1=TMPB.ap[:, tb, 0:n], op=ALU.mult),
                         reads=TMPB.kc(tb), writes=b.pk(bu) + ACT.kc(fc))
        for dc in range(8):
            w, k = b.wnext(('ffn_w_out', l, 0, DFF, 128 * dc, 128))
            for (o, n) in b.pieces(ntok):
                bi = b.rr()
                b.fm_matmul(b.bank(bi)[:, 0:n], w, k, 0, 128, lambda kc: ACT.ap[:, kc, o:o + n],
                            ACT.k(), NFF, b.pk(bi))
                if dc % 2 == 0:
                    S.op('act', lambda e: e.activation(out=b.YT[:, dc, o:o + n], in_=b.bank(bi)[:, 0:n], func=AF.Copy),
                         reads=[], writes=b.pk(bi) + ['YT'])
                else:
                    S.op('dve', lambda e: e.tensor_copy(out=b.YT[:, dc, o:o + n], in_=b.bank(bi)[:, 0:n]),
                         reads=[], writes=b.pk(bi) + ['YT'])
        b.postnorm_add(b.ncol(l, 3), ntok)

    def gdn(b, l, g, ntok):
        S = b.S
        sample = (g == 4)
        b.nrr = 4 if sample else 6
        b.psrr = 0
        nt = ntok // 128
        n = ntok
        sfx = '_s' if sample else '_p'
        nsq = 2 if sample else 6
        b.abegin()
        b.rmsnorm_fm(b.XT, 'XT', b.ncol(l, 0), b.HT, 'HT', ntok)
        QK = b.aalloc([128, 16, n], BF16)
        VT = b.aalloc([128, 8, n], BF16)
        OT = b.aalloc([128, 8, n], BF16)
        BD = b.aalloc([128, 4, 16], F32)
        SM = b.aalloc([128, 10, 4, 8], F32)
        sm = lambda i: SM.ap[:, i, 0:nt, :]
        mark = b.aoff

        w, wk = b.wnext(('a_w_in', l, 0, D, 4096, 16))
        for t in range(nt):
            bi = b.rr()
            for kc in range(8):
                S.op('pe', lambda e, kc=kc: e.matmul(b.bank(bi)[:, 0:16], lhsT=b.HT[:, kc, 128 * t:128 * (t + 1)],
                                                     rhs=w[:, kc, :], start=(kc == 0), stop=(kc == 7)),
                     reads=b.HTK + [wk], writes=b.pk(bi), inc=(kc == 7))
            S.op('act', lambda e: e.activation(out=BD.ap[:, t, :], in_=b.bank(bi)[:, 0:16], func=AF.Copy),
                 reads=[], writes=b.pk(bi) + BD.k())
        braw, draw = BD.ap[:, 0:nt, 0:8], BD.ap[:, 0:nt, 8:16]
        smk = SM.k()
        S.op('dve', lambda e: e.scalar_tensor_tensor(out=sm(2), in0=braw, scalar=-1.0, in1=braw, op0=ALU.mult, op1=ALU.min), reads=BD.k(), writes=smk)
        S.op('act', lambda e: e.activation(out=sm(2), in_=sm(2), func=AF.Exp), reads=smk, writes=smk)
        S.op('act', lambda e: e.activation(out=sm(2), in_=sm(2), func=AF.Ln, bias=b.onec[:, 0:1]), reads=smk, writes=smk)
        S.op('dve', lambda e: e.scalar_tensor_tensor(out=sm(1), in0=braw, scalar=0.0, in1=sm(2), op0=ALU.min, op1=ALU.subtract),
             reads=BD.k() + smk, writes=smk)
        S.op('dve', lambda e: e.tensor_tensor(out=sm(3), in0=draw, in1=bc(b.DTB[:, 8 * l:8 * l + 8], [128, nt, 8], 1), op=ALU.add),
             reads=BD.k(), writes=smk)
        S.op('dve', lambda e: e.scalar_tensor_tensor(out=sm(2), in0=sm(3), scalar=-1.0, in1=sm(3), op0=ALU.mult, op1=ALU.min), reads=smk, writes=smk)
        S.op('act', lambda e: e.activation(out=sm(2), in_=sm(2), func=AF.Exp), reads=smk, writes=smk)
        S.op('act', lambda e: e.activation(out=sm(2), in_=sm(2), func=AF.Ln, bias=b.onec[:, 0:1]), reads=smk, writes=smk)
        S.op('dve', lambda e: e.scalar_tensor_tensor(out=sm(3), in0=sm(3), scalar=0.0, in1=sm(2), op0=ALU.max, op1=ALU.add),
             reads=smk, writes=smk)
        S.op('dve', lambda e: e.tensor_tensor(out=sm(0), in0=sm(3), in1=bc(b.AEXPN[:, 8 * l:8 * l + 8], [128, nt, 8], 1), op=ALU.mult),
             reads=smk, writes=smk)

        CBUF = b.aalloc([128, 3, n + 48], F32)
        CACC2 = b.aalloc([128, 3, n], F32)
        CSIL2 = b.aalloc([128, 3, n], F32)
        RI2 = b.aalloc([128, 3, n], F32)
        if sample:
            SCIN = b.aalloc([48, QKV], F32)
            SHIST = b.aalloc([128, 24, 48], F32)
            SHO = b.aalloc([128, 24, 48], F32)
            S.dma('sp', SCIN.ap, b.dram['sconv'][l].rearrange("b j c -> (b j) c"), writes=SCIN.k())
            for r in range(3):
                bi = b.rr()
                for c8 in range(8):
                    c = 8 * r + c8
                    S.op('pe', lambda e, c=c, c8=c8: e.transpose(out=b.bank(bi)[:, 48 * c8:48 * (c8 + 1)],
                                                              in_=SCIN.ap[:, 128 * c:128 * (c + 1)], identity=b.cf('ident', 0, 48)[0:48, :]),
                         reads=SCIN.k(), writes=b.pk(bi), inc=(c8 == 7))
                S.op('act', lambda e: e.activation(out=SHIST.ap[:, 8 * r:8 * r + 8, :],
                                                   in_=b.bank(bi)[:, 0:384].rearrange("p (c j) -> p c j", c=8), func=AF.Copy),
                     reads=[], writes=b.pk(bi) + SHIST.k())
        sub = lambda V3, j: AV(V3.ap[:, j, :], V3.off + j * (V3.nb // 3), V3.nb // 3, [128, V3.shape[2]], 4)
        for blk in range(8):
            w, wk = b.wnext(('a_w_in', l, 0, D, 384 * blk, 384))
            st = []
            for j in range(3):
                c = 3 * blk + j
                bi = b.rr()
                b.fm_matmul(b.bank(bi)[:, 0:n], w, wk, 128 * j, 128, lambda kc: b.HT[:, kc, 0:n], b.HTK, 8, b.pk(bi))
                cbk = CBUF.kc(j)
                CACC, CSIL, RI = sub(CACC2, j), sub(CSIL2, j), sub(RI2, j)
                cw = (lambda c: (lambda jj: b.CONVW[:, (l * 24 + c) * 4 + jj:(l * 24 + c) * 4 + jj + 1]))(c)
                if not sample:
                    cb_ = CBUF.ap[:, j, :]
                    S.op('pool', lambda e: e.tensor_copy(out=cb_[:, 0:3], in_=b.HIST[:, l, c, :]), reads=['HIST'], writes=cbk)
                    S.op('act', lambda e: e.activation(out=cb_[:, 3:3 + n], in_=b.bank(bi)[:, 0:n], func=AF.Copy),
                         reads=[], writes=b.pk(bi) + cbk)
                    S.op('pool', lambda e: e.tensor_copy(out=b.HIST[:, l, c, :], in_=cb_[:, n:n + 3]), reads=cbk, writes=['HIST'])
                    xs = (lambda cb_: (lambda jj: cb_[:, jj:jj + n]))(cb_)
                    acc = CACC.ap[:, 0:n]
                else:
                    cb3 = CBUF.ap[:, j, 0:176].rearrange("p (s j) -> p s j", s=16)
                    S.op('pool', lambda e: e.tensor_copy(out=cb3[:, :, 0:3], in_=SHIST.ap[:, c, :].rearrange("p (s j) -> p s j", s=16)),
                         reads=SHIST.k(), writes=cbk)
                    S.op('act', lambda e: e.activation(out=cb3[:, :, 3:11], in_=b.bank(bi)[:, 0:n].rearrange("p (s j) -> p s j", s=16), func=AF.Copy),
                         reads=[], writes=b.pk(bi) + cbk)
                    S.op('pool', lambda e: e.tensor_copy(out=SHO.ap[:, c, :].rearrange("p (s j) -> p s j", s=16), in_=cb3[:, :, 8:11]),
                         reads=cbk, writes=SHO.k())
                    xs = (lambda cb3: (lambda jj: cb3[:, :, jj:jj + 8]))(cb3)
                    acc = CACC.ap[:, 0:n].rearrange("p (s j) -> p s j", s=16)
                S.op('pool', lambda e: e.tensor_scalar(out=acc, in0=xs(0), scalar1=cw(0), scalar2=0.0, op0=ALU.mult, op1=ALU.add),
                     reads=cbk, writes=CACC.k())
                st.append((c, cbk, CACC, CSIL, RI, cw, xs, acc))
            for (c, cbk, CACC, CSIL, RI, cw, xs, acc) in st:
                for jj in range(1, 4):
                    S.op('dve', lambda e, jj=jj: e.scalar_tensor_tensor(out=acc, in0=xs(jj), scalar=cw(jj), in1=acc, op0=ALU.mult, op1=ALU.add),
                         reads=cbk + CACC.k(), writes=CACC.k())
            for (c, cbk, CACC, CSIL, RI, cw, xs, acc) in st:
                if c < 16:
                    S.op('act', lambda e: e.activation(out=CSIL.ap[:, 0:n], in_=CACC.ap[:, 0:n], func=AF.Silu), reads=CACC.k(), writes=CSIL.k())
                else:
                    S.op('act', lambda e: e.activation(out=VT.ap[:, c - 16, 0:n], in_=CACC.ap[:, 0:n], func=AF.Silu),
                         reads=CACC.k(), writes=VT.kc(c - 16))
            qk = [x for x in st if x[0] < 16]
            banks = []
            for (c, cbk, CACC, CSIL, RI, cw, xs, acc) in qk:
                S.op('dve', lambda e: e.tensor_tensor(out=RI.ap[:, 0:n], in0=CSIL.ap[:, 0:n], in1=CSIL.ap[:, 0:n], op=ALU.mult), reads=CSIL.k(), writes=RI.k())
                b2 = b.rr()
                S.op('pe', lambda e: e.matmul(b.bank(b2)[:, 0:n], lhsT=b.cf('ones'), rhs=RI.ap[:, 0:n], start=True, stop=True),
                     reads=RI.k(), writes=b.pk(b2))
                banks.append(b2)
            for (c, cbk, CACC, CSIL, RI, cw, xs, acc), b2 in zip(qk, banks):
                S.op('act', lambda e: e.activation(out=RI.ap[:, 0:n], in_=b.bank(b2)[:, 0:n], func=AF.Ln, bias=b.epsc[:, 0:1]),
                     reads=[], writes=b.pk(b2) + RI.k())
            for (c, cbk, CACC, CSIL, RI, cw, xs, acc) in qk:
                S.op('act', lambda e: e.activation(out=RI.ap[:, 0:n], in_=RI.ap[:, 0:n], func=AF.Exp, scale=-0.5,
                                                   bias=(b.lnqs[:, 0:1] if c < 8 else b.zeroc[:, 0:1])),
                     reads=RI.k(), writes=RI.k())
            for (c, cbk, CACC, CSIL, RI, cw, xs, acc) in qk:
                S.op('dve', lambda e: e.tensor_tensor(out=QK.ap[:, c, 0:n], in0=CSIL.ap[:, 0:n], in1=RI.ap[:, 0:n], op=ALU.mult),
                     reads=CSIL.k() + RI.k(), writes=QK.kc(c))
        if sample:
            b._conv_out(SHO.ap, SHO.k(), 48, b.dram['convs'][l].rearrange("b j c -> (b j) c"))
        elif g == 3:
            b._conv_out(b.HIST[:, l], ['HIST'], 3, b.dram['convp'][l])
        b.aoff = mark

        KTOK = b.aalloc([128, 8, 128], BF16)
        VTOK = b.aalloc([128, 8, 128], BF16)
        VB = b.aalloc([128, 8, 128], BF16)
        KDK = b.aalloc([128, 8, 128], BF16)
        UB = b.aalloc([128, 8, 128], BF16)
        O32 = b.aalloc([128, 8, 128], F32)
        ON = KTOK
        T8 = b.aalloc([128, 12, 8], F32)
        RG = b.aalloc([128, 4, 128], F32)
        LBI = b.aalloc([128, 4, 128], F32)
        NB = b.aalloc([128, 2, 512], F32)
        E1F = b.aalloc([128, 512], F32)
        E3B = b.aalloc([128, 512], BF16)
        AA = b.aalloc([128, 2, 512], F32)
        A32 = AV(AA.ap[:, 0, :], AA.off, AA.nb // 2, [128, 512], 4)
        E2F = AV(AA.ap[:, 1, :], AA.off + AA.nb // 2, AA.nb // 2, [128, 512], 4)
        WS = b.aalloc([128, 2, 512], F32)
        AOF2 = b.aalloc([128, 2, 1024], BF16)
        XRB = b.aalloc([128, 4, 128], BF16)
        TPB = b.aalloc([128, 2, 512], BF16)
        DCM = b.aalloc([128, 14, 128], BF16)
        ATT = b.aalloc([128, 4, 128], BF16)
        TF = WS
        WSB = AV(WS.ap[:, 0, :].bitcast(BF16).rearrange("p (a c) -> p a c", a=2), WS.off, WS.nb // 2, [128, 2, 512], 2)
        S.dma('sp', DCM.ap.rearrange("p a c -> p (a c)"), b.dram['dcm'], writes=DCM.k())
        if sample:
            GSEL = b.aalloc([128, 16, 8], F32)
            GES = b.aalloc([128, 16, 8], F32)
            SL32 = b.aalloc([128, 2, 1024], F32)
            SLB = b.aalloc([128, 2, 1024], BF16)
            KQS = b.aalloc([128, 2, 1024], F32)
            UM = b.aalloc([128, 8, 128], BF16)
        SBh = b.SB
        t8 = lambda i: T8.ap[:, i, :]
        t8k = T8.k()
        tri, suf = b.cf('tri' + sfx), b.cf('suf' + sfx)
        neg = [b.cf('neg%d%s' % (i, sfx)) for i in (1, 2, 3)]
        identb = b.cb('ident')
        h4 = lambda ap3: ap3.rearrange("p (h e) -> p h e", h=4)

        for t in range(nt):
            tc_ = slice(128 * t, 128 * (t + 1))
            G_t, LB_t = SM.ap[:, 0, t, :], SM.ap[:, 1, t, :]
            for (src0, dstv) in ((8, KTOK), (None, VTOK)):
                bi = b.rr()
                for h in range(8):
                    srcap = QK.ap[:, src0 + h, tc_] if src0 is not None else VT.ap[:, h, tc_]
                    S.op('pe', lambda e, h=h, srcap=srcap: e.transpose(out=b.bankb(bi)[:, 128 * h:128 * (h + 1)], in_=srcap, identity=identb),
                         reads=(QK.kc(8 + h) if src0 is not None else VT.kc(h)), writes=b.pk(bi), inc=(h == 7))
                S.op('act', lambda e: e.activation(out=dstv.ap, in_=b.bankb(bi).rearrange("p (h e) -> p h e", h=8), func=AF.Copy),
                     reads=[], writes=b.pk(bi) + dstv.k())
            bi = b.rr()
            S.op('pe', lambda e: e.matmul(b.bank(bi)[:, 0:8], lhsT=tri, rhs=G_t, start=True, stop=True), reads=smk, writes=b.pk(bi))
            S.op('pe', lambda e: e.matmul(b.bank(bi)[:, 8:16], lhsT=suf, rhs=G_t, start=True, stop=True), reads=smk, writes=b.pk(bi))
            if not sample:
                S.op('pe', lambda e: e.matmul(b.bank(bi)[:, 16:24], lhsT=b.cf('ones'), rhs=G_t, start=True, stop=True), reads=smk, writes=b.pk(bi))
            S.op('act', lambda e: e.activation(out=t8(0), in_=b.bank(bi)[:, 0:8], func=AF.Copy), reads=[], writes=b.pk(bi) + t8k)
            S.op('act', lambda e: e.activation(out=t8(5), in_=b.bank(bi)[:, 8:16], func=AF.Exp), reads=[], writes=b.pk(bi) + t8k)
            if not sample:
                S.op('act', lambda e: e.activation(out=t8(6), in_=b.bank(bi)[:, 16:24], func=AF.Exp), reads=[], writes=b.pk(bi) + t8k)
            S.op('dve', lambda e: e.tensor_tensor(out=t8(1), in0=t8(0), in1=LB_t, op=ALU.add), reads=t8k + smk, writes=t8k)
            S.op('act', lambda e: e.activation(out=t8(2), in_=t8(0), func=AF.Exp), reads=t8k, writes=t8k)
            S.op('act', lambda e: e.activation(out=t8(3), in_=t8(1), func=AF.Exp), reads=t8k, writes=t8k)
            S.op('act', lambda e: e.activation(out=t8(4), in_=LB_t, func=AF.Exp), reads=smk, writes=t8k)
            if sample:
                S.op('dve', lambda e: e.tensor_tensor(out=GSEL.ap, in0=bc(b.cf('seqsel'), [128, 16, 8], 2), in1=bc(G_t, [128, 16, 8], 1), op=ALU.mult),
                     reads=smk, writes=GSEL.k())
                b2 = b.rr()
                S.op('pe', lambda e: e.matmul(b.bank(b2)[:, 0:128], lhsT=b.cf('ones'), rhs=GSEL.ap.rearrange("p s h -> p (s h)"), start=True, stop=True),
                     reads=GSEL.k(), writes=b.pk(b2))
                S.op('act', lambda e: e.activation(out=GES.ap.rearrange("p s h -> p (s h)"), in_=b.bank(b2)[:, 0:128], func=AF.Exp),
                     reads=[], writes=b.pk(b2) + GES.k())
            S.op('dve', lambda e: e.tensor_tensor(out=VB.ap, in0=VTOK.ap, in1=bc(t8(4), [128, 8, 128], 2), op=ALU.mult),
                 reads=VTOK.k() + t8k, writes=VB.k())
            S.op('dve', lambda e: e.tensor_tensor(out=KDK.ap, in0=KTOK.ap, in1=bc(t8(5), [128, 8, 128], 2), op=ALU.mult),
                 reads=KTOK.k() + t8k, writes=KDK.k())
            if sample:
                bks, bqs = 6, 7
                for s_ in range(NSEQ_S):
                    sl = s_ % 2
                    S.dma('sp', SL32.ap[:, sl, :].rearrange("p (h e) -> p h e", h=8),
                          b.dram['sdelta'][l, s_].rearrange("h d e -> d h e"), writes=SL32.kc(sl))
                    S.op('act', lambda e: e.activation(out=SLB.ap[:, sl, :], in_=SL32.ap[:, sl, :], func=AF.Copy),
                         reads=SL32.kc(sl), writes=SLB.kc(sl))
                    for h in range(8):
                        lw = SLB.ap[:, sl, 128 * h:128 * (h + 1)]
                        S.op('pe', lambda e, h=h, lw=lw: e.matmul(b.PS[:, 2048 + 128 * h + 8 * s_: 2048 + 128 * h + 8 * s_ + 8], lhsT=lw,
                                                                  rhs=QK.ap[:, 8 + h, 8 * s_:8 * s_ + 8], start=True, stop=True),
                             reads=SLB.kc(sl) + QK.kc(8 + h), writes=b.pk(4, 2), inc=False)
                        S.op('pe', lambda e, h=h, lw=lw: e.matmul(b.PS[:, 3072 + 128 * h + 8 * s_: 3072 + 128 * h + 8 * s_ + 8], lhsT=lw,
                                                                  rhs=QK.ap[:, h, 8 * s_:8 * s_ + 8], start=True, stop=True),
                             reads=SLB.kc(sl) + QK.kc(h), writes=b.pk(6, 2), inc=(h == 7))
                S.op('act', lambda e: e.activation(out=KQS.ap[:, 0, :], in_=b.PS[:, 2048:3072], func=AF.Copy), reads=[], writes=b.pk(4, 2) + KQS.kc(0))
                S.op('act', lambda e: e.activation(out=KQS.ap[:, 1, :], in_=b.PS[:, 3072:4096], func=AF.Copy), reads=[], writes=b.pk(6, 2) + KQS.kc(1))
                for kq in range(2):
                    for h in range(8):
                        S.op('pe', lambda e, h=h, kq=kq: e.transpose(out=b.PS[:, 2048 + 1024 * kq + 128 * h: 2048 + 1024 * kq + 128 * (h + 1)],
                                                                     in_=KQS.ap[:, kq, 128 * h:128 * (h + 1)], identity=b.cf('ident')),
                             reads=KQS.kc(kq), writes=b.pk(4 + 2 * kq, 2), inc=(h == 7))

            for hf in range(2):
                hs = slice(4 * hf, 4 * hf + 4)
                S.op('dve', lambda e: e.tensor_tensor(out=RG.ap, in0=bc(tri, [128, 4, 128], 1), in1=bc(G_t[:, hs], [128, 4, 128], 2), op=ALU.mult),
                     reads=smk, writes=RG.k())
                S.op('dve', lambda e: e.tensor_tensor(out=LBI.ap, in0=bc(b.cf('ident'), [128, 4, 128], 1), in1=bc(LB_t[:, hs], [128, 4, 128], 2), op=ALU.mult),
                     reads=smk, writes=LBI.k())
                rgf = RG.ap.rearrange("p h e -> p (h e)")
                lbf = LBI.ap.rearrange("p h e -> p (h e)")
                edst = [E1F, E2F, E3B]
                for i in range(3):
                    nbi = i % 2
                    if i == 0:
                        S.op('dve', lambda e: e.tensor_tensor(out=h4(NB.ap[:, nbi, :]), in0=bc(neg[0], [128, 4, 128], 1), in1=bc(t8(1)[:, hs], [128, 4, 128], 2), op=ALU.add),
                             reads=t8k, writes=NB.kc(nbi))
                        terms = [(b.cf('negones'), rgf, RG.k()), (b.cf('ident'), NB.ap[:, nbi, :], NB.kc(nbi))]
                    else:
                        S.op('dve', lambda e, i=i: e.tensor_tensor(out=h4(NB.ap[:, nbi, :]), in0=bc(neg[i], [128, 4, 128], 1), in1=bc(t8(0)[:, hs], [128, 4, 128], 2), op=ALU.subtract),
                             reads=t8k, writes=NB.kc(nbi))
                        terms = [(b.cf('ones'), rgf, RG.k())] + ([(b.cf('ones'), lbf, LBI.k())] if i == 1 else []) + [(b.cf('ident'), NB.ap[:, nbi, :], NB.kc(nbi))]
                    bi = b.rr()
                    for j, (lt, rh, rk) in enumerate(terms):
                        S.op('pe', lambda e, lt=lt, rh=rh, j=j: e.matmul(b.bank(bi), lhsT=lt, rhs=rh, start=(j == 0), stop=(j == len(terms) - 1)),
                             reads=rk, writes=b.pk(bi), inc=(j == len(terms) - 1))
                    S.op('act', lambda e, i=i: e.activation(out=edst[i].ap, in_=b.bank(bi), func=AF.Exp), reads=[], writes=b.pk(bi) + edst[i].k())
                bkk = b.rr()
                for j in range(4):
                    kT = QK.ap[:, 8 + 4 * hf + j, tc_]
                    S.op('pe', lambda e, j=j, kT=kT: e.matmul(b.bank(bkk)[:, 128 * j:128 * (j + 1)], lhsT=kT, rhs=kT, start=True, stop=True),
                         reads=QK.kc(8 + 4 * hf + j), writes=b.pk(bkk), inc=(j == 3))
                S.op('dve', lambda e: e.tensor_tensor(out=A32.ap, in0=b.bank(bkk), in1=E1F.ap, op=ALU.mult), reads=E1F.k(), writes=b.pk(bkk) + A32.k())
                S.op('dve', lambda e: e.tensor_tensor(out=E2F.ap, in0=b.bank(bkk), in1=E2F.ap, op=ALU.mult), reads=E2F.k(), writes=b.pk(bkk) + E2F.k())
                bqk = b.rr()
                for j in range(4):
                    kT = QK.ap[:, 8 + 4 * hf + j, tc_]
                    qT = QK.ap[:, 4 * hf + j, tc_]
                    S.op('pe', lambda e, j=j, kT=kT, qT=qT: e.matmul(b.bank(bqk)[:, 128 * j:128 * (j + 1)], lhsT=kT, rhs=qT, start=True, stop=True),
                         reads=QK.kc(8 + 4 * hf + j) + QK.kc(4 * hf + j), writes=b.pk(bqk), inc=(j == 3))
                S.op('dve', lambda e: e.tensor_tensor(out=ATT.ap.rearrange("p h e -> p (h e)"), in0=b.bank(bqk), in1=E3B.ap, op=ALU.mult),
                     reads=E3B.k(), writes=b.pk(bqk) + ATT.k())
                aa4 = AA.ap.rearrange("p a (h e) -> p a h e", h=4)
                tp4 = TPB.ap.rearrange("p a (h e) -> p a h e", h=4)
                ws4 = WSB.ap.rearrange("p a (h e) -> p a h e", h=4)
                nlev = 3 if sample else 7
                hq = lambda q: slice(2 * q, 2 * q + 2)

                def kpair(V, base, q):
                    return V.k(base + 512 * q, base + 512 * q + 512) + V.k(base + 1024 + 512 * q, base + 1024 + 512 * q + 512)

                def mask_level(lev, q):
                    sl_ = lev % 2
                    out = AOF2.ap[:, sl_, :].rearrange("p (a h e) -> p a h e", a=2, h=4)[:, :, hq(q), :]
                    S.op('dve', lambda e: e.tensor_tensor(out=out, in0=aa4[:, :, hq(q), :], in1=bc(DCM.ap[:, 2 * lev:2 * lev + 2, :], [128, 2, 2, 128], 2), op=ALU.mult),
                         reads=AA.k() + DCM.k(), writes=kpair(AOF2, sl_ * 2048, q))
                for q in range(2):
                    for a_ in range(2):
                        S.op('dve', lambda e, a_=a_: e.scalar_tensor_tensor(out=tp4[:, a_, hq(q), :], in0=aa4[:, a_, hq(q), :], scalar=-1.0,
                                                                          in1=bc(DCM.ap[:, a_, :], [128, 2, 128], 1), op0=ALU.mult, op1=ALU.mult),
                             reads=AA.k() + DCM.k(), writes=kpair(TPB, 0, q))
                    S.op('dve', lambda e: e.tensor_tensor(out=tp4[:, :, hq(q), :], in0=tp4[:, :, hq(q), :],
                                                          in1=b.cb('ident').unsqueeze(1).unsqueeze(1).to_broadcast([128, 2, 2, 128]), op=ALU.add),
                         reads=kpair(TPB, 0, q), writes=kpair(TPB, 0, q))
                for q in range(2):
                    mask_level(1, q)
                for lev in range(1, nlev):
                    last = (lev == nlev - 1)
                    sl_ = lev % 2
                    aof = AOF2.ap[:, sl_, :]
                    bw = [b.rr(), b.rr()]
                    for q in range(2):
                        for jj in range(2):
                            j = 2 * q + jj
                            js = slice(128 * j, 128 * (j + 1))
                            if not last:
                                S.op('pe', lambda e, j=j, jj=jj, js=js: e.matmul(b.bank(bw[q])[:, 128 * jj:128 * (jj + 1)], lhsT=aof[:, 512 + 128 * j:512 + 128 * (j + 1)],
                                                                                 rhs=TPB.ap[:, 0, js], start=True, stop=True),
                                     reads=kpair(AOF2, sl_ * 2048, q) + kpair(TPB, 0, q), writes=b.pk(bw[q]), inc=False)
                            S.op('pe', lambda e, jj=jj, js=js: e.matmul(b.bank(bw[q])[:, 256 + 128 * jj:256 + 128 * (jj + 1)], lhsT=aof[:, js],
                                                                      rhs=TPB.ap[:, 1, js], start=True, stop=True),
                                 reads=kpair(AOF2, sl_ * 2048, q) + kpair(TPB, 0, q), writes=b.pk(bw[q]), inc=(jj == 1))
                    if not last:
                        for q in range(2):
                            mask_level(lev + 1, q)
                    for q in range(2):
                        if not last:
                            S.op('act', lambda e: e.activation(out=ws4[:, :, hq(q), :], in_=b.bank(bw[q]).rearrange("p (a h e) -> p a h e", a=2, h=2), func=AF.Copy),
                                 reads=[], writes=b.pk(bw[q]) + kpair(WSB, 0, q))
                        else:
                            S.op('act', lambda e: e.activation(out=ws4[:, 1, hq(q), :], in_=b.bank(bw[q])[:, 256:512].rearrange("p (h e) -> p h e", h=2), func=AF.Copy),
                                 reads=[], writes=b.pk(bw[q]) + kpair(WSB, 0, q))
                    bd = [b.rr(), b.rr()]
                    for q in range(2):
                        for jj in range(2):
                            j = 2 * q + jj
                            js = slice(128 * j, 128 * (j + 1))
                            if not last:
                                S.op('pe', lambda e, jj=jj, js=js: e.matmul(b.bank(bd[q])[:, 128 * jj:128 * (jj + 1)], lhsT=TPB.ap[:, 1, js], rhs=WSB.ap[:, 0, js], start=True, stop=True),
                                     reads=kpair(TPB, 0, q) + kpair(WSB, 0, q), writes=b.pk(bd[q]), inc=False)
                            S.op('pe', lambda e, jj=jj, js=js: e.matmul(b.bank(bd[q])[:, 256 + 128 * jj:256 + 128 * (jj + 1)], lhsT=TPB.ap[:, 0, js], rhs=WSB.ap[:, 1, js], start=True, stop=True),
                                 reads=kpair(TPB, 0, q) + kpair(WSB, 0, q), writes=b.pk(bd[q]), inc=(jj == 1))
                    for q in range(2):
                        if not last:
                            S.op('dve', lambda e: e.tensor_tensor(out=tp4[:, :, hq(q), :], in0=tp4[:, :, hq(q), :],
                                                                  in1=b.bank(bd[q]).rearrange("p (a h e) -> p a h e", a=2, h=2), op=ALU.subtract),
                                 reads=kpair(TPB, 0, q), writes=b.pk(bd[q]) + kpair(TPB, 0, q))
                        else:
                            S.op('dve', lambda e: e.tensor_tensor(out=tp4[:, 1, hq(q), :], in0=tp4[:, 1, hq(q), :],
                                                                  in1=b.bank(bd[q])[:, 256:512].rearrange("p (h e) -> p h e", h=2), op=ALU.subtract),
                                 reads=kpair(TPB, 0, q), writes=b.pk(bd[q]) + kpair(TPB, 0, q))
                heads = lambda q: (4 * hf + 2 * q, 4 * hf + 2 * q + 1)
                h2 = lambda ap2: ap2.rearrange("p (h e) -> p h e", h=2)
                tf0 = lambda q: TF.ap[:, 0, 256 * q:256 * q + 256]
                tf1 = lambda q: TF.ap[:, 1, 256 * q:256 * q + 256]
                tf0k = lambda q: TF.k(1024 * q, 1024 * q + 1024)
                tf1k = lambda q: TF.k(2048 + 1024 * q, 2048 + 1024 * q + 1024)
                xrk = lambda q: XRB.k(512 * q, 512 * q + 512)
                ubk = lambda q: UB.k(256 * (4 * hf + 2 * q), 256 * (4 * hf + 2 * q) + 512)
                o32k = lambda q: O32.k(512 * (4 * hf + 2 * q), 512 * (4 * hf + 2 * q) + 1024)
                tpk = lambda q: TPB.k(1024 + 512 * q, 1024 + 512 * q + 512)
                sbk = lambda q: [('SB', 2 * hf + q)]
                s3k = lambda q: [('S32', l, 2 * hf + q)]
                hsq = lambda q: slice(4 * hf + 2 * q, 4 * hf + 2 * q + 2)
                b1, b2_, b3 = [None, None], [None, None], [None, None]
                for q in range(2):
                    if not sample:
                        b1[q] = b.rr()
                        for jj, h in enumerate(heads(q)):
                            S.op('pe', lambda e, h=h, jj=jj: e.matmul(b.bank(b1[q])[:, 128 * jj:128 * (jj + 1)], lhsT=QK.ap[:, 8 + h, tc_], rhs=SBh[:, h, :], start=True, stop=True),
                                 reads=QK.kc(8 + h) + sbk(q), writes=b.pk(b1[q]), inc=False)
                        for jj, h in enumerate(heads(q)):
                            S.op('pe', lambda e, h=h, jj=jj: e.matmul(b.bank(b1[q])[:, 256 + 128 * jj:256 + 128 * (jj + 1)], lhsT=QK.ap[:, h, tc_], rhs=SBh[:, h, :], start=True, stop=True),
                                 reads=QK.kc(h) + sbk(q), writes=b.pk(b1[q]), inc=(jj == 1))
                for q in range(2):
                    if not sample:
                        ks_ap, qs_ap = b.bank(b1[q])[:, 0:256], b.bank(b1[q])[:, 256:512]
                        ksk = qsk = b.pk(b1[q])
                    else:
                        ks_ap = b.PS[:, 2048 + 512 * hf + 256 * q: 2048 + 512 * hf + 256 * (q + 1)]
                        qs_ap = b.PS[:, 3072 + 512 * hf + 256 * q: 3072 + 512 * hf + 256 * (q + 1)]
                        ksk, qsk = b.pk(4 + hf), b.pk(6 + hf)
                    S.op('dve', lambda e: e.tensor_tensor(out=h2(tf0(q)), in0=h2(ks_ap), in1=bc(t8(3)[:, hsq(q)], [128, 2, 128], 2), op=ALU.mult),
                         reads=t8k, writes=ksk + tf0k(q))
                    S.op('dve', lambda e: e.tensor_tensor(out=XRB.ap[:, 2 * q:2 * q + 2, :], in0=VB.ap[:, hsq(q), :], in1=h2(tf0(q)), op=ALU.subtract),
                         reads=VB.k() + tf0k(q), writes=xrk(q))
                    S.op('dve', lambda e: e.tensor_tensor(out=h2(tf1(q)), in0=h2(qs_ap), in1=bc(t8(2)[:, hsq(q)], [128, 2, 128], 2), op=ALU.mult),
                         reads=t8k, writes=qsk + tf1k(q))
                for q in range(2):
                    b2_[q] = b.rr()
                    for jj in range(2):
                        j = 2 * q + jj
                        S.op('pe', lambda e, j=j, jj=jj: e.matmul(b.bank(b2_[q])[:, 128 * jj:128 * (jj + 1)], lhsT=TPB.ap[:, 1, 128 * j:128 * (j + 1)], rhs=XRB.ap[:, j, :], start=True, stop=True),
                             reads=tpk(q) + xrk(q), writes=b.pk(b2_[q]), inc=(jj == 1))
                for q in range(2):
                    S.op('act', lambda e: e.activation(out=UB.ap[:, hsq(q), :], in_=h2(b.bank(b2_[q])[:, 0:256]), func=AF.Copy), reads=[], writes=b.pk(b2_[q]) + ubk(q))
                for q in range(2):
                    for jj, h in enumerate(heads(q)):
                        j = 2 * q + jj
                        S.op('pe', lambda e, h=h, j=j, jj=jj: e.matmul(b.bank(b2_[q])[:, 256 + 128 * jj:256 + 128 * (jj + 1)], lhsT=ATT.ap[:, j, :], rhs=UB.ap[:, h, :], start=True, stop=True),
                             reads=ATT.k() + ubk(q), writes=b.pk(b2_[q]), inc=(jj == 1))
                    if not sample:
                        b3[q] = b.rr()
                        for jj, h in enumerate(heads(q)):
                            S.op('pe', lambda e, h=h, jj=jj: e.matmul(b.bank(b3[q])[:, 128 * jj:128 * (jj + 1)], lhsT=KDK.ap[:, h, :], rhs=UB.ap[:, h, :], start=True, stop=True),
                                 reads=KDK.k() + ubk(q), writes=b.pk(b3[q]), inc=(jj == 1))
                for q in range(2):
                    S.op('dve', lambda e: e.tensor_tensor(out=O32.ap[:, hsq(q), :], in0=h2(b.bank(b2_[q])[:, 256:512]), in1=h2(tf1(q)), op=ALU.add),
                         reads=tf1k(q), writes=b.pk(b2_[q]) + o32k(q))
                    if not sample:
                        s32 = b.S32[:, l, hsq(q), :]
                        S.op('dve', lambda e: e.tensor_tensor(out=s32, in0=s32, in1=bc(t8(6)[:, hsq(q)], [128, 2, 128], 2), op=ALU.mult),
                             reads=s3k(q) + t8k, writes=s3k(q))
                        S.op('dve', lambda e: e.tensor_tensor(out=s32, in0=h2(b.bank(b3[q])[:, 0:256]), in1=s32, op=ALU.add),
                             reads=s3k(q), writes=b.pk(b3[q]) + s3k(q))
                for q in range(2):
                    if not sample:
                        s32 = b.S32[:, l, hsq(q), :]
                        S.op('act', lambda e: e.activation(out=SBh[:, hsq(q), :], in_=s32, func=AF.Copy), reads=s3k(q), writes=sbk(q))
            if sample:
                for s_ in range(NSEQ_S):
                    sl = s_ % 2
                    S.dma('sp', SL32.ap[:, sl, :].rearrange("p (h e) -> p h e", h=8),
                          b.dram['sdelta'][l, s_].rearrange("h d e -> d h e"), writes=SL32.kc(sl))
                    S.op('dve', lambda e: e.tensor_tensor(out=UM.ap, in0=UB.ap, in1=b.cf('seqsel', s_, s_ + 1).unsqueeze(2).to_broadcast([128, 8, 128]), op=ALU.mult),
                         reads=UB.k(), writes=UM.k())
                    for h in range(8):
                        S.op('pe', lambda e, h=h: e.matmul(b.PS[:, 2048 + 128 * h:2048 + 128 * (h + 1)], lhsT=KDK.ap[:, h, :], rhs=UM.ap[:, h, :], start=True, stop=True),
                             reads=KDK.k() + UM.k(), writes=b.pk(4, 2), inc=(h == 7))
                    sv = SL32.ap[:, sl, :].rearrange("p (h e) -> p h e", h=8)
                    S.op('dve', lambda e: e.tensor_tensor(out=sv, in0=sv, in1=bc(GES.ap[:, s_, :], [128, 8, 128], 2), op=ALU.mult),
                         reads=SL32.kc(sl) + GES.k(), writes=SL32.kc(sl))
                    S.op('dve', lambda e: e.tensor_tensor(out=SL32.ap[:, sl, :], in0=b.PS[:, 2048:3072], in1=SL32.ap[:, sl, :], op=ALU.add),
                         reads=SL32.kc(sl), writes=b.pk(4, 2) + SL32.kc(sl))
                    S.dma('sp', b.dram['deltas'][l, s_].rearrange("h d e -> d h e"), sv, reads=SL32.kc(sl))
            S.op('act', lambda e: e.activation(out=TF.ap.rearrange("p a c -> p (a c)"), in_=O32.ap.rearrange("p h e -> p (h e)"), func=AF.Square),
                 reads=O32.k(), writes=TF.k())
            S.op('dve', lambda e: e.tensor_reduce(out=t8(7), in_=TF.ap.rearrange("p a (h e) -> p (a h) e", e=128), axis=AX.X, op=ALU.add),
                 reads=TF.k(), writes=t8k)
            S.op('act', lambda e: e.activation(out=t8(8), in_=t8(7), func=AF.Ln, bias=b.epsc[:, 0:1], scale=1.0 / 128), reads=t8k, writes=t8k)
            S.op('act', lambda e: e.activation(out=t8(8), in_=t8(8), func=AF.Exp, scale=-0.5), reads=t8k, writes=t8k)
            S.op('dve', lambda e: e.tensor_tensor(out=ON.ap, in0=O32.ap, in1=bc(t8(8), [128, 8, 128], 2), op=ALU.mult),
                 reads=O32.k() + t8k, writes=ON.k())
            bi = b.rr()
            for h in range(8):
                S.op('pe', lambda e, h=h: e.transpose(out=b.bankb(bi)[:, 128 * h:128 * (h + 1)], in_=ON.ap[:, h, :], identity=identb),
                     reads=ON.k(), writes=b.pk(bi), inc=(h == 7))
            S.op('act', lambda e: e.activation(out=OT.ap[:, :, tc_], in_=b.bankb(bi).rearrange("p (h e) -> p h e", h=8), func=AF.Copy,
                                               scale=b.OGAIN[:, l:l + 1]),
                 reads=[], writes=b.pk(bi) + OT.k())
        if (not sample) and g == 3:
            S.dma('sp', b.dram['deltap'][l].rearrange("h d e -> d h e"), b.S32[:, l], reads=[('S32', l, i) for i in range(4)])

        b.aoff = mark
        GT = b.aalloc([128, 2, n], BF16)
        it = 0
        for blk in range(4):
            w, wk = b.wnext(('a_w_in', l, 0, D, 3072 + 256 * blk, 256))
            for j in range(2):
                h = 2 * blk + j
                bi = b.rr()
                b.fm_matmul(b.bank(bi)[:, 0:n], w, wk, 128 * j, 128, lambda kc: b.HT[:, kc, 0:n], b.HTK, 8, b.pk(bi))
                tb = it % 2
                it += 1
                S.op('act', lambda e: e.activation(out=GT.ap[:, tb, 0:n], in_=b.bank(bi)[:, 0:n], func=AF.Silu), reads=[], writes=b.pk(bi) + GT.kc(tb))
                S.op('dve', lambda e: e.tensor_tensor(out=OT.ap[:, h, 0:n], in0=OT.ap[:, h, 0:n], in1=GT.ap[:, tb, 0:n], op=ALU.mult),
                     reads=GT.kc(tb) + OT.k(), writes=OT.k())
        for blk in range(4):
            w, wk = b.wnext(('a_w_out', l, 0, D, 256 * blk, 256))
            for j in range(2):
                dc = 2 * blk + j
                bi = b.rr()
                b.fm_matmul(b.bank(bi)[:, 0:n], w, wk, 128 * j, 128, lambda kc: OT.ap[:, kc, 0:n], OT.k(), 8, b.pk(bi))
                if dc % 2 == 0:
                    S.op('act', lambda e: e.activation(out=b.YT[:, dc, 0:n], in_=b.bank(bi)[:, 0:n], func=AF.Copy), reads=[], writes=b.pk(bi) + ['YT'])
                else:
                    S.op('dve', lambda e: e.tensor_copy(out=b.YT[:, dc, 0:n], in_=b.bank(bi)[:, 0:n]), reads=[], writes=b.pk(bi) + ['YT'])
        b.abegin()
        b.postnorm_add(b.ncol(l, 1), ntok)

    def _conv_out(b, src, srck, rows, dst):
        S = b.S
        CVO = b.aalloc([rows, QKV], F32)
        for r in range(6):
            bi = b.rr()
            for c4 in range(4):
                c = 4 * r + c4
                S.op('pe', lambda e, c=c, c4=c4: e.transpose(out=b.bank(bi)[0:rows, 128 * c4:128 * (c4 + 1)], in_=src[:, c, :], identity=b.cf('ident')),
                     reads=srck, writes=b.pk(bi), inc=(c4 == 3))
            S.op('act', lambda e: e.activation(out=CVO.ap[:, 512 * r:512 * (r + 1)], in_=b.bank(bi)[0:rows, :], func=AF.Copy),
                 reads=[], writes=b.pk(bi) + CVO.k())
        S.dma('sp', dst, CVO.ap, reads=CVO.k())

    def kvproj(b, g, ntok):
        S = b.S
        b.nrr = 6
        b.psrr = 0
        sample = (g == 4)
        n = ntok
        b.abegin()
        HK = b.aalloc([128, 8, n], BF16)
        hk = HK.k()
        b.rmsnorm_fm(b.XT, 'XT', b.KVNC, HK.ap, (lambda c: HK.kc(c)), ntok)
        KVO = b.aalloc([128, 2, 512], F32)
        VST = b.aalloc([32, 16, 256], BF16)
        wa, ka = b.wnext(('b_w_kv', None, 0, D, 0, 256))
        wb, kb = b.wnext(('b_w_kv', None, 0, D, 256, 256))
        ktkey = 'KTS' if sample else 'KT2'
        for kv in range(4):
            bi = b.rr()
            for half in range(2):
                for kc in range(8):
                    S.op('pe', lambda e, kc=kc, half=half: e.matmul(b.bank(bi)[64 * half:64 * half + 64, 0:n], lhsT=wa[:, kc, 64 * kv:64 * kv + 64],
                                                                     rhs=HK.ap[:, kc, :], start=(kc == 0), stop=(kc == 7)),
                         reads=hk + [ka], writes=b.pk(bi), inc=(kc == 7 and half == 1))
            dst = b.KTS[:, kv, :] if sample else b.KT2[:, kv, 512 * g:512 * g + n]
            S.op('act', lambda e: e.activation(out=dst, in_=b.bank(bi)[:, 0:n], func=AF.Copy), reads=[], writes=b.pk(bi) + [ktkey])
        for t in range(n // 128):
            it = t % 2
            bk, bv = b.rr(), b.rr()
            for (bb_, w_, k_) in ((bk, wa, ka), (bv, wb, kb)):
                for kc in range(8):
                    S.op('pe', lambda e, kc=kc, bb_=bb_, w_=w_: e.matmul(b.bank(bb_)[:, 0:256], lhsT=HK.ap[:, kc, 128 * t:128 * (t + 1)], rhs=w_[:, kc, :],
                                                                         start=(kc == 0), stop=(kc == 7)),
                         reads=hk + [k_], writes=b.pk(bb_), inc=(kc == 7))
            S.op('act', lambda e: e.activation(out=KVO.ap[:, it, 0:256], in_=b.bank(bk)[:, 0:256], func=AF.Copy), reads=[], writes=b.pk(bk) + KVO.kc(it))
            S.op('act', lambda e: e.activation(out=KVO.ap[:, it, 256:512], in_=b.bank(bv)[:, 0:256], func=AF.Copy), reads=[], writes=b.pk(bv) + KVO.kc(it))
            if not sample:
                S.op('act', lambda e: e.activation(out=b.VD[:, 0, 4 * g + t, :], in_=KVO.ap[:, it, 256:512], func=AF.Copy), reads=KVO.kc(it), writes=['VD'])
                r0 = 512 * g + 128 * t
                S.dma('sp', b.dram['ckp'][r0:r0 + 128, :], KVO.ap[:, it, 0:256], reads=KVO.kc(it))
                S.dma('sp', b.dram['cvp'][r0:r0 + 128, :], KVO.ap[:, it, 256:512], reads=KVO.kc(it))
            else:
                S.op('act', lambda e: e.activation(out=b.VS[:, :], in_=KVO.ap[:, it, 256:512], func=AF.Copy), reads=KVO.kc(it), writes=['VS'])
                S.dma('sp', b.dram['cks'], KVO.ap[:, it, 0:256], reads=KVO.kc(it))
                S.dma('sp', b.dram['cvs'], KVO.ap[:, it, 256:512], reads=KVO.kc(it))
        if sample:
            return
        for r in range(4):
            bi = b.rr()
            for kc in range(8):
                S.op('pe', lambda e, kc=kc: e.matmul(b.bank(bi)[:, 0:256], lhsT=HK.ap[:, kc, r:512:4], rhs=wb[:, kc, :], start=(kc == 0), stop=(kc == 7)),
                     reads=hk + [kb], writes=b.pk(bi), inc=(kc == 7))
            S.op('dve', lambda e: e.tensor_copy(out=b.VD[:, 1, 4 * r + g, :], in_=b.bank(bi)[:, 0:256]), reads=[], writes=b.pk(bi) + ['VD'])
        for r2 in range(8):
            bi = b.rr()
            for rr_ in range(2):
                r = 2 * r2 + rr_
                for kc in range(8):
                    S.op('pe', lambda e, kc=kc, r=r, rr_=rr_: e.matmul(b.bank(bi)[0:32, 256 * rr_:256 * rr_ + 256], lhsT=HK.ap[:, kc, r:512:16], rhs=wb[:, kc, :],
                                                                     start=(kc == 0), stop=(kc == 7)),
                         reads=hk + [kb], writes=b.pk(bi), inc=(kc == 7 and rr_ == 1))
            S.op('dve', lambda e: e.tensor_copy(out=VST.ap[:, 2 * r2:2 * r2 + 2, :], in_=b.bank(bi)[0:32, :].rearrange("p (r c) -> p r c", r=2)),
                 reads=[], writes=b.pk(bi) + VST.k())
        S.dma('sp', b.VD[32 * g:32 * g + 32, 2, :, :], VST.ap, reads=VST.k(), writes=['VD'])

    def attn(b, l, g, ntok):
        if g == 4:
            return b.attn_sample(l)
        S = b.S
        n = 512
        j = l - 2
        sl_ = alibi_slopes()
        dil = (1, 4, 16)
        b.abegin()
        b.rmsnorm_fm(b.XT, 'XT', b.ncol(l, 0), b.HT, 'HT', ntok)
        OT = b.aalloc([128, 8, n], BF16)
        QT = b.aalloc([128, 6, n], BF16)
        ET = b.aalloc([128, 6, 256], BF16)
        EX = b.aalloc([128, 2, 512], BF16)
        PT = b.aalloc([128, 2, 512], BF16)
        RD = b.aalloc([128, 512], F32)
        NUM, DEN = 6, 7
        b.nrr = 6
        b.psrr = 0
        it = 0
        for pr in range(8):
            kvh = pr // 2
            qb = pr % 2
            for grp in range(3):
                w, wk = b.wnext(('b_w_q', j, 0, D, 1024 * grp + 128 * pr, 128))
                bi = b.rr()
                b.fm_matmul(b.bank(bi)[:, 0:n], w, wk, 0, 128, lambda kc: b.HT[:, kc, 0:n], b.HTK, 8, b.pk(bi))
                S.op('act', lambda e: e.activation(out=QT.ap[:, qb * 3 + grp, :], in_=b.bank(bi)[:, 0:n], func=AF.Copy, scale=0.125),
                     reads=[], writes=b.pk(bi) + QT.kc(qb * 3 + grp))
            for hh in range(2):
                for grp in range(3):
                    c = float(sl_[grp, 2 * pr + hh]) * dil[grp]
                    ei = hh * 3 + grp
                    S.op('act', lambda e: e.activation(out=ET.ap[:, ei, :], in_=b.cf('deltac'), func=AF.Exp, scale=-c), reads=[], writes=ET.kc(ei))
                    S.op('dve', lambda e: e.tensor_tensor(out=ET.ap[:, ei, :], in0=ET.ap[:, ei, :], in1=b.cb('maskj'), op=ALU.mult),
                         reads=ET.kc(ei), writes=ET.kc(ei))
            combos = [(0, 1), (0, 0), (1, 1), (1, 0), (2, 1)]
            if g == 0:
                combos = [(0, 1), (0, 0), (1, 1), (2, 1)]
            items = [(hh, grp, own) for hh in range(2) for (grp, own) in combos]
            first = {0: True, 1: True}
            firstd = {0: True, 1: True}

            def stA(i):
                hh, grp, own = items[i]
                P = slice(64 * hh, 64 * hh + 64)
                nsub, qn = (4, 128) if grp < 2 else (16, 32)
                bi = b.rr()
                qi = qb * 3 + grp
                for sub in range(nsub):
                    if grp == 0:
                        qcols = slice(128 * sub, 128 * sub + 128)
                        k0 = max(512 * g + 128 * sub - (0 if own else 128), 0)
                        kcols = slice(k0, k0 + 128)
                    elif grp == 1:
                        qcols = slice(sub, 512, 4)
                        gg = g if own else g - 1
                        kcols = slice(512 * gg + sub, 512 * gg + 512, 4)
                    else:
                        qcols = slice(sub, 512, 16)
                        kcols = slice(sub, SEQ, 16)
                    S.op('pe', lambda e, sub=sub, qcols=qcols, kcols=kcols: e.matmul(b.bank(bi)[:, qn * sub:qn * (sub + 1)], lhsT=b.KT2[P, kvh, kcols],
                                                                                       rhs=QT.ap[P, qi, qcols], start=True, stop=True),
                         reads=['KT2'] + QT.kc(qi), writes=b.pk(bi), inc=(sub == nsub - 1))
                xi = i % 2
                S.op('act', lambda e: e.activation(out=EX.ap[:, xi, :], in_=b.bank(bi), func=AF.Exp), reads=[], writes=b.pk(bi) + EX.kc(xi))
                ei = hh * 3 + grp
                if grp < 2:
                    tab = bc(ET.ap[:, ei, (0 if own else 128):(128 if own else 256)], [128, 4, 128], 1)
                else:
                    tab = bc(ET.ap[:, ei, 32 * g:32 * g + 32], [128, 16, 32], 1)
                S.op('dve', lambda e: e.tensor_tensor(out=PT.ap[:, xi, :].rearrange("p (s q) -> p s q", s=nsub),
                                                      in0=EX.ap[:, xi, :].rearrange("p (s q) -> p s q", s=nsub), in1=tab, op=ALU.mult),
                     reads=EX.kc(xi) + ET.kc(ei), writes=PT.kc(xi))
                if g == 0 and grp == 0 and not own:
                    S.op('pool', lambda e: e.memset(PT.ap[:, xi, 0:128], 0.0), reads=[], writes=PT.kc(xi))

            def stB(i):
                hh, grp, own = items[i]
                P = slice(64 * hh, 64 * hh + 64)
                nsub, qn = (4, 128) if grp < 2 else (16, 32)
                xi = i % 2
                for sub in range(nsub):
                    if grp == 0:
                        vt = max(4 * g + sub - (0 if own else 1), 0)
                        ocols = slice(128 * sub, 128 * sub + 128)
                    elif grp == 1:
                        vt = 4 * sub + (g if own else g - 1)
                        ocols = slice(sub, 512, 4)
                    else:
                        vt = sub
                        ocols = slice(sub, 512, 16)
                    st = first[hh]
                    first[hh] = False
                    rhs = PT.ap[:, xi, qn * sub:qn * (sub + 1)]
                    S.op('pe', lambda e, vt=vt, ocols=ocols, st=st, rhs=rhs: e.matmul(b.bank(NUM)[P, ocols], lhsT=b.VD[:, grp, vt, 64 * kvh:64 * kvh + 64], rhs=rhs,
                                                                                       start=st, stop=False, skip_group_check=True),
                         reads=['VD'] + PT.kc(xi), writes=b.pk(NUM), inc=False)
                std = firstd[hh]
                firstd[hh] = False
                if grp == 0:
                    oap = b.bank(DEN)[P, 0:512].rearrange("p (s q) -> p s q", s=4)
                else:
                    oap = b.bank(DEN)[P, 0:512].rearrange("p (i r) -> p r i", r=nsub)
                S.op('pe', lambda e: e.matmul(oap, lhsT=b.cb('ones', 0, 64), rhs=PT.ap[:, xi, :].rearrange("p (s q) -> p s q", s=nsub),
                                              start=std, stop=False, skip_group_check=True),
                     reads=PT.kc(xi), writes=b.pk(DEN), inc=True)

            for i in range(len(items) + 1):
                if i < len(items):
                    stA(i)
                if i >= 1:
                    stB(i - 1)
            S.op('dve', lambda e: e.reciprocal(out=RD.ap, in_=b.bank(DEN)), reads=[], writes=b.pk(DEN) + RD.k())
            S.op('dve', lambda e: e.tensor_tensor(out=OT.ap[:, pr, :], in0=b.bank(NUM), in1=RD.ap, op=ALU.mult), reads=RD.k(), writes=b.pk(NUM) + OT.kc(pr))
        b._wo(j, OT, n)
        b.abegin()
        b.postnorm_add(b.ncol(l, 1), ntok)

    def _wo(b, j, OT, n):
        S = b.S
        for blk in range(4):
            w, wk = b.wnext(('b_w_o', j, 0, D, 256 * blk, 256))
            for jj in range(2):
                dc = 2 * blk + jj
                bi = b.rr()
                b.fm_matmul(b.bank(bi)[:, 0:n], w, wk, 128 * jj, 128, lambda kc: OT.ap[:, kc, 0:n], OT.k(), 8, b.pk(bi))
                if dc % 2 == 0:
                    S.op('act', lambda e: e.activation(out=b.YT[:, dc, 0:n], in_=b.bank(bi)[:, 0:n], func=AF.Copy), reads=[], writes=b.pk(bi) + ['YT'])
                else:
                    S.op('dve', lambda e: e.tensor_copy(out=b.YT[:, dc, 0:n], in_=b.bank(bi)[:, 0:n]), reads=[], writes=b.pk(bi) + ['YT'])

    def attn_sample(b, l):
        S = b.S
        j = l - 2
        n = 128
        b.abegin()
        b.rmsnorm_fm(b.XT, 'XT', b.ncol(l, 0), b.HT, 'HT', n)
        QS = b.aalloc([64, 48, n], BF16)
        OT = b.aalloc([128, 8, n], BF16)
        ES = b.aalloc([128, NKT, 384], BF16)
        KST = b.aalloc([128, 2, 12 * 256], BF16)
        VBF = b.aalloc([128, 2, 13 * 256], BF16)
        KTT = b.aalloc([64, 2, 512], BF16)
        KTL = b.aalloc([64, 4, 128], BF16)
        EX = b.aalloc([128, 2, 384], BF16)
        PT = b.aalloc([128, 2, 384], BF16)
        PJ = b.aalloc([128, 2, 128], BF16)
        O2 = b.aalloc([128, 2, 64], BF16)
        RD = b.aalloc([128, 1], F32)
        S.dma('sp', ES.ap.rearrange("p t c -> p (t c)"), b.dram['esamp'], writes=ES.k())
        S.op('pool', lambda e: e.memset(VBF.ap.rearrange("p a c -> p (a c)"), 0.0), writes=VBF.k())
        S.op('pool', lambda e: e.memset(KTL.ap.rearrange("p a c -> p (a c)"), 0.0), writes=KTL.k())
        for pr in range(8):
            for grp in range(3):
                w, wk = b.wnext(('b_w_q', j, 0, D, 1024 * grp + 128 * pr, 128))
                for hh in range(2):
                    bi = b.rr()
                    for kc in range(8):
                        S.op('pe', lambda e, kc=kc: e.matmul(b.bank(bi)[0:64, 0:n], lhsT=w[:, kc, 64 * hh:64 * hh + 64], rhs=b.HT[:, kc, 0:n],
                                                             start=(kc == 0), stop=(kc == 7)),
                             reads=b.HTK + [wk], writes=b.pk(bi), inc=(kc == 7))
                    qi = grp * 16 + 2 * pr + hh
                    S.op('act', lambda e: e.activation(out=QS.ap[:, qi, :], in_=b.bank(bi)[0:64, 0:n], func=AF.Copy, scale=0.125),
                         reads=[], writes=b.pk(bi) + QS.kc(qi))
        ACC = 7
        b.nrr = 6
        b.psrr = 0
        it = 0
        qv = QS.ap.rearrange("p (g c) t -> p g c t", g=3)
        for s_ in range(NSEQ_S):
            sl = s_ % 2
            for (nm, dst) in (('ck', KST), ('cv', VBF)):
                src = b.dram[nm][s_]
                cs = bass.AP(src.tensor, src.offset, [[4096, 96], [256, 8], [1, 256]])
                S.dma('pool', dst.ap[0:96, sl, 0:NCT * 256].rearrange("p (t c) -> p t c", t=NCT), cs, writes=dst.kc(sl))
                S.dma('pool', dst.ap[:, sl, NCT * 256:12 * 256].rearrange("p (t c) -> p t c", t=4), src[1536:2048, :].rearrange("(t p) c -> p t c", p=128), writes=dst.kc(sl))
            S.dma('sp', VBF.ap[0:8, sl, 12 * 256:13 * 256], b.VS[8 * s_:8 * s_ + 8, :], reads=['VS'], writes=VBF.kc(sl))
            S.op('act', lambda e: e.activation(out=KTL.ap[:, :, 0:8], in_=b.KTS[0:64, :, 8 * s_:8 * s_ + 8], func=AF.Copy), reads=['KTS'], writes=KTL.k())
            def stA(t):
                if t >= NKT - 1:
                    return
                x2 = t % 2
                npk = 96 if t < NCT else 128
                bt = b.rr()
                for kv in range(4):
                    S.op('pe', lambda e, kv=kv: e.transpose(out=b.bankb(bt)[0:64, 128 * kv:128 * kv + npk],
                                                            in_=KST.ap[0:npk, sl, 256 * t + 64 * kv:256 * t + 64 * kv + 64], identity=b.cb('ident')[0:npk, 0:npk]),
                         reads=KST.kc(sl), writes=b.pk(bt), inc=(kv == 3))
                S.op('act', lambda e: e.activation(out=KTT.ap[:, x2, :], in_=b.bankb(bt)[0:64, 0:512], func=AF.Copy), reads=[], writes=b.pk(bt) + KTT.kc(x2))

            def stB(t):
                x2 = t % 2
                npk = 96 if t < NCT else 128
                if t < NKT - 1:
                    ktap = lambda kv: KTT.ap[:, x2, 128 * kv:128 * kv + npk]
                    ktk = KTT.kc(x2)
                else:
                    ktap = lambda kv: KTL.ap[:, kv, :]
                    ktk = KTL.k()
                bi = b.rr()
                for kv in range(4):
                    S.op('pe', lambda e, kv=kv: e.matmul(b.bank(bi)[0:npk, 96 * kv:96 * kv + 96].rearrange("p (g r l) -> p g r l", g=3, r=4),
                                                       lhsT=ktap(kv), rhs=qv[:, :, 4 * kv:4 * kv + 4, 8 * s_:8 * s_ + 8], start=True, stop=True),
                         reads=ktk + QS.k(), writes=b.pk(bi), inc=(kv == 3))
                xi = t % 2
                S.op('act', lambda e: e.activation(out=EX.ap[0:npk, xi, :], in_=b.bank(bi)[0:npk, 0:384], func=AF.Exp), reads=[], writes=b.pk(bi) + EX.kc(xi))
                S.op('dve', lambda e: e.tensor_tensor(out=PT.ap[0:npk, xi, :], in0=EX.ap[0:npk, xi, :], in1=ES.ap[0:npk, t, :], op=ALU.mult),
                     reads=EX.kc(xi) + ES.k(), writes=PT.kc(xi))
                p4 = PT.ap[0:npk, xi, :].rearrange("p (k g c) -> p k g c", k=4, g=3)
                pj = PJ.ap[0:npk, xi, :].rearrange("p (k c) -> p k c", k=4)
                S.op('dve', lambda e: e.tensor_tensor(out=pj, in0=p4[:, :, 0, :], in1=p4[:, :, 1, :], op=ALU.add), reads=PT.kc(xi), writes=PJ.kc(xi))
                S.op('dve', lambda e: e.tensor_tensor(out=pj, in0=pj, in1=p4[:, :, 2, :], op=ALU.add), reads=PT.kc(xi) + PJ.kc(xi), writes=PJ.kc(xi))

            def stC(t):
                xi = t % 2
                npk = 96 if t < NCT else 128
                for kv in range(4):
                    lw = PJ.ap[0:npk, xi, 32 * kv:32 * kv + 32]
                    S.op('pe', lambda e, kv=kv, lw=lw: e.matmul(b.bank(ACC)[32 * kv:32 * kv + 32, 0:64], lhsT=lw, rhs=VBF.ap[0:npk, sl, 256 * t + 64 * kv:256 * t + 64 * kv + 64],
                                                              start=(t == 0), stop=False, skip_group_check=True, tile_position=(0, 32 * kv)),
                         reads=PJ.kc(xi) + VBF.kc(sl), writes=b.pk(ACC), inc=False)
                S.op('pe', lambda e: e.matmul(b.bank(ACC)[:, 64:65], lhsT=PJ.ap[0:npk, xi, :], rhs=b.cb('ones', 0, 1)[0:npk, :],
                                              start=False, stop=False, skip_group_check=True),
                     reads=PJ.kc(xi), writes=b.pk(ACC), inc=True)

            for step in range(NKT + 2):
                if step < NKT:
                    stA(step)
                if 0 <= step - 1 < NKT:
                    stB(step - 1)
                if 0 <= step - 2 < NKT:
                    stC(step - 2)
            S.op('dve', lambda e: e.reciprocal(out=RD.ap, in_=b.bank(ACC)[:, 64:65]), reads=[], writes=b.pk(ACC) + RD.k())
            S.op('dve', lambda e: e.tensor_scalar(out=O2.ap, in0=bc(b.bank(ACC)[:, 0:64], [128, 2, 64], 1), scalar1=RD.ap[:, 0:1], scalar2=None, op0=ALU.mult),
                 reads=RD.k(), writes=b.pk(ACC) + O2.k())
            bi = b.rr()
            S.op('pe', lambda e: e.transpose(out=b.bankb(bi)[:, 0:128], in_=O2.ap.rearrange("p a c -> p (a c)"), identity=b.cb('ident')),
                 reads=O2.k(), writes=b.pk(bi))
            for hh in range(2):
                P = slice(64 * hh, 64 * hh + 64)
                src = b.bankb(bi)[P, 0:128].rearrange("p (k r h l) -> p k r h l", k=4, h=2, r=2)[:, :, :, hh, :]
                dst = OT.ap[P, :, 8 * s_:8 * s_ + 8].rearrange("p (k r) l -> p k r l", k=4)
                S.op('act', lambda e, src=src, dst=dst: e.activation(out=dst, in_=src, func=AF.Copy), reads=[], writes=b.pk(bi) + OT.k())
        b._wo(j, OT, n)
        b.abegin()
        b.postnorm_add(b.ncol(l, 1), n)

    def load_group(b, g, ntok):
        S = b.S
        b.abegin()
        XIN = b.aalloc([128, 2, D], F32)
        src = b.dram['xp'] if g < 4 else b.dram['xs']
        for t in range(ntok // 128):
            r0 = (512 * g + 128 * t) if g < 4 else 0
            xi = t % 2
            S.dma('sp', XIN.ap[:, xi, :], src[r0:r0 + 128, :], writes=XIN.kc(xi))
            bi = b.rr(2)
            for c in range(8):
                S.op('pe', lambda e, c=c: e.transpose(out=b.bank(bi, 2)[:, 128 * c:128 * (c + 1)],
                                                      in_=XIN.ap[:, xi, 128 * c:128 * (c + 1)], identity=b.cf('ident')),
                     reads=XIN.kc(xi), writes=b.pk(bi, 2), inc=(c == 7))
            S.op('act', lambda e: e.activation(out=b.XT[:, :, 128 * t:128 * (t + 1)],
                                               in_=b.bank(bi, 2).rearrange("p (c n) -> p c n", c=8), func=AF.Copy),
                 reads=[], writes=b.pk(bi, 2) + ['XT'])

    def store_group(b, g, ntok):
        S = b.S
        b.abegin()
        XIN = b.aalloc([128, 2, D], F32)
        dst = b.dram['yp'] if g < 4 else b.dram['ys']
        for t in range(ntok // 128):
            r0 = (512 * g + 128 * t) if g < 4 else 0
            xi = t % 2
            bi = b.rr(2)
            for c in range(8):
                S.op('pe', lambda e, c=c: e.transpose(out=b.bank(bi, 2)[:, 128 * c:128 * (c + 1)],
                                                      in_=b.XT[:, c, 128 * t:128 * (t + 1)], identity=b.cf('ident')),
                     reads=['XT'], writes=b.pk(bi, 2), inc=(c == 7))
            S.op('act', lambda e: e.activation(out=XIN.ap[:, xi, :], in_=b.bank(bi, 2), func=AF.Copy),
                 reads=[], writes=b.pk(bi, 2) + XIN.kc(xi))
            S.dma('sp', dst[r0:r0 + 128, :], XIN.ap[:, xi, :], reads=XIN.kc(xi))

    def build(b):
        nc, S = b.nc, b.S
        for nm, shp in IN_SPECS:
            b.din(nm, shp, BF16 if nm in ('cb', 'esamp', 'dcm') else F32)
        for nm, shp in OUT_SPECS:
            b.dout(nm, shp)
        b.PS = b.es.enter_context(nc.psum_tensor("PS", [128, 4096], F32))
        if not b.dry:
            b.nblk = len(b.wplan) // 5
            b.WSCR = nc.dram_tensor("wscr", [b.nblk, 128, WSLOT], BF16, kind="Internal").ap()
        b.CFT = b.sb('CFT', [128, CF_W], F32)
        b.CBT = b.sb('CBT', [128, CB_W], BF16)
        b.NORMC = b.sb('NORMC', [128, 128], F32)
        b.KVNC = b.sb('KVNC', [128, 8], F32)
        b.CC = b.sb('CC', [128, 4], F32)
        b.AEXPN = b.sb('AEXPN', [128, 16], F32)
        b.DTB = b.sb('DTB', [128, 16], F32)
        b.OGAIN = b.sb('OGAIN', [128, 2], F32)
        b.CONVW = b.sb('CONVW', [128, 2 * 24 * 4], F32)
        b.WB = b.sb('WB', [128, NWS * WSLOT], BF16)
        b.XT = b.sb('XT', [128, 8, 512], F32)
        b.YT = b.sb('YT', [128, 8, 512], F32)
        b.HT = b.sb('HT', [128, 8, 512], BF16)
        b.S32 = b.sb('S32', [128, 2, 8, 128], F32)
        b.SB = b.sb('SB', [128, 8, 128], BF16)
        b.HIST = b.sb('HIST', [128, 2, 24, 3], F32)
        b.KT2 = b.sb('KT2', [128, 4, SEQ], BF16)
        b.VD = b.sb('VD', [128, 3, 16, 256], BF16)
        b.KTS = b.sb('KTS', [128, 4, 128], BF16)
        b.VS = b.sb('VS', [128, 256], BF16)
        b.AR = b.sb('AR', [128, ARBYTES // 2], BF16)
        b.epsc, b.onec, b.zeroc, b.lnqs = b.CC[:, 0:1], b.CC[:, 1:2], b.CC[:, 2:3], b.CC[:, 3:4]
        S.dma('sp', b.CFT[:], b.dram['cf'], writes=['c0'])
        S.dma('sp', b.CBT[:], b.dram['cb'], writes=['c1'])
        S.dma('sp', b.NORMC[:], b.dram['norms_c'], writes=['c2'])
        S.dma('sp', b.KVNC[:], b.dram['kvn_c'], writes=['c3'])
        S.dma('sp', b.AEXPN[:], b.dram['a_log'].partition_broadcast(128), writes=['c4'])
        S.dma('sp', b.DTB[:], b.dram['a_dt_bias'].partition_broadcast(128), writes=['c5'])
        S.dma('sp', b.OGAIN[:], b.dram['ogain_c'], writes=['c6'])
        S.dma('sp', b.CONVW[:], b.dram['convw_c'], writes=['c7'])
        S.op('pool', lambda e: e.memset(b.CC[:, 0:1], EPS), writes=['c8'])
        S.op('pool', lambda e: e.memset(b.CC[:, 1:2], 1.0), writes=['c8'])
        S.op('pool', lambda e: e.memset(b.CC[:, 2:3], 0.0), writes=['c8'])
        S.op('pool', lambda e: e.memset(b.CC[:, 3:4], -0.5 * math.log(128.0)), writes=['c8'])
        S.op('act', lambda e: e.activation(out=b.AEXPN[:], in_=b.AEXPN[:], func=AF.Exp), reads=['c4'], writes=['c4'])
        S.op('act', lambda e: e.mul(out=b.AEXPN[:], in_=b.AEXPN[:], mul=-1.0), reads=['c4'], writes=['c4'])
        S.op('pool', lambda e: e.memset(b.S32[:], 0.0), writes=[('S32', l_, i) for l_ in range(2) for i in range(4)])
        S.op('pool', lambda e: e.memset(b.SB[:], 0.0), writes=[('SB', i) for i in range(4)])
        S.op('pool', lambda e: e.memset(b.HIST[:], 0.0), writes=['HIST'])
        S.op('pool', lambda e: e.memset(b.KT2[:], 0.0), writes=['KT2'])
        S.op('pool', lambda e: e.memset(b.VD[:], 0.0), writes=['VD'])
        ck = ['c%d' % i for i in range(9)]
        for e in ('pe', 'act', 'dve', 'pool'):
            S._sync(e, ck, [], False)
        st = b.stage
        for g in range(5):
            ntok = 512 if g < 4 else 128
            if g < 4:
                pass
            b.load_group(g, ntok)
            for l in range(4):
                if st < 10 * l + 1:
                    break
                if l < 2:
                    if g < 4:
                        S.op('act', lambda e: e.activation(out=b.SB[:], in_=b.S32[:, l], func=AF.Copy), reads=[('S32', l, i) for i in range(4)], writes=[('SB', i) for i in range(4)])
                    b.gdn(l, g, ntok)
                else:
                    b.attn(l, g, ntok)
                if st < 10 * l + 2:
                    break
                b.ffn(l, ntok)
                if l == 1 and st >= 13:
                    b.kvproj(g, ntok)
            b.store_group(g, ntok)
        S.finish()


IN_SPECS = [
    ('xp', [SEQ, D]), ('xs', [128, D]), ('sconv', [2, NSEQ_S, 3, QKV]), ('sdelta', [2, NSEQ_S, 8, 128, 128]),
    ('ck', [NSEQ_S, SEQ, 256]), ('cv', [NSEQ_S, SEQ, 256]),
    ('norms_c', [128, 128]), ('kvn_c', [128, 8]), ('a_log', [16]), ('a_dt_bias', [16]), ('ogain_c', [128, 2]),
    ('convw_c', [128, 192]), ('cf', [128, 0]), ('cb', [128, 0]), ('esamp', [128, NKT * 384]), ('dcm', [128, 14 * 128]),
    ('a_w_in', [2, D, APROJ]), ('a_w_out', [2, D, D]), ('b_w_kv', [D, 512]), ('b_w_q', [2, D, 3072]), ('b_w_o', [2, D, D]),
    ('ffn_w_in', [4, D, 2 * DFF]), ('ffn_w_out', [4, DFF, D]),
]
OUT_SPECS = [
    ('yp', [SEQ, D]), ('ys', [128, D]), ('convp', [2, 3, QKV]), ('deltap', [2, 8, 128, 128]),
    ('ckp', [SEQ, 256]), ('cvp', [SEQ, 256]), ('convs', [2, NSEQ_S, 3, QKV]), ('deltas', [2, NSEQ_S, 8, 128, 128]),
    ('cks', [128, 256]), ('cvs', [128, 256]),
]

CF_ARR, CB_ARR, ESAMP_ARR = _build_consts()
CF_W, CB_W = CF_ARR.shape[1], CB_ARR.shape[1]
for _i, (_n, _s) in enumerate(IN_SPECS):
    if _n == 'cf':
        IN_SPECS[_i] = ('cf', [128, CF_W])
    if _n == 'cb':
        IN_SPECS[_i] = ('cb', [128, CB_W])


def build_program(stage=99):
    wplan = []
    nc0 = bass.Bass("TRN2", target_bir_lowering=False)
    with ExitStack() as es0:
        Builder(nc0, es0, True, wplan, stage).build()
    nc = bass.Bass("TRN2", target_bir_lowering=False)
    es = ExitStack()
    b1 = Builder(nc, es, False, wplan, stage)
    b1.build()
    es.close()
    return nc, b1


def make_in_maps(inp):
    f = lambda a: np.ascontiguousarray(np.asarray(a, dtype=np.float32))
    norms = f(inp['norms'])
    shared = dict(
        norms_c=f(norms.reshape(16, 8, 128).transpose(2, 0, 1).reshape(128, 128)),
        kvn_c=f(f(inp['kv_norm']).reshape(8, 128).T),
        a_log=f(inp['a_log']).reshape(16), a_dt_bias=f(inp['a_dt_bias']).reshape(16),
        ogain_c=f(f(inp['a_o_gain']).T),
        convw_c=f(f(inp['a_conv_w']).reshape(2, 4, 24, 128).transpose(3, 0, 2, 1).reshape(128, 192)),
        cf=CF_ARR, cb=CB_ARR, esamp=ESAMP_ARR, dcm=DCM_ARR,
        a_w_in=f(inp['a_w_in']), a_w_out=f(inp['a_w_out']), b_w_kv=f(inp['b_w_kv']), b_w_q=f(inp['b_w_q']),
        b_w_o=f(inp['b_w_o']), ffn_w_in=f(inp['ffn_w_in']), ffn_w_out=f(inp['ffn_w_out']),
    )
    maps = []
    for c in range(NCORE):
        sl = slice(NSEQ_S * c, NSEQ_S * (c + 1))
        m = dict(shared)
        m['xp'] = f(inp['x_prompt'][c])
        m['xs'] = f(np.asarray(inp['x_sample'])[sl].reshape(128, D))
        m['sconv'] = f(np.asarray(inp['state_conv'])[:, sl])
        m['sdelta'] = f(np.asarray(inp['state_delta'])[:, sl])
        m['ck'] = f(np.asarray(inp['cache_k'])[sl].reshape(NSEQ_S, SEQ, 256))
        m['cv'] = f(np.asarray(inp['cache_v'])[sl].reshape(NSEQ_S, SEQ, 256))
        maps.append(m)
    return maps


_PROG = {}


def kernel(**inputs):
    if 'nc' not in _PROG:
        _PROG['nc'] = build_program()[0]
    nc = _PROG['nc']
    maps = make_in_maps(inputs)
    res = run_bass_kernel_spmd(nc, maps, core_ids=list(range(NCORE)))
    R = res.results
    st = lambda k: np.stack([np.asarray(R[c][k], dtype=np.float32) for c in range(NCORE)], axis=0)
    y_prompt = st('yp')
    y_sample = st('ys').reshape(NCORE * NSEQ_S, LS, D)
    conv_p = st('convp').transpose(1, 0, 2, 3)
    delta_p = st('deltap').transpose(1, 0, 2, 3, 4)
    ck_p = st('ckp').reshape(NCORE, SEQ, 4, 64)
    cv_p = st('cvp').reshape(NCORE, SEQ, 4, 64)
    conv_s = np.concatenate([np.asarray(R[c]['convs'], dtype=np.float32) for c in range(NCORE)], axis=1)
    delta_s = np.concatenate([np.asarray(R[c]['deltas'], dtype=np.float32) for c in range(NCORE)], axis=1)
    ck_s = st('cks').reshape(NCORE * NSEQ_S, LS, 4, 64)
    cv_s = st('cvs').reshape(NCORE * NSEQ_S, LS, 4, 64)
    return (y_prompt, y_sample, np.ascontiguousarray(conv_p), np.ascontiguousarray(delta_p), ck_p, cv_p,
            np.ascontiguousarray(conv_s), np.ascontiguousarray(delta_s), ck_s, cv_s)
```
